# Optimizing a Trainium2 kernel written in Bass

```python
import math
import jax
import jax.numpy as jnp
from jax import lax
import numpy as np

D_MODEL = 1024
BATCH = 16
SEQ = 2048
DEPTH = 2

HEAD_DIM = 64
A_GROUPS = ((128, 1), (512, 4), (2048, 16))
A_HEADS_PER_GROUP = 2
A_HEADS = A_HEADS_PER_GROUP * len(A_GROUPS)
B_HEADS = 6
B_KV_HEADS = 2
B_GROUP = B_HEADS // B_KV_HEADS
C_HEADS = 4
N_HEADS = A_HEADS + B_HEADS + C_HEADS
MIX_WIDTH = N_HEADS * HEAD_DIM
NSA_CMP_BLOCK = 32
NSA_CMP_STRIDE = 16
NSA_CMP_HIDDEN = 256
NSA_SLC_BLOCK = 64
NSA_SLC_TOPK = 16
NSA_WINDOW = 512
MOBA_BLOCK = 256
MOBA_TOPK = 3
REL_BUCKETS = 32
REL_MAX_DIST = 128
D_FF = 2816
CONV_WIDTH = 3
BAND_BLOCK = 128
SEL_Q_CHUNK = 32
RMS_EPS = 1e-6
NEG_INF = -1e30
FORCE_BONUS = 1e4
ATTN_SCALE = HEAD_DIM ** -0.5
IN_SPLITS = ((A_HEADS * HEAD_DIM,) * 3 + (B_HEADS * HEAD_DIM,) + (B_KV_HEADS * HEAD_DIM,) * 6
             + (B_HEADS * 3,) + (C_HEADS * HEAD_DIM,) * 3 + (3 * D_MODEL,))
IN_COLS = sum(IN_SPLITS)

kernel_name = 'hybrid_dilated_nsa_moba_convffn'


def rmsnorm(x, g):
    xf = x.astype(jnp.float32)
    y = xf * lax.rsqrt(jnp.mean(xf * xf, axis=-1, keepdims=True) + RMS_EPS)
    return (y * g.astype(jnp.float32)).astype(x.dtype)


def rel_bucket(dist):
    n = jnp.maximum(dist, 0)
    exact = REL_BUCKETS // 2
    nf = jnp.maximum(n, 1).astype(jnp.float32)
    large = exact + (jnp.log(nf / exact) / math.log(REL_MAX_DIST / exact)
                     * (REL_BUCKETS - exact)).astype(jnp.int32)
    return jnp.where(n < exact, n, jnp.minimum(large, REL_BUCKETS - 1))


def band_attention(q, k, v, tbl, max_rel, dil):
    n, L, hkv, g, dh = q.shape
    n_prev = -(-max_rel // BAND_BLOCK)
    nb = -(-L // BAND_BLOCK)
    pad = nb * BAND_BLOCK - L

    def pad_len(a):
        return jnp.pad(a, [(0, 0), (0, pad)] + [(0, 0)] * (a.ndim - 2))

    qb = pad_len(q).reshape(n, nb, BAND_BLOCK, hkv, g, dh)
    kb = pad_len(k).reshape(n, nb, BAND_BLOCK, hkv, dh)
    vb = pad_len(v).reshape(n, nb, BAND_BLOCK, hkv, dh)

    def window(a):
        ap = jnp.pad(a, [(0, 0), (n_prev, 0), (0, 0), (0, 0), (0, 0)])
        return jnp.concatenate([ap[:, i:i + nb] for i in range(n_prev + 1)], axis=2)

    kw, vw = window(kb), window(vb)
    kw_len = (n_prev + 1) * BAND_BLOCK
    rel = jnp.arange(BAND_BLOCK)[:, None] + n_prev * BAND_BLOCK - jnp.arange(kw_len)[None, :]
    key_pos = (jnp.arange(nb)[:, None] * BAND_BLOCK + jnp.arange(kw_len)[None, :]
               - n_prev * BAND_BLOCK)
    valid = ((rel >= 0) & (rel <= max_rel))[None] & (key_pos >= 0)[:, None, :]
    bias = tbl[rel_bucket(rel * dil)].reshape(BAND_BLOCK, kw_len, hkv, g).transpose(2, 3, 0, 1)
    logits = jnp.einsum('nbqhgd,nbkhd->nbhgqk', qb, kw).astype(jnp.float32) * ATTN_SCALE + bias
    logits = jnp.where(valid[None, :, None, None], logits, NEG_INF)
    lse = jax.nn.logsumexp(logits, axis=-1)
    p = jnp.exp(logits - lse[..., None]).astype(v.dtype)
    out = jnp.einsum('nbhgqk,nbkhd->nbqhgd', p, vw).reshape(n, nb * BAND_BLOCK, hkv, g, dh)[:, :L]
    lse = lse.transpose(0, 1, 4, 2, 3).reshape(n, nb * BAND_BLOCK, hkv, g)[:, :L]
    return out, lse


def dilated_mixer(q, k, v, tbl):
    b, s, _, dh = q.shape
    hg = A_HEADS_PER_GROUP
    outs, lses = [], []
    for gi, (win, dil) in enumerate(A_GROUPS):
        hs = slice(gi * hg, (gi + 1) * hg)
        L = s // dil

        def to_sub(a):
            return a.reshape(b, L, dil, hg, dh).swapaxes(1, 2).reshape(b * dil, L, hg, dh)

        o, lse = band_attention(to_sub(q[:, :, hs])[:, :, :, None], to_sub(k[:, :, hs]),
                                to_sub(v[:, :, hs]), tbl[:, hs], max_rel=win // dil, dil=dil)
        outs.append(o[:, :, :, 0].reshape(b, dil, L, hg, dh).swapaxes(1, 2).reshape(b, s, hg, dh))
        lses.append(lse[..., 0].reshape(b, dil, L, hg).swapaxes(1, 2).reshape(b, s, hg))
    alpha = jax.nn.softmax(jnp.stack(lses), axis=0)
    o = jnp.concatenate([alpha[gi][..., None] * outs[gi] for gi in range(len(A_GROUPS))], axis=2)
    return o.reshape(b, s, A_HEADS * HEAD_DIM).astype(q.dtype)


def nsa_mixer(q, kc, vc, ks, vs, kw, vw, gate_logits, tbl,
              pe_k, w1_k, w2_k, pe_v, w1_v, w2_v):
    b, s = q.shape[:2]
    hkv, g, dh = B_KV_HEADS, B_GROUP, HEAD_DIM
    q = q.reshape(b, s, hkv, g, dh)
    t = jnp.arange(s)
    n_cmp = (s - NSA_CMP_BLOCK) // NSA_CMP_STRIDE + 1
    blk_idx = jnp.arange(n_cmp)[:, None] * NSA_CMP_STRIDE + jnp.arange(NSA_CMP_BLOCK)[None, :]

    def compress(a, pe, w1, w2):
        blocks = a[:, blk_idx] + pe[:, None, :]
        flat = blocks.transpose(0, 1, 3, 2, 4).reshape(b, n_cmp, hkv, NSA_CMP_BLOCK * dh)
        return jax.nn.gelu(flat @ w1) @ w2

    kcmp = compress(kc, pe_k, w1_k, w2_k)
    vcmp = compress(vc, pe_v, w1_v, w2_v)
    cmp_ok = (jnp.arange(n_cmp) * NSA_CMP_STRIDE + NSA_CMP_BLOCK - 1)[None, :] <= t[:, None]
    logits = jnp.einsum('bshgd,bchd->bhgsc', q, kcmp).astype(jnp.float32) * ATTN_SCALE
    p_cmp = jax.nn.softmax(jnp.where(cmp_ok, logits, NEG_INF), axis=-1) * cmp_ok
    o_cmp = jnp.einsum('bhgsc,bchd->bshgd', p_cmp.astype(vcmp.dtype), vcmp)
    n_slc = s // NSA_SLC_BLOCK
    c_start = jnp.arange(n_cmp) * NSA_CMP_STRIDE
    j_start = jnp.arange(n_slc) * NSA_SLC_BLOCK
    overlap = ((c_start[:, None] < j_start[None, :] + NSA_SLC_BLOCK)
               & (c_start[:, None] + NSA_CMP_BLOCK > j_start[None, :])).astype(jnp.float32)
    imp = jnp.einsum('bhgsc,cj->bhsj', p_cmp, overlap)
    j = jnp.arange(n_slc)[None, :]
    jt = (t // NSA_SLC_BLOCK)[:, None]
    forced = (j == 0) | (j == jt) | (j == jt - 1)
    score = jnp.where(j <= jt, imp + FORCE_BONUS * forced, NEG_INF)
    n_top = min(NSA_SLC_TOPK, n_slc)
    top_val, top_idx = lax.top_k(score, n_top)
    top_idx = top_idx.transpose(0, 2, 1, 3)
    top_ok = (top_val > NEG_INF / 2).transpose(0, 2, 1, 3)
    ks_t = ks.reshape(b, n_slc, NSA_SLC_BLOCK, hkv, dh).transpose(0, 3, 1, 2, 4)
    vs_t = vs.reshape(b, n_slc, NSA_SLC_BLOCK, hkv, dh).transpose(0, 3, 1, 2, 4)
    b_ix = jnp.arange(b)[:, None, None, None]
    h_ix = jnp.arange(hkv)[None, None, :, None]
    tbl_t = tbl.reshape(REL_BUCKETS, hkv, g).transpose(1, 0, 2)

    def sel_chunk(c):
        t0 = c * SEL_Q_CHUNK
        q_c = lax.dynamic_slice_in_dim(q, t0, SEL_Q_CHUNK, axis=1)
        idx_c = lax.dynamic_slice_in_dim(top_idx, t0, SEL_Q_CHUNK, axis=1)
        ok_c = lax.dynamic_slice_in_dim(top_ok, t0, SEL_Q_CHUNK, axis=1)
        kg = ks_t[b_ix, h_ix, idx_c]
        vg = vs_t[b_ix, h_ix, idx_c]
        pos = idx_c[..., None] * NSA_SLC_BLOCK + jnp.arange(NSA_SLC_BLOCK)
        dist = (t0 + jnp.arange(SEL_Q_CHUNK))[None, :, None, None, None] - pos
        ok = ok_c[..., None] & (dist >= 0)
        bias = tbl_t[h_ix[..., None], rel_bucket(dist)].transpose(0, 1, 2, 5, 3, 4)
        lg = jnp.einsum('bqhgd,bqhkld->bqhgkl', q_c, kg).astype(jnp.float32) * ATTN_SCALE + bias
        lg = jnp.where(ok[:, :, :, None], lg, NEG_INF)
        lg = lg.reshape(b, SEL_Q_CHUNK, hkv, g, n_top * NSA_SLC_BLOCK)
        p = jax.nn.softmax(lg, axis=-1).astype(vg.dtype)
        return jnp.einsum('bqhgn,bqhnd->bqhgd', p,
                          vg.reshape(b, SEL_Q_CHUNK, hkv, n_top * NSA_SLC_BLOCK, dh))

    o_slc = lax.map(sel_chunk, jnp.arange(s // SEL_Q_CHUNK))
    o_slc = jnp.moveaxis(o_slc, 0, 1).reshape(b, s, hkv, g, dh)
    o_win, _ = band_attention(q, kw, vw, tbl, max_rel=NSA_WINDOW - 1, dil=1)
    gates = jax.nn.sigmoid(gate_logits.astype(jnp.float32)).reshape(b, s, hkv, g, 3)
    o = gates[..., 0:1] * o_cmp + gates[..., 1:2] * o_slc + gates[..., 2:3] * o_win
    return o.reshape(b, s, B_HEADS * HEAD_DIM).astype(q.dtype)


def moba_mixer(q, k, v, tbl):
    b, s, h, dh = q.shape
    n_blk = -(-s // MOBA_BLOCK)
    pad = n_blk * MOBA_BLOCK - s
    k_p = jnp.pad(k, ((0, 0), (0, pad), (0, 0), (0, 0)))
    v_p = jnp.pad(v, ((0, 0), (0, pad), (0, 0), (0, 0)))
    kb = k_p.reshape(b, n_blk, MOBA_BLOCK, h, dh)
    vb = v_p.reshape(b, n_blk, MOBA_BLOCK, h, dh)
    kmean = jnp.mean(kb.astype(jnp.float32), axis=2)
    t = jnp.arange(s)
    past = jnp.arange(n_blk)[None, :] < (t // MOBA_BLOCK)[:, None]
    gate = jnp.einsum('bshd,bnhd->bshn', q.astype(jnp.float32), kmean)
    gate = jnp.where(past[None, :, None, :], gate, NEG_INF)
    n_top = min(MOBA_TOPK, n_blk)
    top_val, top_idx = lax.top_k(gate, n_top)
    top_ok = top_val > NEG_INF / 2
    kb_t = kb.transpose(0, 3, 1, 2, 4)
    vb_t = vb.transpose(0, 3, 1, 2, 4)
    b_ix = jnp.arange(b)[:, None, None, None]
    h_ix = jnp.arange(h)[None, None, :, None]
    tbl_t = tbl.T
    n_sel = n_top * MOBA_BLOCK

    def chunk(c):
        t0 = c * SEL_Q_CHUNK
        tq = t0 + jnp.arange(SEL_Q_CHUNK)
        q_c = lax.dynamic_slice_in_dim(q, t0, SEL_Q_CHUNK, axis=1)
        idx_c = lax.dynamic_slice_in_dim(top_idx, t0, SEL_Q_CHUNK, axis=1)
        ok_c = lax.dynamic_slice_in_dim(top_ok, t0, SEL_Q_CHUNK, axis=1)
        kg = kb_t[b_ix, h_ix, idx_c]
        vg = vb_t[b_ix, h_ix, idx_c]
        dist = tq[None, :, None, None, None] - (idx_c[..., None] * MOBA_BLOCK + jnp.arange(MOBA_BLOCK))
        bias = tbl_t[h_ix[..., None], rel_bucket(dist)]
        l_sel = jnp.einsum('bqhd,bqhkld->bqhkl', q_c, kg).astype(jnp.float32) * ATTN_SCALE + bias
        l_sel = jnp.where(ok_c[..., None], l_sel, NEG_INF).reshape(b, SEL_Q_CHUNK, h, n_sel)
        own0 = (t0 // MOBA_BLOCK) * MOBA_BLOCK
        k_own = lax.dynamic_slice_in_dim(k_p, own0, MOBA_BLOCK, axis=1)
        v_own = lax.dynamic_slice_in_dim(v_p, own0, MOBA_BLOCK, axis=1)
        dist_own = tq[:, None] - (own0 + jnp.arange(MOBA_BLOCK))[None, :]
        l_own = (jnp.einsum('bqhd,blhd->bqhl', q_c, k_own).astype(jnp.float32) * ATTN_SCALE
                 + tbl[rel_bucket(dist_own)].transpose(0, 2, 1))
        l_own = jnp.where((dist_own >= 0)[:, None, :], l_own, NEG_INF)
        p = jax.nn.softmax(jnp.concatenate([l_sel, l_own], axis=-1), axis=-1).astype(v.dtype)
        return (jnp.einsum('bqhn,bqhnd->bqhd', p[..., :n_sel],
                           vg.reshape(b, SEL_Q_CHUNK, h, n_sel, dh))
                + jnp.einsum('bqhl,blhd->bqhd', p[..., n_sel:], v_own))

    o = lax.map(chunk, jnp.arange(s // SEL_Q_CHUNK))
    return jnp.moveaxis(o, 0, 1).reshape(b, s, h * dh).astype(q.dtype)


def conv_ffn(x, w_up, conv_w, conv_b, w_down):
    h = x @ w_up
    h = lax.conv_general_dilated(h, conv_w[:, None, :].astype(h.dtype), window_strides=(1,),
                                 padding=[(CONV_WIDTH - 1, 0)],
                                 dimension_numbers=('NWC', 'WIO', 'NWC'),
                                 feature_group_count=h.shape[-1]) + conv_b
    a, u = jnp.split(h, 2, axis=-1)
    return (jax.nn.silu(a) * u) @ w_down


def setup_inputs(seed: int = 0) -> dict:
    key = jax.random.key(seed)
    k = jax.random.split(key, 18)

    def nrm(kk, shape, scale):
        return jax.random.normal(kk, shape, jnp.float32) * scale

    two_ff = 2 * D_FF
    cmp_in = NSA_CMP_BLOCK * HEAD_DIM
    return {
        'x': nrm(k[0], (BATCH, SEQ, D_MODEL), 1.0),
        'rel_bias': nrm(k[1], (REL_BUCKETS, N_HEADS), 0.3),
        'norm_mix': 1.0 + nrm(k[2], (DEPTH, D_MODEL), 0.01),
        'w_in': nrm(k[3], (DEPTH, D_MODEL, IN_COLS), D_MODEL ** -0.5),
        'cmp_pe_k': nrm(k[4], (DEPTH, NSA_CMP_BLOCK, HEAD_DIM), 0.1),
        'cmp_w1_k': nrm(k[5], (DEPTH, cmp_in, NSA_CMP_HIDDEN), cmp_in ** -0.5),
        'cmp_w2_k': nrm(k[6], (DEPTH, NSA_CMP_HIDDEN, HEAD_DIM), NSA_CMP_HIDDEN ** -0.5),
        'cmp_pe_v': nrm(k[7], (DEPTH, NSA_CMP_BLOCK, HEAD_DIM), 0.1),
        'cmp_w1_v': nrm(k[8], (DEPTH, cmp_in, NSA_CMP_HIDDEN), cmp_in ** -0.5),
        'cmp_w2_v': nrm(k[9], (DEPTH, NSA_CMP_HIDDEN, HEAD_DIM), NSA_CMP_HIDDEN ** -0.5),
        'w_branch': nrm(k[10], (DEPTH, MIX_WIDTH, D_MODEL), MIX_WIDTH ** -0.5),
        'w_out': nrm(k[11], (DEPTH, D_MODEL, D_MODEL), D_MODEL ** -0.5),
        'norm_ffn': 1.0 + nrm(k[12], (DEPTH, D_MODEL), 0.01),
        'w_up': nrm(k[13], (DEPTH, D_MODEL, two_ff), D_MODEL ** -0.5),
        'conv_w': nrm(k[14], (DEPTH, CONV_WIDTH, two_ff), CONV_WIDTH ** -0.5),
        'conv_b': nrm(k[15], (DEPTH, two_ff), 0.01),
        'w_down': nrm(k[16], (DEPTH, D_FF, D_MODEL), D_FF ** -0.5),
        'norm_final': 1.0 + nrm(k[17], (D_MODEL,), 0.01),
    }


def reference(x, rel_bias, norm_mix, w_in, cmp_pe_k, cmp_w1_k, cmp_w2_k, cmp_pe_v, cmp_w1_v,
              cmp_w2_v, w_branch, w_out, norm_ffn, w_up, conv_w, conv_b, w_down, norm_final):
    b, s, _ = x.shape
    split_at = np.cumsum(IN_SPLITS)[:-1].tolist()
    tbl_a = rel_bias[:, :A_HEADS]
    tbl_b = rel_bias[:, A_HEADS:A_HEADS + B_HEADS]
    tbl_c = rel_bias[:, A_HEADS + B_HEADS:]
    r0 = A_HEADS * HEAD_DIM
    r1 = r0 + B_HEADS * HEAD_DIM

    def heads(a):
        return a.reshape(b, s, -1, HEAD_DIM)

    for i in range(DEPTH):
        hn = rmsnorm(x, norm_mix[i])
        (aq, ak, av, bq, bkc, bvc, bks, bvs, bkw, bvw, bg, cq, ck, cv, mg) = jnp.split(
            hn @ w_in[i], split_at, axis=-1)
        o_a = dilated_mixer(heads(aq), heads(ak), heads(av), tbl_a)
        o_b = nsa_mixer(heads(bq), heads(bkc), heads(bvc), heads(bks), heads(bvs), heads(bkw),
                        heads(bvw), bg, tbl_b, cmp_pe_k[i], cmp_w1_k[i], cmp_w2_k[i],
                        cmp_pe_v[i], cmp_w1_v[i], cmp_w2_v[i])
        o_c = moba_mixer(heads(cq), heads(ck), heads(cv), tbl_c)
        wb = w_branch[i]
        gates = jax.nn.sigmoid(mg.astype(jnp.float32)).reshape(b, s, 3, D_MODEL)
        merged = (gates[:, :, 0] * (o_a @ wb[:r0]) + gates[:, :, 1] * (o_b @ wb[r0:r1])
                  + gates[:, :, 2] * (o_c @ wb[r1:]))
        x = x + merged.astype(x.dtype) @ w_out[i]
        x = x + conv_ffn(rmsnorm(x, norm_ffn[i]), w_up[i], conv_w[i], conv_b[i], w_down[i])
    return rmsnorm(x, norm_final)
```

```python
import math
from contextlib import ExitStack
import numpy as np
import concourse.bass as bass
import concourse.mybir as mybir
from concourse.bass_utils import run_bass_kernel_spmd

F32 = mybir.dt.float32
BF16 = mybir.dt.bfloat16
AF = mybir.ActivationFunctionType
ALU = mybir.AluOpType
AX = mybir.AxisListType

S = 2048
D = 1024
NCORE = 8
SEQ_PER_CORE = 2
DEPTH = 2
INC = 6162
DFF = 2816
BIG = 240000.0
NEGB = -30000.0
C_AQ, C_AK, C_AV = 0, 384, 768
C_BQ, C_BKC, C_BVC, C_BKS, C_BVS, C_BKW, C_BVW, C_BG = 1152, 1536, 1664, 1792, 1920, 2048, 2176, 2304
C_CQ, C_CK, C_CV, C_MG = 2322, 2578, 2834, 3090
A_DIL = (1, 4, 16)
NSTRIP = 10 * 1024 + 3 * 512 + 1024 + 2048
OFF_B = lambda h: (h - 6) * 1024
OFF_A = lambda g: 10240 + g * 512
OFF_UM = 10240 + 1536
OFF_CMP = OFF_UM + 1024

ENGS = ("tensor", "vector", "scalar", "gpsimd", "sync")


class Unit:
    __slots__ = ("w", "r")

    def __init__(self):
        self.w = None
        self.r = {}


class _Rec:
    def __init__(self):
        self.call = None

    def __getattr__(self, name):
        def f(*a, **kw):
            self.call = (name, a, kw)
            return self
        return f


class Prog:
    def __init__(self, nc, stack):
        self.nc = nc
        self.stack = stack
        self.q = {e: [] for e in ENGS}
        self.sems = {}
        self.cnt = {}
        self.seen = {e: {} for e in ENGS}
        for e in ENGS:
            self._sem("E_" + e)
        self.units = {}
        self.waitmax = {}
        self.limit = None
        self.nrec = 0
        self.log = []
        self.phase = "setup"
        self.pe_phase = []

    def _sem(self, key):
        if key not in self.sems:
            self.sems[key] = self.stack.enter_context(self.nc.semaphore(key))
            self.cnt[key] = 0
        return self.sems[key]

    def unit(self, key):
        u = self.units.get(key)
        if u is None:
            u = Unit()
            self.units[key] = u
        return u

    def _deps(self, eng, reads, writes, same_war=False, own_sem=None):
        need = {}

        def add(tok, kind):
            if tok is None:
                return
            k, v = tok
            if kind == "waw" and own_sem is not None and k == own_sem:
                return
            if k == "E_" + eng:
                if eng == "tensor":
                    return
                if kind == "war" and not same_war:
                    return
            if v > need.get(k, 0):
                need[k] = v

        for u in reads:
            add(u.w, "raw")
        for u in writes:
            add(u.w, "waw")
            for k, v in u.r.items():
                add((k, v), "war")
        out = []
        seen = self.seen[eng]
        for k in list(need):
            if not k.startswith("E_"):
                need[k] = self.cnt[k]
        for k, v in need.items():
            if seen.get(k, 0) < v:
                seen[k] = v
                out.append((k, v))
                if not k.startswith("E_") and v > self.waitmax.get(k, 0):
                    self.waitmax[k] = v
        return out

    def _mark(self, tok, reads, writes):
        for u in reads:
            if u.r.get(tok[0], 0) < tok[1]:
                u.r[tok[0]] = tok[1]
        for u in writes:
            u.w = tok
            u.r = {}

    def op(self, eng, fn, reads=(), writes=(), inc=True):
        self.nrec += 1
        if self.limit is not None and self.nrec > self.limit:
            return None
        if "hnT" in reads:
            reads = list(reads) + ["hnTd"]
        reads = [self.unit(u) for u in reads]
        writes = [self.unit(u) for u in writes]
        waits = self._deps(eng, reads, writes)
        k = "E_" + eng
        if inc:
            self.cnt[k] += 1
            tok = (k, self.cnt[k])
        else:
            tok = (k, self.cnt[k] + 1)
        rec = _Rec()
        fn(rec)
        name, a, kw = rec.call
        self.log.append((self.nrec, eng, name, [str(getattr(x, "shape", x)) for x in list(a) + list(kw.values())][:4]))
        if eng == "tensor":
            self.pe_phase.append(self.phase)
        self.q[eng].append((waits, (lambda e: getattr(e, name)(*a, **kw)), (k, 1) if inc else None))
        self._mark(tok, reads, writes)
        return tok

    def dma(self, eng, semkey, out, in_, reads=(), writes=()):
        self.nrec += 1
        if self.limit is not None and self.nrec > self.limit:
            return None
        self.log.append((self.nrec, eng, "dma", semkey))
        reads = [self.unit(u) for u in reads]
        writes = [self.unit(u) for u in writes]
        self._sem(semkey)
        waits = self._deps(eng, reads, writes, same_war=True, own_sem=semkey)
        wm = self.waitmax.get(semkey, 0)
        if wm > self.seen[eng].get(semkey, 0):
            self.seen[eng][semkey] = wm
            waits.append((semkey, wm))
        self.cnt[semkey] += 16
        tok = (semkey, self.cnt[semkey])
        self.q[eng].append((waits, lambda e: e.dma_start(out=out, in_=in_), (semkey, 16)))
        self._mark(tok, reads, writes)
        return tok

    def barrier(self):
        cur = [(k, v) for k, v in self.cnt.items() if v > 0]
        for e in ENGS:
            waits = []
            for k, v in cur:
                if k == "E_" + e:
                    continue
                if self.seen[e].get(k, 0) < v:
                    self.seen[e][k] = v
                    waits.append((k, v))
                    if not k.startswith("E_") and v > self.waitmax.get(k, 0):
                        self.waitmax[k] = v
            if waits:
                self.q[e].append((waits, None, None))

    def finish(self, final_tokens):
        prog = self
        fin = [t for t in final_tokens if t is not None]
        for k, v in self.cnt.items():
            if v > 0 and not k.startswith("E_tensor"):
                fin.append((k, v))

        def emit(engname):
            def body(e):
                for waits, fn, inc in prog.q[engname]:
                    for (k, v) in waits:
                        e.wait_ge(prog.sems[k], v)
                    if fn is None:
                        continue
                    ins = fn(e)
                    if inc is not None:
                        ins.then_inc(prog.sems[inc[0]], inc[1])
                if engname == "sync":
                    for (k, v) in fin:
                        e.wait_ge(prog.sems[k], v)
            return body

        with self.nc.Block() as block:
            block.tensor(emit("tensor"))
            block.vector(emit("vector"))
            block.scalar(emit("scalar"))
            block.gpsimd(emit("gpsimd"))
            block.sync(emit("sync"))


def _bucket(d):
    n = np.maximum(d, 0)
    nf = np.maximum(n, 1).astype(np.float32)
    large = 16 + (np.log(nf / np.float32(16)) / np.float32(math.log(8.0)) * np.float32(16)).astype(np.int32)
    return np.where(n < 16, n, np.minimum(large, 31)).astype(np.int64)


def _host_consts(rel_bias):
    rb = np.asarray(rel_bias, np.float32)
    ki = np.arange(128)[:, None]
    strips = np.zeros((128, NSTRIP), np.float32)
    u = np.arange(1024)[None, :]
    d = u - 384 - ki
    bk = _bucket(d)
    for h in range(6, 16):
        strips[:, OFF_B(h):OFF_B(h) + 1024] = np.where(d >= 0, rb[bk, h], np.float32(NEGB))
    qi = np.arange(128)[None, :]
    for g, dil in enumerate(A_DIL):
        for hh in range(2):
            h = 2 * g + hh
            ds = qi - ki + 128
            prev = np.where(ds <= 128, rb[_bucket(ds * dil), h], np.float32(NEGB))
            ds2 = qi - ki
            diag = np.where(ds2 >= 0, rb[_bucket(ds2 * dil), h], np.float32(NEGB))
            o = OFF_A(g) + hh * 256
            strips[:, o:o + 128] = prev
            strips[:, o + 128:o + 256] = diag
    w = np.arange(1024)[None, :]
    strips[:, OFF_UM:OFF_UM + 1024] = np.where(w - ki <= 511, np.float32(0), np.float32(NEGB))
    q = np.arange(2048)[None, :]
    ok = (16 * ki + 31 <= q) & (ki < 127)
    strips[:, OFF_CMP:OFF_CMP + 2048] = np.where(ok, np.float32(0), np.float32(NEGB))
    tabs = np.zeros((128, 128 + 256 + 64 + 16), np.float32)
    tabs[:, 0:128] = np.eye(128, dtype=np.float32)
    p = np.arange(128)[:, None]
    for i in range(8, 16):
        t = 128 * i + p
        jt = t // 64
        j = np.arange(32)[None, :]
        forced = (j == 0) | (j == jt) | (j == jt - 1)
        fb = np.where(j <= jt, np.where(forced, np.float32(1e4), np.float32(0)), np.float32(-1e30))
        tabs[:, 128 + (i - 8) * 32:128 + (i - 8) * 32 + 32] = fb
        nb = t // 256
        n = np.arange(8)[None, :]
        pm = np.where(n < nb, np.float32(0), np.where(n == nb, np.float32(1e4), np.float32(-1e30)))
        tabs[:, 384 + (i - 8) * 8:384 + (i - 8) * 8 + 8] = pm
    tabs[:, 448:464] = rb[31][None, :]
    ov = np.zeros((128, 64), np.float32)
    c = np.arange(128)[:, None]
    j = np.arange(32)[None, :]
    ov[:, 0:32] = ((16 * c < 64 * j + 64) & (16 * c + 32 > 64 * j) & (c < 127)).astype(np.float32)
    ov[:127, 32] = 1.0
    k = np.arange(2048)[None, :]
    inds = (k // 64 == np.arange(32)[:, None]).astype(np.float32)
    indm = (k // 256 == np.arange(8)[:, None]).astype(np.float32)
    sel = np.zeros((18, 18 * 64), np.float32)
    for jj in range(18):
        sel[jj, jj * 64:(jj + 1) * 64] = 1.0
    return dict(strips=strips, tabs=tabs, ov=ov, inds=inds, indm=indm, sel=sel)


def build_program(debug=False, stages=None, nseq=SEQ_PER_CORE, depth=DEPTH, limit=None):
    ST = stages if stages is not None else {"A", "C", "B", "merge", "outproj", "ffn", "final"}
    nc = bass.Bass("TRN2", target_bir_lowering=False)
    dt_in = lambda name, shape: nc.dram_tensor(name, shape, F32, kind="ExternalInput").ap()
    x_d = dt_in("x", [SEQ_PER_CORE, S, D])
    w_in = dt_in("w_in", [DEPTH, D, INC])
    w1k = dt_in("cmp_w1_k", [DEPTH, 2048, 256])
    w2k = dt_in("cmp_w2_k", [DEPTH, 256, 64])
    w1v = dt_in("cmp_w1_v", [DEPTH, 2048, 256])
    w2v = dt_in("cmp_w2_v", [DEPTH, 256, 64])
    pe2 = dt_in("pe2", [DEPTH, 2, 128, 16])
    w_br = dt_in("w_branch", [DEPTH, D, D])
    w_out = dt_in("w_out", [DEPTH, D, D])
    w_up = dt_in("w_up", [DEPTH, D, 2 * DFF])
    w_dn = dt_in("w_down", [DEPTH, DFF, D])
    gains = dt_in("gains_in", [128, 32])
    cwt = dt_in("cwt", [DEPTH, 128, 176])
    gfin = dt_in("norm_final", [D])
    strips_d = dt_in("strips_in", [128, NSTRIP])
    tabs_d = dt_in("tabs_in", [128, 464])
    ov_d = dt_in("ov_in", [128, 64])
    inds_d = dt_in("inds", [32, 2048])
    indm_d = dt_in("indm", [8, 2048])
    sel_d = dt_in("sel_in", [18, 18 * 64])
    out_d = nc.dram_tensor("out", [SEQ_PER_CORE, S, D], F32, kind="ExternalOutput").ap()
    scr = nc.dram_tensor("xscr", [S, D], F32, kind="Internal").ap()
    dbg = {}
    if debug:
        dbg["omT"] = nc.dram_tensor("dbg_omT", [128, 8, S], BF16, kind="ExternalOutput").ap()
        dbg["x1"] = nc.dram_tensor("dbg_x1", [S, D], F32, kind="ExternalOutput").ap()

    with ExitStack() as st:
        P = Prog(nc, st)
        P.limit = limit
        build_program.P = P
        sb = lambda name, shape, dt: st.enter_context(nc.sbuf_tensor(name, shape, dt))
        hnT = sb("hnT", [128, 8, S], BF16)
        wsl_raw = sb("wsl", [128, 8192], BF16)
        wsl = wsl_raw[:].rearrange("p (s c n) -> p s c n", s=2, c=8)
        strips = sb("strips", [128, NSTRIP], BF16)
        tabs = sb("tabs", [128, 464], F32)
        gains_t = sb("gains", [128, 32], F32)
        cw_t = sb("cw", [128, 176], F32)
        gfin_t = sb("gfin", [128, D], F32)
        ov_t = sb("ov", [128, 64], BF16)
        sel_t = sb("sel", [18, 18 * 64], F32)
        xt = sb("xt", [128, 2, D], F32)
        small = sb("small", [128, 64], F32)
        halo = sb("halo", [128, 44, 2], F32)
        ones_bf = sb("ones_bf", [128, 64], BF16)
        ARENA = 44 * 1024
        arena = sb("arena", [128, ARENA], BF16)
        psum = st.enter_context(nc.psum_tensor("ps", [128, 8, 512], F32))

        ident = tabs[:, 0:128]
        omT = arena[:, 0:16384].rearrange("p (c t) -> p c t", c=8)
        QP = arena[:, 16384:26624].rearrange("p (i t) -> p i t", i=5)
        VA = arena[:, 26624:32768].rearrange("p (t h e) -> p t h e", t=16, h=3)
        mergedT = arena[:, 16384:32768].rearrange("p (c t) -> p c t", c=8)
        dent = arena[:, 32768:36864].bitcast(F32)
        Ebuf = arena[:, 36864:38912].bitcast(F32).rearrange("p (b n) -> p b n", b=2)
        Pbuf = arena[:, 38912:40448].rearrange("p (b n) -> p b n", b=3)
        rtmp = arena[:, 40448:42496].bitcast(F32).rearrange("p (b n) -> p b n", b=2)
        misc = arena[:, 42496:44032].bitcast(F32)
        hb = misc[:, 0:4]
        gx = misc[:, 8:136]
        gu = misc[:, 136:264]
        scb = misc[:, 264:296]
        sctmp = misc[:, 296:328]
        m8 = misc[:, 328:344]
        nmk = misc[:, 344:376]
        rec3 = misc[:, 376:380]
        KCMP = misc[:, 384:448].bitcast(BF16)
        VCMP = misc[:, 448:512].bitcast(BF16)
        gT = misc[:, 512:640].bitcast(BF16).rearrange("p (m c) -> p m c", m=2)
        w2s = misc[:, 640:704].bitcast(BF16).rearrange("p (m d) -> p m d", m=2)
        pe2s = misc[:, 704:712].bitcast(BF16)
        kmT = misc[:, 712:720].bitcast(BF16)
        GT = dent
        gTf = arena[:, 0:22528].rearrange("p (j t) -> p j t", j=22)
        wdsl = arena[:, 22528:25600].rearrange("p (s n) -> p s n", s=6)
        _o = 25600
        rbufs = [[None, None], [None, None]]
        obufs = [[None, None], [None, None]]
        for au_ in range(2):
            for par_ in range(2):
                rbufs[au_][par_] = arena[:, _o:_o + 2056].bitcast(F32)
                _o += 2056
        for au_ in range(2):
            for par_ in range(2):
                obufs[au_][par_] = arena[:, _o:_o + 2048].bitcast(F32)
                _o += 2048
        sg = arena[:, _o:_o + 2048].bitcast(F32)
        assert _o + 2048 <= ARENA

        PS = lambda b: psum[:, b, :]
        psn = lambda b: "ps%d" % b
        rr = {"s": 0, "o": 0, "m": 0, "e": 0, "p": 0, "w": 0, "x": 0, "f": 0, "w4": 0, "wd": 0}

        def nxt(kind, n, base=0):
            v = base + rr[kind] % n
            rr[kind] += 1
            return v

        P.dma("sync", "ldc", tabs[:], tabs_d, writes=["tabs"])
        P.dma("sync", "ldc", gains_t[:], gains, writes=["gains"])
        P.dma("sync", "ldc", sel_t[:], sel_d, writes=["sel"])
        P.dma("sync", "ldc", gfin_t[:], gfin.partition_broadcast(128), writes=["gfin"])
        P.dma("gpsimd", "ldg", ov_t[:], ov_d, writes=["ov"])
        P.op("vector", lambda e: e.memset(ones_bf[:], 1.0), writes=["ones_bf"])
        stg = arena[:, 0:8192].bitcast(F32)
        off = 0
        while off < NSTRIP:
            n = min(4096, NSTRIP - off)
            o_, n_ = off, n
            P.dma("sync", "ldc", stg[:, 0:n_], strips_d[:, o_:o_ + n_], writes=["stg"])
            P.op("scalar", lambda e, o_=o_, n_=n_: e.activation(out=strips[:, o_:o_ + n_], in_=stg[:, 0:n_], func=AF.Exp),
                 reads=["stg"], writes=["strips"])
            off += n
        P.barrier()

        def SB(h, lo, n):
            o = OFF_B(h) + lo
            return strips[:, o:o + n]

        def load_w(slot, col, dram2d, ncols, nm):
            P.dma("gpsimd", "ldw%d" % slot, wsl[:, slot, :, col:col + ncols],
                  dram2d.rearrange("(c p) n -> p c n", p=128), writes=["wsl%d" % slot])

        def proj_fm(slot, c0, M, evac, tbs=range(4)):
            for tb in tbs:
                b = nxt("m", 2, 5)
                for c in range(8):
                    P.op("tensor", lambda e, b=b, c=c, tb=tb: e.matmul(
                        psum[0:M, b, :], lhsT=wsl[:, slot, c, c0:c0 + M], rhs=hnT[:, c, tb * 512:(tb + 1) * 512],
                        start=(c == 0), stop=(c == 7)),
                        reads=["wsl%d" % slot, "hnT"], writes=[psn(b)], inc=(c == 7))
                evac(tb, b)

        def proj_tm(slot, c0, N, tok_ap, evac):
            for t in range(16):
                b = nxt("m", 2, 5)
                for c in range(8):
                    P.op("tensor", lambda e, b=b, c=c, t=t: e.matmul(
                        psum[:, b, 0:N], lhsT=tok_ap(t, c), rhs=wsl[:, slot, c, c0:c0 + N],
                        start=(c == 0), stop=(c == 7)),
                        reads=["wsl%d" % slot, "hnT"], writes=[psn(b)], inc=(c == 7))
                evac(t, b)

        alt = {"i": 0}

        def cp(out, in_, reads, writes, eng=None):
            if eng is None:
                eng = "scalar" if alt["i"] % 2 == 0 else "vector"
                alt["i"] += 1
            if eng == "scalar":
                P.op("scalar", lambda e: e.copy(out=out, in_=in_), reads=reads, writes=writes)
            else:
                P.op("vector", lambda e: e.tensor_copy(out=out, in_=in_), reads=reads, writes=writes)

        def s_tile(sb_, kt_ap, q_ap, kd, reads, c0=0, c1=512):
            P.op("tensor", lambda e: e.matmul(psum[:, sb_, c0:c1], lhsT=kt_ap, rhs=q_ap[:, c0:c1], start=True, stop=True),
                 reads=reads, writes=[psn(sb_)])

        def softmax_tile(sb_, mode, h, strip_ap=None, strip2_ap=None, c0=0, c1=512):
            pb = nxt("p", 3)
            pn = "P%d" % pb
            if mode == "const":
                P.op("scalar", lambda e: e.activation(out=Pbuf[:, pb, c0:c1], in_=psum[:, sb_, c0:c1], func=AF.Exp,
                                                      bias=tabs[:, 448 + h:449 + h], scale=0.125),
                     reads=[psn(sb_), "tabs"], writes=[pn])
                return pb
            eb = nxt("e", 2)
            en = "E%d" % eb
            if mode == "eb":
                P.op("scalar", lambda e: e.activation(out=Ebuf[:, eb, c0:c1], in_=psum[:, sb_, c0:c1], func=AF.Exp, scale=0.125),
                     reads=[psn(sb_)], writes=[en])
            else:
                P.op("scalar", lambda e: e.activation(out=Ebuf[:, eb, c0:c1], in_=psum[:, sb_, c0:c1], func=AF.Exp,
                                                      bias=tabs[:, 448 + h:449 + h], scale=0.125),
                     reads=[psn(sb_), "tabs"], writes=[en])
            if strip2_ap is not None:
                P.op("vector", lambda e: e.tensor_tensor(out=Ebuf[:, eb, c0:c1], in0=Ebuf[:, eb, c0:c1], in1=strip2_ap[:, c0:c1], op=ALU.mult),
                     reads=[en, "strips"], writes=[en])
            P.op("vector", lambda e: e.tensor_tensor(out=Pbuf[:, pb, c0:c1], in0=Ebuf[:, eb, c0:c1], in1=strip_ap[:, c0:c1], op=ALU.mult),
                 reads=[en, "strips"], writes=[pn])
            return pb

        deferred = []
        pend = []

        def flush_evac():
            while deferred:
                deferred.pop(0)()

        def flush_all():
            while pend:
                pend.pop(0)()
            flush_evac()

        def attend_block(h, q_ap, kd, tiles, qreads, evac):
            ob = nxt("o", 2, 3)
            n = len(tiles)
            assert n >= 2 and tiles[0][7] == 0 and tiles[0][8] == 512
            for i, (kt_ap, kreads, va_ap, vreads, mode, strip, strip2, c0, c1) in enumerate(tiles):
                sb_ = nxt("s", 3, 0)
                s_tile(sb_, kt_ap, q_ap, kd, qreads + kreads, c0, c1)
                if len(pend) >= 2:
                    pend.pop(0)()
                pb = softmax_tile(sb_, mode, h, strip, strip2, c0, c1)
                if i == 1:
                    flush_evac()
                pend.append(lambda va_ap=va_ap, vreads=vreads, pb=pb, i=i, c0=c0, c1=c1: P.op(
                    "tensor", lambda e: e.matmul(psum[:, ob, c0:c1], lhsT=va_ap, rhs=Pbuf[:, pb, c0:c1], start=(i == 0), stop=(i == n - 1)),
                    reads=vreads + ["P%d" % pb], writes=[psn(ob)], inc=True))
            deferred.append(lambda: evac(ob))

        eps_t = sb("eps", [128, 1], F32)
        P.op("vector", lambda e: e.memset(eps_t[:], 1e-6), writes=["eps"])

        def norm_tile(xb, t, gcol, post=True):
            xn = "xt%d" % xb
            P.op("scalar", lambda e: e.activation(out=rtmp[:, 0, :], in_=xt[:, xb, 0:512], func=AF.Square, accum_out=small[:, 0:1]),
                 reads=[xn], writes=["rtmp0", "st0"])
            P.op("scalar", lambda e: e.activation(out=rtmp[:, 0, :], in_=xt[:, xb, 512:1024], func=AF.Square, accum_out=small[:, 1:2]),
                 reads=[xn], writes=["rtmp0", "st1"])
            P.op("vector", lambda e: e.tensor_tensor(out=small[:, 2:3], in0=small[:, 0:1], in1=small[:, 1:2], op=ALU.add),
                 reads=["st0", "st1"], writes=["st2"])
            P.op("scalar", lambda e: e.activation(out=small[:, 3:4], in_=small[:, 2:3], func=AF.Sqrt, bias=eps_t[:, 0:1], scale=1.0 / D),
                 reads=["st2", "eps"], writes=["st3"])
            P.op("vector", lambda e: e.reciprocal(out=small[:, 4:5], in_=small[:, 3:4]), reads=["st3"], writes=["st4"])
            P.op("vector", lambda e: e.tensor_scalar(out=xt[:, xb, :], in0=xt[:, xb, :], scalar1=small[:, 4:5], scalar2=None, op0=ALU.mult),
                 reads=[xn, "st4"], writes=[xn])
            if gcol is None or not post:
                return
            norm_post(xb, t, gcol)

        def norm_post(xb, t, gcol):
            xn = "xt%d" % xb
            for half in range(2):
                b = nxt("f", 8, 0)
                for cc in range(4):
                    c = half * 4 + cc
                    P.op("tensor", lambda e, b=b, c=c, cc=cc: e.transpose(psum[:, b, cc * 128:(cc + 1) * 128], xt[:, xb, c * 128:(c + 1) * 128], ident),
                         reads=[xn, "tabs"], writes=[psn(b)], inc=(cc == 3))
                for cc in range(4):
                    c = half * 4 + cc
                    if half == 0:
                        P.op("scalar", lambda e, b=b, c=c, cc=cc: e.activation(out=hnT[:, c, t * 128:(t + 1) * 128], in_=psum[:, b, cc * 128:(cc + 1) * 128],
                                                                              func=AF.Copy, scale=gains_t[:, gcol + c:gcol + c + 1]),
                             reads=[psn(b), "gains"], writes=["hnT"])
                    else:
                        P.op("vector", lambda e, b=b, c=c, cc=cc: e.tensor_scalar(out=hnT[:, c, t * 128:(t + 1) * 128], in0=psum[:, b, cc * 128:(cc + 1) * 128],
                                                                                 scalar1=gains_t[:, gcol + c:gcol + c + 1], scalar2=None, op0=ALU.mult),
                             reads=[psn(b), "gains"], writes=["hnTd"])

        final_toks = []

        for sq in range(nseq):
            for l in range(depth):
                src_x = (lambda t: x_d[sq, t * 128:(t + 1) * 128, :]) if l == 0 else (lambda t: scr[t * 128:(t + 1) * 128, :])
                P.phase = "norm1"
                prev_post = None
                for t in range(16):
                    xb = nxt("x", 2)
                    P.dma("sync", "ldx%d" % xb, xt[:, xb, :], src_x(t), reads=["scr"], writes=["xt%d" % xb])
                    norm_tile(xb, t, (l * 2 + 0) * 8, post=False)
                    if prev_post is not None:
                        norm_post(*prev_post)
                    prev_post = (xb, t, (l * 2 + 0) * 8)
                norm_post(*prev_post)
                W = w_in[l]
                P.op("vector", lambda e: e.memset(VA[:, :, :, 64:128], 1.0), writes=["VA"])
                P.op("vector", lambda e: e.memset(VCMP[:, 64:128], 1.0), writes=["VCMP"])
                P.op("vector", lambda e: e.memset(gT[:], 0.0), writes=["gT"])

                P.phase = "A"
                for g, dil in (enumerate(A_DIL) if "A" in ST else []):
                    slot = nxt("w", 2)
                    L = S // dil
                    load_w(slot, 0, W[:, C_AQ + g * 128:C_AQ + g * 128 + 128], 128, "aq")
                    load_w(slot, 128, W[:, C_AK + g * 128:C_AK + g * 128 + 128], 128, "ak")
                    load_w(slot, 256, W[:, C_AV + g * 128:C_AV + g * 128 + 128], 128, "av")
                    for which in range(2):
                        def evac(tb, b, which=which):
                            dst = QP[:, which, :].rearrange("p (r l) -> p r l", r=dil)[:, :, tb * 512 // dil:(tb + 1) * 512 // dil]
                            src = psum[:, b, :].rearrange("p (l r) -> p r l", r=dil)
                            cp(dst, src, [psn(b)], ["QP%d" % which])
                        proj_fm(slot, which * 128, 128, evac)

                    def tok_ap(t, c):
                        pos0 = t * 128
                        r = pos0 // L
                        l0 = pos0 % L
                        st_ = r + dil * l0
                        return hnT[:, c, st_:st_ + dil * 127 + 1:dil]

                    def evac_v(t, b):
                        cp(VA[:, t, 0:2, 0:64], psum[:, b, 0:128].rearrange("p (h d) -> p h d", h=2), [psn(b)], ["VA"])
                    proj_tm(slot, 256, 128, tok_ap, evac_v)
                    tps = L // 128
                    Sv = lambda b: psum[:, b:b + 2, 0:256]
                    astr = strips[:, OFF_A(g):OFF_A(g) + 512]
                    def a_front(i):
                        has_prev = (i % tps) != 0
                        sb_ = (0, 5)[nxt("s", 2)]
                        spair = [psn(sb_), psn(sb_ + 1)]
                        qc = slice(i * 128, (i + 1) * 128)
                        for hh in range(2):
                            pr = slice(64 * hh, 64 * hh + 64)
                            if has_prev:
                                P.op("tensor", lambda e, hh=hh, pr=pr: e.matmul(psum[:, sb_ + hh, 0:128], lhsT=QP[pr, 1, (i - 1) * 128:i * 128],
                                                                              rhs=QP[pr, 0, qc], start=True, stop=True),
                                     reads=["QP0", "QP1"], writes=spair, inc=False)
                            P.op("tensor", lambda e, hh=hh, pr=pr: e.matmul(psum[:, sb_ + hh, 128:256], lhsT=QP[pr, 1, qc],
                                                                          rhs=QP[pr, 0, qc], start=True, stop=True),
                                 reads=["QP0", "QP1"], writes=spair, inc=(hh == 1))

                        def mid():
                            eb = nxt("e", 2)
                            pb = nxt("p", 3)
                            lo = 0 if has_prev else 128
                            Ev = Ebuf[:, eb, :].rearrange("p (h x) -> p h x", h=2)
                            Pv = Pbuf[:, pb, :].rearrange("p (h x) -> p h x", h=2)
                            Av = astr.rearrange("p (h x) -> p h x", h=2)
                            P.op("scalar", lambda e: e.activation(out=Ev[:, :, lo:256], in_=Sv(sb_)[:, :, lo:256], func=AF.Exp, scale=0.125),
                                 reads=spair, writes=["E%d" % eb])
                            P.op("vector", lambda e: e.tensor_tensor(out=Pv[:, :, lo:256], in0=Ev[:, :, lo:256], in1=Av[:, :, lo:256], op=ALU.mult),
                                 reads=["E%d" % eb, "strips"], writes=["P%d" % pb])
                            return pb

                        def back(pb):
                            ob = nxt("o", 2, 3)
                            for hh in range(2):
                                if has_prev:
                                    P.op("tensor", lambda e, hh=hh: e.matmul(psum[:, ob, hh * 128:(hh + 1) * 128], lhsT=VA[:, i - 1, hh, :],
                                                                           rhs=Pbuf[:, pb, hh * 256:hh * 256 + 128], start=True, stop=False),
                                         reads=["VA", "P%d" % pb], writes=[psn(ob)], inc=False)
                                P.op("tensor", lambda e, hh=hh: e.matmul(psum[:, ob, hh * 128:(hh + 1) * 128], lhsT=VA[:, i, hh, :],
                                                                       rhs=Pbuf[:, pb, hh * 256 + 128:hh * 256 + 256], start=(not has_prev), stop=True),
                                     reads=["VA", "P%d" % pb], writes=[psn(ob)], inc=(hh == 1))
                            pos0 = i * 128
                            r = pos0 // L
                            l0 = pos0 % L
                            st_ = r + dil * l0
                            cols = slice(st_, st_ + dil * 127 + 1, dil)
                            for hh in range(2):
                                P.op("scalar", lambda e, hh=hh: e.copy(out=QP[64 * hh:64 * hh + 64, 2 + g, cols], in_=psum[0:64, ob, hh * 128:(hh + 1) * 128]),
                                     reads=[psn(ob)], writes=["QP%d" % (2 + g)])
                                if g == 0:
                                    P.op("vector", lambda e, hh=hh: e.tensor_copy(out=dent[64 * hh:64 * hh + 64, cols], in_=psum[64:128, ob, hh * 128:(hh + 1) * 128]),
                                         reads=[psn(ob)], writes=["dent"])
                                else:
                                    P.op("vector", lambda e, hh=hh: e.tensor_tensor(out=dent[64 * hh:64 * hh + 64, cols], in0=dent[64 * hh:64 * hh + 64, cols],
                                                                                  in1=psum[64:128, ob, hh * 128:(hh + 1) * 128], op=ALU.add),
                                         reads=[psn(ob), "dent"], writes=["dent"])
                        return mid, back

                    pendA = None
                    for i in range(16):
                        mid, back = a_front(i)
                        if pendA is not None:
                            pendA[0](pendA[1])
                        pb = mid()
                        pendA = (back, pb)
                    pendA[0](pendA[1])
                if "A" in ST:
                    P.op("vector", lambda e: e.reciprocal(out=dent[:, :], in_=dent[:, :]), reads=["dent"], writes=["dent"])
                for g in (range(3) if "A" in ST else []):
                    P.op("vector", lambda e, g=g: e.tensor_tensor(out=omT[:, g, :], in0=QP[:, 2 + g, :], in1=dent[:, :], op=ALU.mult),
                         reads=["QP%d" % (2 + g), "dent"], writes=["omT"])

                P.phase = "C"
                for j in (range(2) if "C" in ST else []):
                    slot = nxt("w", 2)
                    load_w(slot, 0, W[:, C_CQ + j * 128:C_CQ + j * 128 + 128], 128, "cq")
                    load_w(slot, 128, W[:, C_CK + j * 128:C_CK + j * 128 + 128], 128, "ck")
                    load_w(slot, 256, W[:, C_CV + j * 128:C_CV + j * 128 + 128], 128, "cv")
                    for hh in range(2):
                        P.op("vector", lambda e, hh=hh: e.memset(QP[64:72, hh, 0:1024], 0.0), writes=["QP%d" % hh])
                        P.dma("gpsimd", "ldg", QP[64:72, 2 + hh, :], indm_d, writes=["QP%d" % (2 + hh)])
                    for which in range(2):
                        def evac(tb, b, which=which):
                            for hh in range(2):
                                cp(QP[0:64, which * 2 + hh, tb * 512:(tb + 1) * 512], psum[64 * hh:64 * hh + 64, b, :], [psn(b)], ["QP%d" % (which * 2 + hh)])
                        proj_fm(slot, which * 128, 128, evac)

                    def evac_v(t, b):
                        cp(VA[:, t, 0:2, 0:64], psum[:, b, 0:128].rearrange("p (h d) -> p h d", h=2), [psn(b)], ["VA"])
                    proj_tm(slot, 256, 128, lambda t, c: hnT[:, c, t * 128:(t + 1) * 128], evac_v)
                    for hh in range(2):
                        h = 12 + 2 * j + hh
                        qn, kn = "QP%d" % hh, "QP%d" % (2 + hh)
                        P.op("vector", lambda e, hh=hh: e.tensor_reduce(out=gx[0:64, 0:8], in_=QP[0:64, 2 + hh, :].rearrange("p (n k) -> p n k", k=256),
                                                                      axis=AX.X, op=ALU.add),
                             reads=[kn], writes=["gx"])
                        P.op("vector", lambda e, hh=hh: e.tensor_scalar(out=kmT[0:64, hh * 8:hh * 8 + 8], in0=gx[0:64, 0:8], scalar1=1.0 / 256, scalar2=None, op0=ALU.mult),
                             reads=["gx"], writes=["kmT"])
                        for i in range(8, 16):
                            b = nxt("m", 2, 5)
                            P.op("tensor", lambda e, b=b, i=i, hh=hh: e.matmul(psum[:, b, 0:8], lhsT=QP[0:64, hh, i * 128:(i + 1) * 128], rhs=kmT[0:64, hh * 8:hh * 8 + 8],
                                                                             start=True, stop=True),
                                 reads=[qn, "kmT"], writes=[psn(b)])
                            P.op("vector", lambda e, b=b, i=i: e.tensor_tensor(out=scb[:, 0:8], in0=psum[:, b, 0:8], in1=tabs[:, 384 + (i - 8) * 8:384 + (i - 8) * 8 + 8], op=ALU.add),
                                 reads=[psn(b), "tabs"], writes=["scb"])
                            P.op("vector", lambda e: e.max(out=m8[:, 0:8], in_=scb[:, 0:8]), reads=["scb"], writes=["m8"])
                            P.op("vector", lambda e: e.tensor_scalar(out=nmk[:, 0:8], in0=scb[:, 0:8], scalar1=m8[:, 3:4], scalar2=None, op0=ALU.is_ge),
                                 reads=["scb", "m8"], writes=["nmk"])
                            P.op("vector", lambda e: e.tensor_scalar(out=nmk[:, 0:8], in0=nmk[:, 0:8], scalar1=-1.0, scalar2=BIG, op0=ALU.add, op1=ALU.mult),
                                 reads=["nmk"], writes=["nmk"])
                            b2 = nxt("m", 2, 5)
                            P.op("tensor", lambda e, b2=b2: e.transpose(psum[0:8, b2, 0:128], nmk[:, 0:8], ident), reads=["nmk", "tabs"], writes=[psn(b2)])
                            P.op("scalar", lambda e, b2=b2, i=i, hh=hh: e.copy(out=QP[64:72, hh, i * 128:(i + 1) * 128], in_=psum[0:8, b2, 0:128]),
                                 reads=[psn(b2)], writes=[qn])
                        for qb in range(4):
                            tiles = []
                            for kt in range(4 * qb + 4):
                                dl = 512 * qb - 128 * kt
                                if dl >= 256:
                                    mode, strip = "const", None
                                else:
                                    mode, strip = "eb", SB(h, dl + 384, 512)
                                tiles.append((QP[0:72, 2 + hh, kt * 128:(kt + 1) * 128], [kn], VA[:, kt, hh, :], ["VA"], mode, strip, None,
                                              max(0, -dl), 512))

                            def evac_o(ob, qb=qb, hh=hh, j=j):
                                rb = nxt("e", 2)
                                P.op("scalar", lambda e: e.activation(out=rtmp[0:64, rb, :], in_=psum[64:128, ob, :], func=AF.Ln), reads=[psn(ob)], writes=["rtmp%d" % rb])
                                P.op("scalar", lambda e: e.activation(out=rtmp[0:64, rb, :], in_=rtmp[0:64, rb, :], func=AF.Exp, scale=-1.0), reads=["rtmp%d" % rb], writes=["rtmp%d" % rb])
                                P.op("vector", lambda e: e.tensor_tensor(out=omT[64 * hh:64 * hh + 64, 6 + j, qb * 512:(qb + 1) * 512], in0=psum[0:64, ob, :],
                                                                       in1=rtmp[0:64, rb, :], op=ALU.mult),
                                     reads=[psn(ob), "rtmp%d" % rb], writes=["omT"])
                            attend_block(h, QP[0:72, hh, qb * 512:(qb + 1) * 512], 72, tiles, [qn], evac_o)
                        flush_all()

                P.phase = "B"
                for g in (range(2) if "B" in ST else []):
                    P.phase = "B.compress"
                    slot = nxt("w", 2)
                    for kv, cbase in enumerate((C_BKC, C_BVC)):
                        for rep in range(2):
                            load_w(slot, kv * 128 + rep * 64, W[:, cbase + g * 64:cbase + g * 64 + 64], 64, "kc")
                    for kv in range(2):
                        def evac(tb, b, kv=kv):
                            cp(QP[0:64, kv, tb * 512:(tb + 1) * 512], psum[0:64, b, :], [psn(b)], ["QP%d" % kv])
                            if tb == 0:
                                cp(QP[64:128, kv, 0:511], psum[64:128, b, 1:512], [psn(b)], ["QP%d" % kv])
                            else:
                                cp(QP[64:128, kv, tb * 512 - 1:tb * 512 + 511], psum[64:128, b, :], [psn(b)], ["QP%d" % kv])
                        proj_fm(slot, kv * 128, 128, evac)
                    for kv, (w1d, w2d) in enumerate(((w1k, w2k), (w1v, w2v))):
                        slot = nxt("w", 2)
                        w1s = wsl[:, slot, :, :].rearrange("p c n -> p (c n)").rearrange("p (r m) -> p r m", m=256)
                        P.dma("gpsimd", "ldw%d" % slot, w1s, w1d[l].rearrange("(r p) m -> p r m", p=128), writes=["wsl%d" % slot])
                        P.dma("gpsimd", "ldg", w2s, w2d[l].rearrange("(m p) d -> p m d", p=128), reads=["gT"], writes=["w2s"])
                        P.dma("gpsimd", "ldg", pe2s, pe2[l, kv], writes=["pe2s"])
                        for mc in range(2):
                            b = nxt("m", 2, 5)
                            for r in range(16):
                                P.op("tensor", lambda e, b=b, r=r, mc=mc: e.matmul(psum[:, b, 0:1], lhsT=w1s[:, r, mc * 128:(mc + 1) * 128], rhs=pe2s[:, r:r + 1],
                                                                                 start=(r == 0), stop=(r == 15)),
                                     reads=["wsl%d" % slot, "pe2s"], writes=[psn(b)], inc=(r == 15))
                            P.op("vector", lambda e, b=b, mc=mc: e.tensor_copy(out=hb[:, mc:mc + 1], in_=psum[:, b, 0:1]), reads=[psn(b)], writes=["hb"])
                            b = nxt("m", 2, 5)
                            for r in range(16):
                                P.op("tensor", lambda e, b=b, r=r, mc=mc: e.matmul(psum[:, b, 0:127], lhsT=w1s[:, r, mc * 128:(mc + 1) * 128],
                                                                                 rhs=QP[:, kv, 2 * r:2 * r + 16 * 126 + 1:16], start=(r == 0), stop=(r == 15)),
                                     reads=["wsl%d" % slot, "QP%d" % kv], writes=[psn(b)], inc=(r == 15))
                            P.op("scalar", lambda e, b=b, mc=mc: e.activation(out=gx[:, 0:127], in_=psum[:, b, 0:127], func=AF.Identity, bias=hb[:, mc:mc + 1], scale=1.0),
                                 reads=[psn(b), "hb"], writes=["gx"])
                            P.op("vector", lambda e: e.tensor_tensor(out=gu[:, 0:127], in0=gx[:, 0:127], in1=gx[:, 0:127], op=ALU.mult), reads=["gx"], writes=["gu"])
                            P.op("vector", lambda e: e.tensor_scalar(out=gu[:, 0:127], in0=gu[:, 0:127], scalar1=0.044715, scalar2=1.0, op0=ALU.mult, op1=ALU.add),
                                 reads=["gu"], writes=["gu"])
                            P.op("vector", lambda e: e.tensor_tensor(out=gu[:, 0:127], in0=gu[:, 0:127], in1=gx[:, 0:127], op=ALU.mult), reads=["gu", "gx"], writes=["gu"])
                            P.op("scalar", lambda e: e.activation(out=gu[:, 0:127], in_=gu[:, 0:127], func=AF.Sigmoid, scale=2.0 * math.sqrt(2.0 / math.pi)),
                                 reads=["gu"], writes=["gu"])
                            P.op("vector", lambda e, mc=mc: e.tensor_tensor(out=gT[:, mc, 0:127], in0=gu[:, 0:127], in1=gx[:, 0:127], op=ALU.mult),
                                 reads=["gu", "gx"], writes=["gT"])
                        b = nxt("m", 2, 5)
                        if kv == 0:
                            for mc in range(2):
                                P.op("tensor", lambda e, b=b, mc=mc: e.matmul(psum[0:64, b, 0:128], lhsT=w2s[:, mc, :], rhs=gT[:, mc, :], start=(mc == 0), stop=(mc == 1)),
                                     reads=["w2s", "gT"], writes=[psn(b)], inc=(mc == 1))
                            P.op("vector", lambda e, b=b: e.tensor_copy(out=KCMP[0:64, :], in_=psum[0:64, b, 0:128]), reads=[psn(b)], writes=["KCMP"])
                        else:
                            for mc in range(2):
                                P.op("tensor", lambda e, b=b, mc=mc: e.matmul(psum[:, b, 0:64], lhsT=gT[:, mc, :], rhs=w2s[:, mc, :], start=(mc == 0), stop=(mc == 1)),
                                     reads=["w2s", "gT"], writes=[psn(b)], inc=(mc == 1))
                            P.op("vector", lambda e, b=b: e.tensor_copy(out=VCMP[:, 0:64], in_=psum[:, b, 0:64]), reads=[psn(b)], writes=["VCMP"])
                    P.phase = "B.proj"
                    slot = nxt("w", 2)
                    load_w(slot, 0, W[:, C_BQ + g * 192:C_BQ + g * 192 + 192], 192, "bq")
                    load_w(slot, 192, W[:, C_BKS + g * 64:C_BKS + g * 64 + 64], 64, "bks")
                    load_w(slot, 256, W[:, C_BKW + g * 64:C_BKW + g * 64 + 64], 64, "bkw")
                    load_w(slot, 320, W[:, C_BVS + g * 64:C_BVS + g * 64 + 64], 64, "bvs")
                    load_w(slot, 384, W[:, C_BVW + g * 64:C_BVW + g * 64 + 64], 64, "bvw")
                    load_w(slot, 448, W[:, C_BG + g * 9:C_BG + g * 9 + 9], 9, "bg")
                    for hq in range(3):
                        P.op("vector", lambda e, hq=hq: e.memset(QP[64:96, hq, 0:1024], 0.0), reads=["QP0", "QP1"], writes=["QP%d" % hq])
                    P.dma("gpsimd", "ldg", QP[64:96, 3, :], inds_d, writes=["QP3"])

                    def evac1(tb, b):
                        for hh in range(2):
                            cp(QP[0:64, hh, tb * 512:(tb + 1) * 512], psum[64 * hh:64 * hh + 64, b, :], [psn(b)], ["QP%d" % hh])
                    proj_fm(slot, 0, 128, evac1)

                    def evac2(tb, b):
                        for hh in range(2):
                            cp(QP[0:64, 2 + hh, tb * 512:(tb + 1) * 512], psum[64 * hh:64 * hh + 64, b, :], [psn(b)], ["QP%d" % (2 + hh)])
                    proj_fm(slot, 128, 128, evac2)

                    def evac3(tb, b):
                        cp(QP[0:64, 4, tb * 512:(tb + 1) * 512], psum[0:64, b, :], [psn(b)], ["QP4"])
                    proj_fm(slot, 256, 64, evac3)

                    def evac4(tb, b):
                        P.op("scalar", lambda e: e.activation(out=GT[0:9, tb * 512:(tb + 1) * 512], in_=psum[0:9, b, :], func=AF.Sigmoid),
                             reads=[psn(b)], writes=["GT"])
                    proj_fm(slot, 448, 9, evac4)

                    def evac_v(t, b):
                        cp(VA[:, t, 0:2, 0:64], psum[:, b, 0:128].rearrange("p (h d) -> p h d", h=2), [psn(b)], ["VA"])
                    proj_tm(slot, 320, 128, lambda t, c: hnT[:, c, t * 128:(t + 1) * 128], evac_v)

                    def gate_mult(ob, hq, br, qb, clampden, ph):
                        rb = nxt("e", 2)
                        rn = "rtmp%d" % rb
                        if clampden:
                            P.op("vector", lambda e: e.tensor_scalar(out=rtmp[ph:ph + 64, rb, :], in0=psum[64:128, ob, :], scalar1=1e-30, scalar2=None, op0=ALU.max),
                                 reads=[psn(ob)], writes=[rn])
                            P.op("scalar", lambda e: e.activation(out=rtmp[ph:ph + 64, rb, :], in_=rtmp[ph:ph + 64, rb, :], func=AF.Ln), reads=[rn], writes=[rn])
                        else:
                            P.op("scalar", lambda e: e.activation(out=rtmp[ph:ph + 64, rb, :], in_=psum[64:128, ob, :], func=AF.Ln), reads=[psn(ob)], writes=[rn])
                        P.op("scalar", lambda e: e.activation(out=rtmp[ph:ph + 64, rb, :], in_=rtmp[ph:ph + 64, rb, :], func=AF.Exp, scale=-1.0), reads=[rn], writes=[rn])
                        b = nxt("m", 2, 5)
                        jj = hq * 3 + br
                        P.op("tensor", lambda e: e.matmul(psum[0:64, b, :], lhsT=sel_t[0:9, jj * 64:(jj + 1) * 64], rhs=GT[0:9, qb * 512:(qb + 1) * 512], start=True, stop=True),
                             reads=["sel", "GT"], writes=[psn(b)])
                        P.op("vector", lambda e: e.tensor_tensor(out=rtmp[ph:ph + 64, rb, :], in0=rtmp[ph:ph + 64, rb, :], in1=psum[0:64, b, :], op=ALU.mult),
                             reads=[rn, psn(b)], writes=[rn])
                        return rb

                    P.phase = "B.cmp"
                    for qb in range(4):
                        impb = 7 if qb >= 2 else None
                        for hq in range(3):
                            hb_ = 3 * g + hq
                            h = 6 + hb_
                            ch, ph = 3 + hb_ // 2, 64 * (hb_ % 2)
                            pbs = {}

                            def evac_o(ob, qb=qb, hq=hq, ch=ch, ph=ph):
                                rb = gate_mult(ob, hq, 0, qb, True, ph)
                                P.op("vector", lambda e: e.tensor_tensor(out=omT[ph:ph + 64, ch, qb * 512:(qb + 1) * 512], in0=psum[0:64, ob, :],
                                                                       in1=rtmp[ph:ph + 64, rb, :], op=ALU.mult),
                                     reads=[psn(ob), "rtmp%d" % rb], writes=["omT"])
                            ob = nxt("o", 2, 3)
                            sb_ = nxt("s", 3, 0)
                            s_tile(sb_, KCMP[0:64, :], QP[0:64, hq, qb * 512:(qb + 1) * 512], 64, ["KCMP", "QP%d" % hq])
                            pb = softmax_tile(sb_, "eb", h, strips[:, OFF_CMP + qb * 512:OFF_CMP + (qb + 1) * 512])
                            P.op("tensor", lambda e, ob=ob, pb=pb: e.matmul(PS(ob), lhsT=VCMP[:, :], rhs=Pbuf[:, pb, :], start=True, stop=True),
                                 reads=["VCMP", "P%d" % pb], writes=[psn(ob)])
                            if impb is not None:
                                for ti in range(4):
                                    P.op("tensor", lambda e, pb=pb, ti=ti, hq=hq: e.matmul(psum[:, 7, ti * 128 + hq * 33:ti * 128 + hq * 33 + 33],
                                                                                         lhsT=Pbuf[:, pb, ti * 128:(ti + 1) * 128], rhs=ov_t[:, 0:33], start=True, stop=True),
                                         reads=["P%d" % pb, "ov"], writes=["ps7"], inc=(ti == 3))
                            evac_o(ob)
                        if impb is not None:
                            import os as _os
                            for ti in [int(c) for c in _os.environ.get("TI_ORDER", "0123")]:
                                i = 4 * qb + ti
                                base = ti * 128
                                P.op("vector", lambda e, base=base: e.reciprocal(out=rec3[:, 0:3], in_=psum[:, 7, base + 32:base + 99:33]), reads=["ps7"], writes=["rec3"])
                                P.op("vector", lambda e, base=base: e.tensor_scalar(out=scb[:, :], in0=psum[:, 7, base:base + 32], scalar1=rec3[:, 0:1], scalar2=None, op0=ALU.mult),
                                     reads=["ps7", "rec3"], writes=["scb"])
                                for hq in (1, 2):
                                    P.op("vector", lambda e, base=base, hq=hq: e.scalar_tensor_tensor(out=scb[:, :], in0=psum[:, 7, base + hq * 33:base + hq * 33 + 32],
                                                                                                     scalar=rec3[:, hq:hq + 1], in1=scb[:, :], op0=ALU.mult, op1=ALU.add),
                                         reads=["ps7", "rec3", "scb"], writes=["scb"])
                                P.op("vector", lambda e, i=i: e.tensor_tensor(out=scb[:, :], in0=scb[:, :], in1=tabs[:, 128 + (i - 8) * 32:128 + (i - 8) * 32 + 32], op=ALU.add),
                                     reads=["scb", "tabs"], writes=["scb"])
                                P.op("vector", lambda e: e.max(out=m8[:, 0:8], in_=scb[:, :]), reads=["scb"], writes=["m8"])
                                P.op("vector", lambda e: e.match_replace(out=sctmp[:, :], in_to_replace=m8[:, 0:8], in_values=scb[:, :], imm_value=-1e30),
                                     reads=["scb", "m8"], writes=["sctmp"])
                                P.op("vector", lambda e: e.max(out=m8[:, 8:16], in_=sctmp[:, :]), reads=["sctmp"], writes=["m8"])
                                P.op("vector", lambda e: e.tensor_scalar(out=nmk[:, :], in0=scb[:, :], scalar1=m8[:, 15:16], scalar2=None, op0=ALU.is_ge),
                                     reads=["scb", "m8"], writes=["nmk"])
                                P.op("vector", lambda e: e.tensor_scalar(out=nmk[:, :], in0=nmk[:, :], scalar1=-1.0, scalar2=BIG, op0=ALU.add, op1=ALU.mult),
                                     reads=["nmk"], writes=["nmk"])
                                b2 = nxt("m", 2, 5)
                                P.op("tensor", lambda e, b2=b2: e.transpose(psum[0:32, b2, 0:128], nmk[:, :], ident), reads=["nmk", "tabs"], writes=[psn(b2)])
                                for hq in range(3):
                                    cp(QP[64:96, hq, i * 128:(i + 1) * 128], psum[0:32, b2, 0:128], [psn(b2)], ["QP%d" % hq], eng="scalar")
                    P.phase = "B.slcwin"
                    for hq in range(3):
                        hb_ = 3 * g + hq
                        h = 6 + hb_
                        ch, ph = 3 + hb_ // 2, 64 * (hb_ % 2)
                        qn = "QP%d" % hq
                        for qb in range(4):
                            for br in (1, 2):
                                tiles = []
                                if br == 1:
                                    for kt in range(4 * qb + 4):
                                        dl = 512 * qb - 128 * kt
                                        if dl >= 256:
                                            mode, strip = "const", None
                                        else:
                                            mode, strip = "eb", SB(h, dl + 384, 512)
                                        tiles.append((QP[0:96, 3, kt * 128:(kt + 1) * 128], ["QP3"], VA[:, kt, 0, :], ["VA"], mode, strip, None,
                                                      max(0, -dl), 512))
                                    kd = 96
                                else:
                                    for kt in range(max(0, 4 * qb - 4), 4 * qb + 4):
                                        dl = 512 * qb - 128 * kt
                                        if dl >= 256:
                                            mode, strip, strip2 = "cb", strips[:, OFF_UM + dl:OFF_UM + dl + 512], None
                                        elif dl == 128:
                                            mode, strip, strip2 = "eb", strips[:, OFF_UM + dl:OFF_UM + dl + 512], SB(h, dl + 384, 512)
                                        else:
                                            mode, strip, strip2 = "eb", SB(h, dl + 384, 512), None
                                        c0_ = max(0, -dl)
                                        c1_ = 512 if dl <= 128 else 640 - dl
                                        tiles.append((QP[0:64, 4, kt * 128:(kt + 1) * 128], ["QP4"], VA[:, kt, 1, :], ["VA"], mode, strip, strip2, c0_, c1_))
                                    tiles.sort(key=lambda tl: 0 if (tl[7] == 0 and tl[8] == 512) else 1)
                                    kd = 64

                                def evac_o(ob, qb=qb, hq=hq, ch=ch, ph=ph, br=br):
                                    rb = gate_mult(ob, hq, br, qb, False, ph)
                                    rn = "rtmp%d" % rb
                                    P.op("vector", lambda e: e.tensor_tensor(out=rtmp[ph:ph + 64, rb, :], in0=psum[0:64, ob, :], in1=rtmp[ph:ph + 64, rb, :], op=ALU.mult),
                                         reads=[psn(ob), rn], writes=[rn])
                                    P.op("vector", lambda e: e.tensor_tensor(out=omT[ph:ph + 64, ch, qb * 512:(qb + 1) * 512], in0=omT[ph:ph + 64, ch, qb * 512:(qb + 1) * 512],
                                                                           in1=rtmp[ph:ph + 64, rb, :], op=ALU.add),
                                         reads=["omT", rn], writes=["omT"])
                                attend_block(h, QP[0:kd, hq, qb * 512:(qb + 1) * 512], kd, tiles, [qn], evac_o)
                    flush_all()

                if debug and sq == 0 and l == 0:
                    P.dma("sync", "dbg", dbg["omT"], omT, reads=["omT"])

                P.phase = "merge"
                Gm = dent.rearrange("p (b n) -> p b n", b=4)
                for fc in (range(8) if "merge" in ST else []):
                    slot = nxt("w", 2)
                    load_w(slot, 0, w_br[l][:, fc * 128:(fc + 1) * 128], 128, "wb")
                    for gi in range(3):
                        load_w(slot, 128 + gi * 128, W[:, C_MG + gi * 1024 + fc * 128:C_MG + gi * 1024 + fc * 128 + 128], 128, "mg")
                    for tb in range(4):
                        tsl = slice(tb * 512, (tb + 1) * 512)
                        bs = [5, 6, 7]
                        gb = [nxt("s", 3, 0) for _ in range(3)]
                        rng = ((0, 3), (3, 6), (6, 8))
                        for gi in range(3):
                            for c in range(8):
                                P.op("tensor", lambda e, gi=gi, c=c: e.matmul(PS(gb[gi]), lhsT=wsl[:, slot, c, 128 + gi * 128:256 + gi * 128], rhs=hnT[:, c, tsl],
                                                                            start=(c == 0), stop=(c == 7)),
                                     reads=["wsl%d" % slot, "hnT"], writes=[psn(gb[gi])], inc=(c == 7))
                            P.op("scalar", lambda e, gi=gi: e.activation(out=Gm[:, gi, :], in_=PS(gb[gi]), func=AF.Sigmoid), reads=[psn(gb[gi])], writes=["Gm%d" % gi])
                            k0, k1 = rng[gi]
                            for k in range(k0, k1):
                                P.op("tensor", lambda e, gi=gi, k=k, k0=k0, k1=k1: e.matmul(PS(bs[gi]), lhsT=wsl[:, slot, k, 0:128], rhs=omT[:, k, tsl],
                                                                                          start=(k == k0), stop=(k == k1 - 1)),
                                     reads=["wsl%d" % slot, "omT"], writes=[psn(bs[gi])], inc=(k == k1 - 1))
                        P.op("vector", lambda e: e.tensor_tensor(out=Gm[:, 0, :], in0=Gm[:, 0, :], in1=PS(bs[0]), op=ALU.mult), reads=["Gm0", psn(bs[0])], writes=["Gm0"])
                        P.op("vector", lambda e: e.tensor_tensor(out=Gm[:, 1, :], in0=Gm[:, 1, :], in1=PS(bs[1]), op=ALU.mult), reads=["Gm1", psn(bs[1])], writes=["Gm1"])
                        P.op("vector", lambda e: e.tensor_tensor(out=Gm[:, 2, :], in0=Gm[:, 2, :], in1=PS(bs[2]), op=ALU.mult), reads=["Gm2", psn(bs[2])], writes=["Gm2"])
                        P.op("vector", lambda e: e.tensor_tensor(out=Gm[:, 0, :], in0=Gm[:, 0, :], in1=Gm[:, 1, :], op=ALU.add), reads=["Gm0", "Gm1"], writes=["Gm0"])
                        P.op("vector", lambda e, fc=fc, tsl=tsl: e.tensor_tensor(out=mergedT[:, fc, tsl], in0=Gm[:, 0, :], in1=Gm[:, 2, :], op=ALU.add),
                             reads=["Gm0", "Gm2", "QP0", "QP1", "QP2", "QP3", "QP4", "VA"], writes=["mergedT"])

                P.phase = "outproj"
                wo = wsl_raw[:].rearrange("p (c n) -> p c n", c=8)
                P.dma("gpsimd", "ldw0", wo, w_out[l].rearrange("(c p) n -> p c n", p=128), reads=["wsl1"], writes=["wsl0", "wsl1"])
                prev_post = None
                for t in (range(16) if "outproj" in ST else []):
                    xb = nxt("x", 2)
                    xn = "xt%d" % xb
                    P.dma("sync", "ldx%d" % xb, xt[:, xb, :], src_x(t), reads=["scr"], writes=[xn])
                    for hf in range(2):
                        b = nxt("f", 8, 0)
                        for c in range(8):
                            P.op("tensor", lambda e, b=b, c=c, hf=hf, t=t: e.matmul(PS(b), lhsT=mergedT[:, c, t * 128:(t + 1) * 128], rhs=wo[:, c, hf * 512:(hf + 1) * 512],
                                                                                  start=(c == 0), stop=(c == 7)),
                                 reads=["mergedT", "wsl0", "wsl1"], writes=[psn(b)], inc=(c == 7))
                        P.op("vector", lambda e, b=b, hf=hf: e.tensor_tensor(out=xt[:, xb, hf * 512:(hf + 1) * 512], in0=xt[:, xb, hf * 512:(hf + 1) * 512], in1=PS(b), op=ALU.add),
                             reads=[psn(b), xn], writes=[xn])
                    P.dma("sync", "stx", scr[t * 128:(t + 1) * 128, :], xt[:, xb, :], reads=[xn], writes=["scr_t%d" % t])
                    if debug and sq == 0 and l == 0:
                        P.dma("sync", "dbg", dbg["x1"][t * 128:(t + 1) * 128, :], xt[:, xb, :], reads=[xn])
                    norm_tile(xb, t, (l * 2 + 1) * 8, post=False)
                    if prev_post is not None:
                        norm_post(*prev_post)
                    prev_post = (xb, t, (l * 2 + 1) * 8)
                if prev_post is not None:
                    norm_post(*prev_post)
                P.barrier()

                P.phase = "ffn"
                P.dma("sync", "ldc", cw_t[:], cwt[l], writes=["cw"])
                P.op("vector", lambda e: e.memset(halo[:], 0.0), writes=["halo%d" % q_ for q_ in range(44)])
                wsl4 = wsl_raw[:].rearrange("p (s c n) -> p s c n", s=4, c=8)
                for hf in (range(2) if "ffn" in ST else []):
                    P.phase = "ffn.up"
                    for j in range(22):
                        s4 = nxt("w4", 4)
                        wn = "wf%d" % s4
                        P.dma("gpsimd", "ldf%d" % s4, wsl4[:, s4, :, 0:128], w_up[l][:, j * 128:(j + 1) * 128].rearrange("(c p) n -> p c n", p=128),
                              writes=[wn, "wsl%d" % (s4 // 2)])
                        P.dma("gpsimd", "ldf%d" % s4, wsl4[:, s4, :, 128:256], w_up[l][:, DFF + j * 128:DFF + (j + 1) * 128].rearrange("(c p) n -> p c n", p=128),
                              writes=[wn])
                        par = j % 2
                        for au in range(2):
                            rbuf = rbufs[au][par]
                            obuf = obufs[au][par]
                            jj = au * 22 + j
                            rn, on = "r%d_%d" % (au, par), "o%d_%d" % (au, par)
                            P.op("scalar", lambda e, rbuf=rbuf, jj=jj: e.copy(out=rbuf[:, 0:2], in_=halo[:, jj, :]), reads=["halo%d" % jj], writes=[rn])
                            for tbl in range(2):
                                tb = hf * 2 + tbl
                                b = nxt("f", 8, 0)
                                for c in range(8):
                                    P.op("tensor", lambda e, b=b, c=c, tb=tb, au=au: e.matmul(PS(b), lhsT=wsl4[:, s4, c, au * 128:(au + 1) * 128], rhs=hnT[:, c, tb * 512:(tb + 1) * 512],
                                                                                            start=(c == 0), stop=(c == 7)),
                                         reads=[wn, "hnT"], writes=[psn(b)], inc=(c == 7))
                                P.op("scalar", lambda e, b=b, rbuf=rbuf, tbl=tbl: e.copy(out=rbuf[:, 2 + tbl * 512:2 + (tbl + 1) * 512], in_=PS(b)), reads=[psn(b)], writes=[rn])
                            cwb = jj * 4
                            P.op("scalar", lambda e, rbuf=rbuf, obuf=obuf, cwb=cwb: e.activation(out=obuf[:, :], in_=rbuf[:, 2:1026], func=AF.Identity,
                                                                                               bias=cw_t[:, cwb + 3:cwb + 4], scale=cw_t[:, cwb + 2:cwb + 3]),
                                 reads=[rn, "cw"], writes=[on])
                            P.op("vector", lambda e, rbuf=rbuf, obuf=obuf, cwb=cwb: e.scalar_tensor_tensor(out=obuf[:, :], in0=rbuf[:, 1:1025], scalar=cw_t[:, cwb + 1:cwb + 2],
                                                                                                         in1=obuf[:, :], op0=ALU.mult, op1=ALU.add),
                                 reads=[rn, on, "cw"], writes=[on])
                            P.op("vector", lambda e, rbuf=rbuf, obuf=obuf, cwb=cwb: e.scalar_tensor_tensor(out=obuf[:, :], in0=rbuf[:, 0:1024], scalar=cw_t[:, cwb:cwb + 1],
                                                                                                         in1=obuf[:, :], op0=ALU.mult, op1=ALU.add),
                                 reads=[rn, on, "cw"], writes=[on])
                            if hf == 0:
                                P.op("scalar", lambda e, rbuf=rbuf, jj=jj: e.copy(out=halo[:, jj, :], in_=rbuf[:, 1024:1026]), reads=[rn], writes=["halo%d" % jj])
                        oa_, ou_ = obufs[0][par], obufs[1][par]
                        P.op("scalar", lambda e: e.activation(out=sg[:, :], in_=oa_[:, :], func=AF.Silu), reads=["o0_%d" % par], writes=["sg"])
                        P.op("vector", lambda e, j=j: e.tensor_tensor(out=gTf[:, j, :], in0=sg[:, :], in1=ou_[:, :], op=ALU.mult), reads=["sg", "o1_%d" % par], writes=["gTf%d" % j])
                    P.phase = "ffn.dn"
                    xt4 = xt[:].rearrange("p b (h n) -> p (b h) n", h=2)
                    for chh in range(2):
                        xbufs = {}
                        for tt in range(4):
                            t = hf * 8 + tt
                            xn = "xq%d" % tt
                            P.dma("sync", "ldx4_%d" % tt, xt4[:, tt, :], scr[t * 128:(t + 1) * 128, chh * 512:(chh + 1) * 512],
                                  reads=["scr_t%d" % t], writes=[xn, "xt0", "xt1"])
                            xbufs[tt] = (tt, xn)
                        for j in range(22):
                            sd = nxt("wd", 6)
                            dn = "wd%d" % sd
                            P.dma("gpsimd", "ldd%d" % sd, wdsl[:, sd, :], w_dn[l][j * 128:(j + 1) * 128, chh * 512:(chh + 1) * 512], writes=[dn])
                            for tt in range(8):
                                P.op("tensor", lambda e, j=j, tt=tt: e.matmul(PS(tt), lhsT=gTf[:, j, tt * 128:(tt + 1) * 128], rhs=wdsl[:, sd, :], start=(j == 0), stop=(j == 21)),
                                     reads=["gTf%d" % j, dn], writes=[psn(tt)], inc=(tt == 7))
                        def dn_add(tt):
                            xb4, xn = xbufs[tt]
                            P.op("vector", lambda e: e.tensor_tensor(out=xt4[:, xb4, :], in0=xt4[:, xb4, :], in1=PS(tt), op=ALU.add), reads=[psn(tt), xn], writes=[xn])

                        def dn_store(tt):
                            t = hf * 8 + tt
                            xb4, xn = xbufs[tt]
                            P.dma("sync", "stx", scr[t * 128:(t + 1) * 128, chh * 512:(chh + 1) * 512], xt4[:, xb4, :], reads=[xn], writes=["scr_t%d" % t, "scr"])

                        for tt in range(4):
                            dn_add(tt)
                        for tt in range(4):
                            dn_store(tt)
                        for tt in range(4, 8):
                            t2 = hf * 8 + tt
                            xb4, xn = xbufs[tt - 4]
                            P.dma("sync", "ldx4_%d" % xb4, xt4[:, xb4, :], scr[t2 * 128:(t2 + 1) * 128, chh * 512:(chh + 1) * 512], reads=["scr_t%d" % t2], writes=[xn])
                            xbufs[tt] = (xb4, xn)
                        for tt in range(4, 8):
                            dn_add(tt)
                            dn_store(tt)
                P.barrier()
            P.phase = "final"
            for t in (range(16) if "final" in ST else []):
                xb = nxt("x", 2)
                xn = "xt%d" % xb
                P.dma("sync", "ldx%d" % xb, xt[:, xb, :], scr[t * 128:(t + 1) * 128, :], reads=["scr", "scr_t%d" % t], writes=[xn])
                norm_tile(xb, t, None)
                P.op("vector", lambda e, xb=xb: e.tensor_tensor(out=xt[:, xb, :], in0=xt[:, xb, :], in1=gfin_t[:, :], op=ALU.mult), reads=[xn, "gfin"], writes=[xn])
                final_toks.append(P.dma("sync", "sto", out_d[sq, t * 128:(t + 1) * 128, :], xt[:, xb, :], reads=[xn], writes=["scr"]))
            P.barrier()
        P.finish(final_toks)
    return nc


_CACHE = {}


def _prep_inputs(inputs):
    f = lambda a: np.ascontiguousarray(np.asarray(a, np.float32))
    hc = _host_consts(inputs["rel_bias"])
    gains = np.zeros((128, 32), np.float32)
    for l in range(DEPTH):
        gains[:, (l * 2 + 0) * 8:(l * 2 + 0) * 8 + 8] = f(inputs["norm_mix"])[l].reshape(8, 128).T
        gains[:, (l * 2 + 1) * 8:(l * 2 + 1) * 8 + 8] = f(inputs["norm_ffn"])[l].reshape(8, 128).T
    cwt = np.zeros((DEPTH, 128, 176), np.float32)
    for l in range(DEPTH):
        cw = f(inputs["conv_w"])[l]
        cb = f(inputs["conv_b"])[l]
        for k in range(3):
            cwt[l, :, k::4] = cw[k].reshape(44, 128).T
        cwt[l, :, 3::4] = cb.reshape(44, 128).T
    pe2 = np.zeros((DEPTH, 2, 128, 16), np.float32)
    for l in range(DEPTH):
        for kv, nm in enumerate(("cmp_pe_k", "cmp_pe_v")):
            pe = f(inputs[nm])[l]
            pe2[l, kv] = pe.reshape(16, 2, 64).transpose(1, 2, 0).reshape(128, 16)
    shared = dict(
        w_in=f(inputs["w_in"]), cmp_w1_k=f(inputs["cmp_w1_k"]), cmp_w2_k=f(inputs["cmp_w2_k"]),
        cmp_w1_v=f(inputs["cmp_w1_v"]), cmp_w2_v=f(inputs["cmp_w2_v"]), pe2=pe2,
        w_branch=f(inputs["w_branch"]), w_out=f(inputs["w_out"]), w_up=f(inputs["w_up"]), w_down=f(inputs["w_down"]),
        gains_in=gains, cwt=cwt, norm_final=f(inputs["norm_final"]),
        strips_in=hc["strips"], tabs_in=hc["tabs"], ov_in=hc["ov"], inds=hc["inds"], indm=hc["indm"], sel_in=hc["sel"],
    )
    x = f(inputs["x"])
    return [dict(shared, x=np.ascontiguousarray(x[c * SEQ_PER_CORE:(c + 1) * SEQ_PER_CORE])) for c in range(NCORE)]


def kernel(**inputs):
    if "nc" not in _CACHE:
        _CACHE["nc"] = build_program()
    nc = _CACHE["nc"]
    in_maps = _prep_inputs(inputs)
    res = run_bass_kernel_spmd(nc, in_maps, core_ids=list(range(NCORE)))
    out = np.concatenate([np.asarray(r["out"], np.float32) for r in res.results], axis=0)
    return out
```

```python
import math
from contextlib import ExitStack
import numpy as np
import concourse.bass as bass
import concourse.mybir as mybir
from concourse.bass_utils import run_bass_kernel_spmd

F32 = mybir.dt.float32
BF16 = mybir.dt.bfloat16
AF = mybir.ActivationFunctionType
ALU = mybir.AluOpType
AX = mybir.AxisListType

S = 2048
D = 1024
NCORE = 8
SEQ_PER_CORE = 2
DEPTH = 2
INC = 6162
DFF = 2816
BIG = 240000.0
NEGB = -30000.0
C_AQ, C_AK, C_AV = 0, 384, 768
C_BQ, C_BKC, C_BVC, C_BKS, C_BVS, C_BKW, C_BVW, C_BG = 1152, 1536, 1664, 1792, 1920, 2048, 2176, 2304
C_CQ, C_CK, C_CV, C_MG = 2322, 2578, 2834, 3090
A_DIL = (1, 4, 16)
NSTRIP = 10 * 1024 + 3 * 512 + 1024 + 2048
OFF_B = lambda h: (h - 6) * 1024
OFF_A = lambda g: 10240 + g * 512
OFF_UM = 10240 + 1536
OFF_CMP = OFF_UM + 1024

ENGS = ("tensor", "vector", "scalar", "gpsimd", "sync")


class Unit:
    __slots__ = ("w", "r")

    def __init__(self):
        self.w = None
        self.r = {}


class _Rec:
    def __init__(self):
        self.call = None

    def __getattr__(self, name):
        def f(*a, **kw):
            self.call = (name, a, kw)
            return self
        return f


class Prog:
    def __init__(self, nc, stack):
        self.nc = nc
        self.stack = stack
        self.q = {e: [] for e in ENGS}
        self.sems = {}
        self.cnt = {}
        self.seen = {e: {} for e in ENGS}
        for e in ENGS:
            self._sem("E_" + e)
        self.units = {}
        self.waitmax = {}
        self.limit = None
        self.nrec = 0
        self.log = []
        self.phase = "setup"
        self.pe_phase = []

    def _sem(self, key):
        if key not in self.sems:
            self.sems[key] = self.stack.enter_context(self.nc.semaphore(key))
            self.cnt[key] = 0
        return self.sems[key]

    def unit(self, key):
        u = self.units.get(key)
        if u is None:
            u = Unit()
            self.units[key] = u
        return u

    def _deps(self, eng, reads, writes, same_war=False, own_sem=None):
        need = {}

        def add(tok, kind):
            if tok is None:
                return
            k, v = tok
            if kind == "waw" and own_sem is not None and k == own_sem:
                return
            if k == "E_" + eng:
                if eng == "tensor":
                    return
                if kind == "war" and not same_war:
                    return
            if v > need.get(k, 0):
                need[k] = v

        for u in reads:
            add(u.w, "raw")
        for u in writes:
            add(u.w, "waw")
            for k, v in u.r.items():
                add((k, v), "war")
        out = []
        seen = self.seen[eng]
        for k in list(need):
            if not k.startswith("E_"):
                need[k] = self.cnt[k]
        for k, v in need.items():
            if seen.get(k, 0) < v:
                seen[k] = v
                out.append((k, v))
                if not k.startswith("E_") and v > self.waitmax.get(k, 0):
                    self.waitmax[k] = v
        return out

    def _mark(self, tok, reads, writes):
        for u in reads:
            if u.r.get(tok[0], 0) < tok[1]:
                u.r[tok[0]] = tok[1]
        for u in writes:
            u.w = tok
            u.r = {}

    def op(self, eng, fn, reads=(), writes=(), inc=True):
        self.nrec += 1
        if self.limit is not None and self.nrec > self.limit:
            return None
        if "hnT" in reads:
            reads = list(reads) + ["hnTd"]
        reads = [self.unit(u) for u in reads]
        writes = [self.unit(u) for u in writes]
        waits = self._deps(eng, reads, writes)
        k = "E_" + eng
        if inc:
            self.cnt[k] += 1
            tok = (k, self.cnt[k])
        else:
            tok = (k, self.cnt[k] + 1)
        rec = _Rec()
        fn(rec)
        name, a, kw = rec.call
        self.log.append((self.nrec, eng, name, [str(getattr(x, "shape", x)) for x in list(a) + list(kw.values())][:4]))
        if eng == "tensor":
            self.pe_phase.append(self.phase)
        self.q[eng].append((waits, (lambda e: getattr(e, name)(*a, **kw)), (k, 1) if inc else None))
        self._mark(tok, reads, writes)
        return tok

    def dma(self, eng, semkey, out, in_, reads=(), writes=()):
        self.nrec += 1
        if self.limit is not None and self.nrec > self.limit:
            return None
        self.log.append((self.nrec, eng, "dma", semkey))
        reads = [self.unit(u) for u in reads]
        writes = [self.unit(u) for u in writes]
        self._sem(semkey)
        waits = self._deps(eng, reads, writes, same_war=True, own_sem=semkey)
        wm = self.waitmax.get(semkey, 0)
        if wm > self.seen[eng].get(semkey, 0):
            self.seen[eng][semkey] = wm
            waits.append((semkey, wm))
        self.cnt[semkey] += 16
        tok = (semkey, self.cnt[semkey])
        self.q[eng].append((waits, lambda e: e.dma_start(out=out, in_=in_), (semkey, 16)))
        self._mark(tok, reads, writes)
        return tok

    def barrier(self):
        cur = [(k, v) for k, v in self.cnt.items() if v > 0]
        for e in ENGS:
            waits = []
            for k, v in cur:
                if k == "E_" + e:
                    continue
                if self.seen[e].get(k, 0) < v:
                    self.seen[e][k] = v
                    waits.append((k, v))
                    if not k.startswith("E_") and v > self.waitmax.get(k, 0):
                        self.waitmax[k] = v
            if waits:
                self.q[e].append((waits, None, None))

    def finish(self, final_tokens):
        prog = self
        fin = [t for t in final_tokens if t is not None]
        for k, v in self.cnt.items():
            if v > 0 and not k.startswith("E_tensor"):
                fin.append((k, v))

        def emit(engname):
            def body(e):
                for waits, fn, inc in prog.q[engname]:
                    for (k, v) in waits:
                        e.wait_ge(prog.sems[k], v)
                    if fn is None:
                        continue
                    ins = fn(e)
                    if inc is not None:
                        ins.then_inc(prog.sems[inc[0]], inc[1])
                if engname == "sync":
                    for (k, v) in fin:
                        e.wait_ge(prog.sems[k], v)
            return body

        with self.nc.Block() as block:
            block.tensor(emit("tensor"))
            block.vector(emit("vector"))
            block.scalar(emit("scalar"))
            block.gpsimd(emit("gpsimd"))
            block.sync(emit("sync"))


def _bucket(d):
    n = np.maximum(d, 0)
    nf = np.maximum(n, 1).astype(np.float32)
    large = 16 + (np.log(nf / np.float32(16)) / np.float32(math.log(8.0)) * np.float32(16)).astype(np.int32)
    return np.where(n < 16, n, np.minimum(large, 31)).astype(np.int64)


def _host_consts(rel_bias):
    rb = np.asarray(rel_bias, np.float32)
    ki = np.arange(128)[:, None]
    strips = np.zeros((128, NSTRIP), np.float32)
    u = np.arange(1024)[None, :]
    d = u - 384 - ki
    bk = _bucket(d)
    for h in range(6, 16):
        strips[:, OFF_B(h):OFF_B(h) + 1024] = np.where(d >= 0, rb[bk, h], np.float32(NEGB))
    qi = np.arange(128)[None, :]
    for g, dil in enumerate(A_DIL):
        for hh in range(2):
            h = 2 * g + hh
            ds = qi - ki + 128
            prev = np.where(ds <= 128, rb[_bucket(ds * dil), h], np.float32(NEGB))
            ds2 = qi - ki
            diag = np.where(ds2 >= 0, rb[_bucket(ds2 * dil), h], np.float32(NEGB))
            o = OFF_A(g) + hh * 256
            strips[:, o:o + 128] = prev
            strips[:, o + 128:o + 256] = diag
    w = np.arange(1024)[None, :]
    strips[:, OFF_UM:OFF_UM + 1024] = np.where(w - ki <= 511, np.float32(0), np.float32(NEGB))
    q = np.arange(2048)[None, :]
    ok = (16 * ki + 31 <= q) & (ki < 127)
    strips[:, OFF_CMP:OFF_CMP + 2048] = np.where(ok, np.float32(0), np.float32(NEGB))
    tabs = np.zeros((128, 128 + 256 + 64 + 16), np.float32)
    tabs[:, 0:128] = np.eye(128, dtype=np.float32)
    p = np.arange(128)[:, None]
    for i in range(8, 16):
        t = 128 * i + p
        jt = t // 64
        j = np.arange(32)[None, :]
        forced = (j == 0) | (j == jt) | (j == jt - 1)
        fb = np.where(j <= jt, np.where(forced, np.float32(1e4), np.float32(0)), np.float32(-1e30))
        tabs[:, 128 + (i - 8) * 32:128 + (i - 8) * 32 + 32] = fb
        nb = t // 256
        n = np.arange(8)[None, :]
        pm = np.where(n < nb, np.float32(0), np.where(n == nb, np.float32(1e4), np.float32(-1e30)))
        tabs[:, 384 + (i - 8) * 8:384 + (i - 8) * 8 + 8] = pm
    tabs[:, 448:464] = rb[31][None, :]
    ov = np.zeros((128, 64), np.float32)
    c = np.arange(128)[:, None]
    j = np.arange(32)[None, :]
    ov[:, 0:32] = ((16 * c < 64 * j + 64) & (16 * c + 32 > 64 * j) & (c < 127)).astype(np.float32)
    ov[:127, 32] = 1.0
    k = np.arange(2048)[None, :]
    inds = (k // 64 == np.arange(32)[:, None]).astype(np.float32)
    indm = (k // 256 == np.arange(8)[:, None]).astype(np.float32)
    sel = np.zeros((18, 18 * 64), np.float32)
    for jj in range(18):
        sel[jj, jj * 64:(jj + 1) * 64] = 1.0
    return dict(strips=strips, tabs=tabs, ov=ov, inds=inds, indm=indm, sel=sel)


def build_program(debug=False, stages=None, nseq=SEQ_PER_CORE, depth=DEPTH, limit=None):
    ST = stages if stages is not None else {"A", "C", "B", "merge", "outproj", "ffn", "final"}
    nc = bass.Bass("TRN2", target_bir_lowering=False)
    dt_in = lambda name, shape: nc.dram_tensor(name, shape, F32, kind="ExternalInput").ap()
    x_d = dt_in("x", [SEQ_PER_CORE, S, D])
    w_in = dt_in("w_in", [DEPTH, D, INC])
    w1k = dt_in("cmp_w1_k", [DEPTH, 2048, 256])
    w2k = dt_in("cmp_w2_k", [DEPTH, 256, 64])
    w1v = dt_in("cmp_w1_v", [DEPTH, 2048, 256])
    w2v = dt_in("cmp_w2_v", [DEPTH, 256, 64])
    pe2 = dt_in("pe2", [DEPTH, 2, 128, 16])
    w_br = dt_in("w_branch", [DEPTH, D, D])
    w_out = dt_in("w_out", [DEPTH, D, D])
    w_up = dt_in("w_up", [DEPTH, D, 2 * DFF])
    w_dn = dt_in("w_down", [DEPTH, DFF, D])
    gains = dt_in("gains_in", [128, 32])
    cwt = dt_in("cwt", [DEPTH, 128, 176])
    gfin = dt_in("norm_final", [D])
    strips_d = dt_in("strips_in", [128, NSTRIP])
    tabs_d = dt_in("tabs_in", [128, 464])
    ov_d = dt_in("ov_in", [128, 64])
    inds_d = dt_in("inds", [32, 2048])
    indm_d = dt_in("indm", [8, 2048])
    sel_d = dt_in("sel_in", [18, 18 * 64])
    out_d = nc.dram_tensor("out", [SEQ_PER_CORE, S, D], F32, kind="ExternalOutput").ap()
    scr = nc.dram_tensor("xscr", [S, D], F32, kind="Internal").ap()
    dbg = {}
    if debug:
        dbg["omT"] = nc.dram_tensor("dbg_omT", [128, 8, S], BF16, kind="ExternalOutput").ap()
        dbg["x1"] = nc.dram_tensor("dbg_x1", [S, D], F32, kind="ExternalOutput").ap()

    with ExitStack() as st:
        P = Prog(nc, st)
        P.limit = limit
        build_program.P = P
        sb = lambda name, shape, dt: st.enter_context(nc.sbuf_tensor(name, shape, dt))
        hnT = sb("hnT", [128, 8, S], BF16)
        wsl_raw = sb("wsl", [128, 8192], BF16)
        wsl = wsl_raw[:].rearrange("p (s c n) -> p s c n", s=2, c=8)
        strips = sb("strips", [128, NSTRIP], BF16)
        tabs = sb("tabs", [128, 464], F32)
        gains_t = sb("gains", [128, 32], F32)
        cw_t = sb("cw", [128, 176], F32)
        gfin_t = sb("gfin", [128, D], F32)
        ov_t = sb("ov", [128, 64], BF16)
        sel_t = sb("sel", [18, 18 * 64], F32)
        xt = sb("xt", [128, 2, D], F32)
        small = sb("small", [128, 64], F32)
        halo = sb("halo", [128, 44, 2], F32)
        ones_bf = sb("ones_bf", [128, 64], BF16)
        ARENA = 44 * 1024
        arena = sb("arena", [128, ARENA], BF16)
        psum = st.enter_context(nc.psum_tensor("ps", [128, 8, 512], F32))

        ident = tabs[:, 0:128]
        omT = arena[:, 0:16384].rearrange("p (c t) -> p c t", c=8)
        QP = arena[:, 16384:26624].rearrange("p (i t) -> p i t", i=5)
        VA = arena[:, 26624:32768].rearrange("p (t h e) -> p t h e", t=16, h=3)
        mergedT = arena[:, 16384:32768].rearrange("p (c t) -> p c t", c=8)
        dent = arena[:, 32768:36864].bitcast(F32)
        Ebuf = arena[:, 36864:38912].bitcast(F32).rearrange("p (b n) -> p b n", b=2)
        Pbuf = arena[:, 38912:40448].rearrange("p (b n) -> p b n", b=3)
        rtmp = arena[:, 40448:42496].bitcast(F32).rearrange("p (b n) -> p b n", b=2)
        misc = arena[:, 42496:44032].bitcast(F32)
        hb = misc[:, 0:4]
        gx = misc[:, 8:136]
        gu = misc[:, 136:264]
        scb = misc[:, 264:296]
        sctmp = misc[:, 296:328]
        m8 = misc[:, 328:344]
        nmk = misc[:, 344:376]
        rec3 = misc[:, 376:380]
        KCMP = misc[:, 384:448].bitcast(BF16)
        VCMP = misc[:, 448:512].bitcast(BF16)
        gT = misc[:, 512:640].bitcast(BF16).rearrange("p (m c) -> p m c", m=2)
        w2s = misc[:, 640:704].bitcast(BF16).rearrange("p (m d) -> p m d", m=2)
        pe2s = misc[:, 704:712].bitcast(BF16)
        kmT = misc[:, 712:720].bitcast(BF16)
        GT = dent
        gTf = arena[:, 0:22528].rearrange("p (j t) -> p j t", j=22)
        wdsl = arena[:, 22528:25600].rearrange("p (s n) -> p s n", s=6)
        _o = 25600
        rbufs = [[None, None], [None, None]]
        obufs = [[None, None], [None, None]]
        for au_ in range(2):
            for par_ in range(2):
                rbufs[au_][par_] = arena[:, _o:_o + 2056].bitcast(F32)
                _o += 2056
        for au_ in range(2):
            for par_ in range(2):
                obufs[au_][par_] = arena[:, _o:_o + 2048].bitcast(F32)
                _o += 2048
        sg = arena[:, _o:_o + 2048].bitcast(F32)
        assert _o + 2048 <= ARENA

        PS = lambda b: psum[:, b, :]
        psn = lambda b: "ps%d" % b
        rr = {"s": 0, "o": 0, "m": 0, "e": 0, "p": 0, "w": 0, "x": 0, "f": 0, "w4": 0, "wd": 0}

        def nxt(kind, n, base=0):
            v = base + rr[kind] % n
            rr[kind] += 1
            return v

        P.dma("sync", "ldc", tabs[:], tabs_d, writes=["tabs"])
        P.dma("sync", "ldc", gains_t[:], gains, writes=["gains"])
        P.dma("sync", "ldc", sel_t[:], sel_d, writes=["sel"])
        P.dma("sync", "ldc", gfin_t[:], gfin.partition_broadcast(128), writes=["gfin"])
        P.dma("gpsimd", "ldg", ov_t[:], ov_d, writes=["ov"])
        P.op("vector", lambda e: e.memset(ones_bf[:], 1.0), writes=["ones_bf"])
        stg = arena[:, 0:8192].bitcast(F32)
        off = 0
        while off < NSTRIP:
            n = min(4096, NSTRIP - off)
            o_, n_ = off, n
            P.dma("sync", "ldc", stg[:, 0:n_], strips_d[:, o_:o_ + n_], writes=["stg"])
            P.op("scalar", lambda e, o_=o_, n_=n_: e.activation(out=strips[:, o_:o_ + n_], in_=stg[:, 0:n_], func=AF.Exp),
                 reads=["stg"], writes=["strips"])
            off += n
        P.barrier()

        def SB(h, lo, n):
            o = OFF_B(h) + lo
            return strips[:, o:o + n]

        def load_w(slot, col, dram2d, ncols, nm):
            P.dma("gpsimd", "ldw%d" % slot, wsl[:, slot, :, col:col + ncols],
                  dram2d.rearrange("(c p) n -> p c n", p=128), writes=["wsl%d" % slot])

        def proj_fm(slot, c0, M, evac, tbs=range(4)):
            for tb in tbs:
                b = nxt("m", 2, 5)
                for c in range(8):
                    P.op("tensor", lambda e, b=b, c=c, tb=tb: e.matmul(
                        psum[0:M, b, :], lhsT=wsl[:, slot, c, c0:c0 + M], rhs=hnT[:, c, tb * 512:(tb + 1) * 512],
                        start=(c == 0), stop=(c == 7)),
                        reads=["wsl%d" % slot, "hnT"], writes=[psn(b)], inc=(c == 7))
                evac(tb, b)

        def proj_tm(slot, c0, N, tok_ap, evac):
            for t in range(16):
                b = nxt("m", 2, 5)
                for c in range(8):
                    P.op("tensor", lambda e, b=b, c=c, t=t: e.matmul(
                        psum[:, b, 0:N], lhsT=tok_ap(t, c), rhs=wsl[:, slot, c, c0:c0 + N],
                        start=(c == 0), stop=(c == 7)),
                        reads=["wsl%d" % slot, "hnT"], writes=[psn(b)], inc=(c == 7))
                evac(t, b)

        alt = {"i": 0}

        def cp(out, in_, reads, writes, eng=None):
            if eng is None:
                eng = "scalar" if alt["i"] % 2 == 0 else "vector"
                alt["i"] += 1
            if eng == "scalar":
                P.op("scalar", lambda e: e.copy(out=out, in_=in_), reads=reads, writes=writes)
            else:
                P.op("vector", lambda e: e.tensor_copy(out=out, in_=in_), reads=reads, writes=writes)

        def s_tile(sb_, kt_ap, q_ap, kd, reads, c0=0, c1=512):
            P.op("tensor", lambda e: e.matmul(psum[:, sb_, c0:c1], lhsT=kt_ap, rhs=q_ap[:, c0:c1], start=True, stop=True),
                 reads=reads, writes=[psn(sb_)])

        def softmax_tile(sb_, mode, h, strip_ap=None, strip2_ap=None, c0=0, c1=512):
            pb = nxt("p", 3)
            pn = "P%d" % pb
            if mode == "const":
                P.op("scalar", lambda e: e.activation(out=Pbuf[:, pb, c0:c1], in_=psum[:, sb_, c0:c1], func=AF.Exp,
                                                      bias=tabs[:, 448 + h:449 + h], scale=0.125),
                     reads=[psn(sb_), "tabs"], writes=[pn])
                return pb
            eb = nxt("e", 2)
            en = "E%d" % eb
            if mode == "eb":
                P.op("scalar", lambda e: e.activation(out=Ebuf[:, eb, c0:c1], in_=psum[:, sb_, c0:c1], func=AF.Exp, scale=0.125),
                     reads=[psn(sb_)], writes=[en])
            else:
                P.op("scalar", lambda e: e.activation(out=Ebuf[:, eb, c0:c1], in_=psum[:, sb_, c0:c1], func=AF.Exp,
                                                      bias=tabs[:, 448 + h:449 + h], scale=0.125),
                     reads=[psn(sb_), "tabs"], writes=[en])
            if strip2_ap is not None:
                P.op("vector", lambda e: e.tensor_tensor(out=Ebuf[:, eb, c0:c1], in0=Ebuf[:, eb, c0:c1], in1=strip2_ap[:, c0:c1], op=ALU.mult),
                     reads=[en, "strips"], writes=[en])
            P.op("vector", lambda e: e.tensor_tensor(out=Pbuf[:, pb, c0:c1], in0=Ebuf[:, eb, c0:c1], in1=strip_ap[:, c0:c1], op=ALU.mult),
                 reads=[en, "strips"], writes=[pn])
            return pb

        deferred = []
        pend = []

        def flush_evac():
            while deferred:
                deferred.pop(0)()

        def flush_all():
            while pend:
                pend.pop(0)()
            flush_evac()

        def attend_block(h, q_ap, kd, tiles, qreads, evac):
            ob = nxt("o", 2, 3)
            n = len(tiles)
            assert n >= 2 and tiles[0][7] == 0 and tiles[0][8] == 512
            for i, (kt_ap, kreads, va_ap, vreads, mode, strip, strip2, c0, c1) in enumerate(tiles):
                sb_ = nxt("s", 3, 0)
                s_tile(sb_, kt_ap, q_ap, kd, qreads + kreads, c0, c1)
                if len(pend) >= 2:
                    pend.pop(0)()
                pb = softmax_tile(sb_, mode, h, strip, strip2, c0, c1)
                if i == 1:
                    flush_evac()
                pend.append(lambda va_ap=va_ap, vreads=vreads, pb=pb, i=i, c0=c0, c1=c1: P.op(
                    "tensor", lambda e: e.matmul(psum[:, ob, c0:c1], lhsT=va_ap, rhs=Pbuf[:, pb, c0:c1], start=(i == 0), stop=(i == n - 1)),
                    reads=vreads + ["P%d" % pb], writes=[psn(ob)], inc=True))
            deferred.append(lambda: evac(ob))

        eps_t = sb("eps", [128, 1], F32)
        P.op("vector", lambda e: e.memset(eps_t[:], 1e-6), writes=["eps"])

        def norm_tile(xb, t, gcol, post=True):
            xn = "xt%d" % xb
            P.op("scalar", lambda e: e.activation(out=rtmp[:, 0, :], in_=xt[:, xb, 0:512], func=AF.Square, accum_out=small[:, 0:1]),
                 reads=[xn], writes=["rtmp0", "st0"])
            P.op("scalar", lambda e: e.activation(out=rtmp[:, 0, :], in_=xt[:, xb, 512:1024], func=AF.Square, accum_out=small[:, 1:2]),
                 reads=[xn], writes=["rtmp0", "st1"])
            P.op("vector", lambda e: e.tensor_tensor(out=small[:, 2:3], in0=small[:, 0:1], in1=small[:, 1:2], op=ALU.add),
                 reads=["st0", "st1"], writes=["st2"])
            P.op("scalar", lambda e: e.activation(out=small[:, 3:4], in_=small[:, 2:3], func=AF.Sqrt, bias=eps_t[:, 0:1], scale=1.0 / D),
                 reads=["st2", "eps"], writes=["st3"])
            P.op("vector", lambda e: e.reciprocal(out=small[:, 4:5], in_=small[:, 3:4]), reads=["st3"], writes=["st4"])
            P.op("vector", lambda e: e.tensor_scalar(out=xt[:, xb, :], in0=xt[:, xb, :], scalar1=small[:, 4:5], scalar2=None, op0=ALU.mult),
                 reads=[xn, "st4"], writes=[xn])
            if gcol is None or not post:
                return
            norm_post(xb, t, gcol)

        def norm_post(xb, t, gcol):
            xn = "xt%d" % xb
            for half in range(2):
                b = nxt("f", 8, 0)
                for cc in range(4):
                    c = half * 4 + cc
                    P.op("tensor", lambda e, b=b, c=c, cc=cc: e.transpose(psum[:, b, cc * 128:(cc + 1) * 128], xt[:, xb, c * 128:(c + 1) * 128], ident),
                         reads=[xn, "tabs"], writes=[psn(b)], inc=(cc == 3))
                for cc in range(4):
                    c = half * 4 + cc
                    if half == 0:
                        P.op("scalar", lambda e, b=b, c=c, cc=cc: e.activation(out=hnT[:, c, t * 128:(t + 1) * 128], in_=psum[:, b, cc * 128:(cc + 1) * 128],
                                                                              func=AF.Copy, scale=gains_t[:, gcol + c:gcol + c + 1]),
                             reads=[psn(b), "gains"], writes=["hnT"])
                    else:
                        P.op("vector", lambda e, b=b, c=c, cc=cc: e.tensor_scalar(out=hnT[:, c, t * 128:(t + 1) * 128], in0=psum[:, b, cc * 128:(cc + 1) * 128],
                                                                                 scalar1=gains_t[:, gcol + c:gcol + c + 1], scalar2=None, op0=ALU.mult),
                             reads=[psn(b), "gains"], writes=["hnTd"])

        final_toks = []

        for sq in range(nseq):
            for l in range(depth):
                src_x = (lambda t: x_d[sq, t * 128:(t + 1) * 128, :]) if l == 0 else (lambda t: scr[t * 128:(t + 1) * 128, :])
                P.phase = "norm1"
                prev_post = None
                for t in range(16):
                    xb = nxt("x", 2)
                    P.dma("sync", "ldx%d" % xb, xt[:, xb, :], src_x(t), reads=["scr"], writes=["xt%d" % xb])
                    norm_tile(xb, t, (l * 2 + 0) * 8, post=False)
                    if prev_post is not None:
                        norm_post(*prev_post)
                    prev_post = (xb, t, (l * 2 + 0) * 8)
                norm_post(*prev_post)
                W = w_in[l]
                P.op("vector", lambda e: e.memset(VA[:, :, :, 64:128], 1.0), writes=["VA"])
                P.op("vector", lambda e: e.memset(VCMP[:, 64:128], 1.0), writes=["VCMP"])
                P.op("vector", lambda e: e.memset(gT[:], 0.0), writes=["gT"])

                P.phase = "A"
                for g, dil in (enumerate(A_DIL) if "A" in ST else []):
                    slot = nxt("w", 2)
                    L = S // dil
                    load_w(slot, 0, W[:, C_AQ + g * 128:C_AQ + g * 128 + 128], 128, "aq")
                    load_w(slot, 128, W[:, C_AK + g * 128:C_AK + g * 128 + 128], 128, "ak")
                    load_w(slot, 256, W[:, C_AV + g * 128:C_AV + g * 128 + 128], 128, "av")
                    for which in range(2):
                        def evac(tb, b, which=which):
                            dst = QP[:, which, :].rearrange("p (r l) -> p r l", r=dil)[:, :, tb * 512 // dil:(tb + 1) * 512 // dil]
                            src = psum[:, b, :].rearrange("p (l r) -> p r l", r=dil)
                            cp(dst, src, [psn(b)], ["QP%d" % which])
                        proj_fm(slot, which * 128, 128, evac)

                    def tok_ap(t, c):
                        pos0 = t * 128
                        r = pos0 // L
                        l0 = pos0 % L
                        st_ = r + dil * l0
                        return hnT[:, c, st_:st_ + dil * 127 + 1:dil]

                    def evac_v(t, b):
                        cp(VA[:, t, 0:2, 0:64], psum[:, b, 0:128].rearrange("p (h d) -> p h d", h=2), [psn(b)], ["VA"])
                    proj_tm(slot, 256, 128, tok_ap, evac_v)
                    tps = L // 128
                    Sv = lambda b: psum[:, b:b + 2, 0:256]
                    astr = strips[:, OFF_A(g):OFF_A(g) + 512]
                    def a_front(i):
                        has_prev = (i % tps) != 0
                        sb_ = (0, 5)[nxt("s", 2)]
                        spair = [psn(sb_), psn(sb_ + 1)]
                        qc = slice(i * 128, (i + 1) * 128)
                        for hh in range(2):
                            pr = slice(64 * hh, 64 * hh + 64)
                            if has_prev:
                                P.op("tensor", lambda e, hh=hh, pr=pr: e.matmul(psum[:, sb_ + hh, 0:128], lhsT=QP[pr, 1, (i - 1) * 128:i * 128],
                                                                              rhs=QP[pr, 0, qc], start=True, stop=True),
                                     reads=["QP0", "QP1"], writes=spair, inc=False)
                            P.op("tensor", lambda e, hh=hh, pr=pr: e.matmul(psum[:, sb_ + hh, 128:256], lhsT=QP[pr, 1, qc],
                                                                          rhs=QP[pr, 0, qc], start=True, stop=True),
                                 reads=["QP0", "QP1"], writes=spair, inc=(hh == 1))

                        def mid():
                            eb = nxt("e", 2)
                            pb = nxt("p", 3)
                            lo = 0 if has_prev else 128
                            Ev = Ebuf[:, eb, :].rearrange("p (h x) -> p h x", h=2)
                            Pv = Pbuf[:, pb, :].rearrange("p (h x) -> p h x", h=2)
                            Av = astr.rearrange("p (h x) -> p h x", h=2)
                            P.op("scalar", lambda e: e.activation(out=Ev[:, :, lo:256], in_=Sv(sb_)[:, :, lo:256], func=AF.Exp, scale=0.125),
                                 reads=spair, writes=["E%d" % eb])
                            P.op("vector", lambda e: e.tensor_tensor(out=Pv[:, :, lo:256], in0=Ev[:, :, lo:256], in1=Av[:, :, lo:256], op=ALU.mult),
                                 reads=["E%d" % eb, "strips"], writes=["P%d" % pb])
                            return pb

                        def back(pb):
                            ob = nxt("o", 2, 3)
                            for hh in range(2):
                                if has_prev:
                                    P.op("tensor", lambda e, hh=hh: e.matmul(psum[:, ob, hh * 128:(hh + 1) * 128], lhsT=VA[:, i - 1, hh, :],
                                                                           rhs=Pbuf[:, pb, hh * 256:hh * 256 + 128], start=True, stop=False),
                                         reads=["VA", "P%d" % pb], writes=[psn(ob)], inc=False)
                                P.op("tensor", lambda e, hh=hh: e.matmul(psum[:, ob, hh * 128:(hh + 1) * 128], lhsT=VA[:, i, hh, :],
                                                                       rhs=Pbuf[:, pb, hh * 256 + 128:hh * 256 + 256], start=(not has_prev), stop=True),
                                     reads=["VA", "P%d" % pb], writes=[psn(ob)], inc=(hh == 1))
                            pos0 = i * 128
                            r = pos0 // L
                            l0 = pos0 % L
                            st_ = r + dil * l0
                            cols = slice(st_, st_ + dil * 127 + 1, dil)
                            for hh in range(2):
                                P.op("scalar", lambda e, hh=hh: e.copy(out=QP[64 * hh:64 * hh + 64, 2 + g, cols], in_=psum[0:64, ob, hh * 128:(hh + 1) * 128]),
                                     reads=[psn(ob)], writes=["QP%d" % (2 + g)])
                                if g == 0:
                                    P.op("vector", lambda e, hh=hh: e.tensor_copy(out=dent[64 * hh:64 * hh + 64, cols], in_=psum[64:128, ob, hh * 128:(hh + 1) * 128]),
                                         reads=[psn(ob)], writes=["dent"])
                                else:
                                    P.op("vector", lambda e, hh=hh: e.tensor_tensor(out=dent[64 * hh:64 * hh + 64, cols], in0=dent[64 * hh:64 * hh + 64, cols],
                                                                                  in1=psum[64:128, ob, hh * 128:(hh + 1) * 128], op=ALU.add),
                                         reads=[psn(ob), "dent"], writes=["dent"])
                        return mid, back

                    pendA = None
                    for i in range(16):
                        mid, back = a_front(i)
                        if pendA is not None:
                            pendA[0](pendA[1])
                        pb = mid()
                        pendA = (back, pb)
                    pendA[0](pendA[1])
                if "A" in ST:
                    P.op("vector", lambda e: e.reciprocal(out=dent[:, :], in_=dent[:, :]), reads=["dent"], writes=["dent"])
                for g in (range(3) if "A" in ST else []):
                    P.op("vector", lambda e, g=g: e.tensor_tensor(out=omT[:, g, :], in0=QP[:, 2 + g, :], in1=dent[:, :], op=ALU.mult),
                         reads=["QP%d" % (2 + g), "dent"], writes=["omT"])

                P.phase = "C"
                for j in (range(2) if "C" in ST else []):
                    slot = nxt("w", 2)
                    load_w(slot, 0, W[:, C_CQ + j * 128:C_CQ + j * 128 + 128], 128, "cq")
                    load_w(slot, 128, W[:, C_CK + j * 128:C_CK + j * 128 + 128], 128, "ck")
                    load_w(slot, 256, W[:, C_CV + j * 128:C_CV + j * 128 + 128], 128, "cv")
                    for hh in range(2):
                        P.op("vector", lambda e, hh=hh: e.memset(QP[64:72, hh, 0:1024], 0.0), writes=["QP%d" % hh])
                        P.dma("gpsimd", "ldg", QP[64:72, 2 + hh, :], indm_d, writes=["QP%d" % (2 + hh)])
                    for which in range(2):
                        def evac(tb, b, which=which):
                            for hh in range(2):
                                cp(QP[0:64, which * 2 + hh, tb * 512:(tb + 1) * 512], psum[64 * hh:64 * hh + 64, b, :], [psn(b)], ["QP%d" % (which * 2 + hh)])
                        proj_fm(slot, which * 128, 128, evac)

                    def evac_v(t, b):
                        cp(VA[:, t, 0:2, 0:64], psum[:, b, 0:128].rearrange("p (h d) -> p h d", h=2), [psn(b)], ["VA"])
                    proj_tm(slot, 256, 128, lambda t, c: hnT[:, c, t * 128:(t + 1) * 128], evac_v)
                    for hh in range(2):
                        h = 12 + 2 * j + hh
                        qn, kn = "QP%d" % hh, "QP%d" % (2 + hh)
                        P.op("vector", lambda e, hh=hh: e.tensor_reduce(out=gx[0:64, 0:8], in_=QP[0:64, 2 + hh, :].rearrange("p (n k) -> p n k", k=256),
                                                                      axis=AX.X, op=ALU.add),
                             reads=[kn], writes=["gx"])
                        P.op("vector", lambda e, hh=hh: e.tensor_scalar(out=kmT[0:64, hh * 8:hh * 8 + 8], in0=gx[0:64, 0:8], scalar1=1.0 / 256, scalar2=None, op0=ALU.mult),
                             reads=["gx"], writes=["kmT"])
                        sc64, m64, nm64 = gu[:, 0:64], gu[:, 64:128], gx[:, 64:128]
                        for ti in range(8):
                            i = 8 + ti
                            P.op("tensor", lambda e, i=i, ti=ti, hh=hh: e.matmul(psum[:, 7, ti * 8:ti * 8 + 8], lhsT=QP[0:64, hh, i * 128:(i + 1) * 128],
                                                                               rhs=kmT[0:64, hh * 8:hh * 8 + 8], start=True, stop=True),
                                 reads=[qn, "kmT"], writes=["ps7"], inc=(ti == 7))
                        P.op("vector", lambda e: e.tensor_tensor(out=sc64, in0=psum[:, 7, 0:64], in1=tabs[:, 384:448], op=ALU.add),
                             reads=["ps7", "tabs"], writes=["gu"])
                        for ti in range(8):
                            P.op("vector", lambda e, ti=ti: e.max(out=m64[:, ti * 8:ti * 8 + 8], in_=sc64[:, ti * 8:ti * 8 + 8]), reads=["gu"], writes=["gu_m"])
                        for ti in range(8):
                            P.op("vector", lambda e, ti=ti: e.tensor_scalar(out=nm64[:, ti * 8:ti * 8 + 8], in0=sc64[:, ti * 8:ti * 8 + 8],
                                                                            scalar1=m64[:, ti * 8 + 3:ti * 8 + 4], scalar2=None, op0=ALU.is_ge),
                                 reads=["gu", "gu_m"], writes=["gx"])
                        P.op("vector", lambda e: e.tensor_scalar(out=nm64, in0=nm64, scalar1=-1.0, scalar2=BIG, op0=ALU.add, op1=ALU.mult),
                             reads=["gx"], writes=["gx"])
                        for ti in range(8):
                            bt = 5 + ti // 4
                            P.op("tensor", lambda e, ti=ti, bt=bt: e.transpose(psum[0:8, bt, (ti % 4) * 128:(ti % 4 + 1) * 128], nm64[:, ti * 8:ti * 8 + 8], ident),
                                 reads=["gx", "tabs"], writes=[psn(bt)], inc=(ti % 4 == 3))
                        for half_ in range(2):
                            P.op("scalar", lambda e, half_=half_, hh=hh: e.copy(out=QP[64:72, hh, 1024 + half_ * 512:1536 + half_ * 512], in_=psum[0:8, 5 + half_, :]),
                                 reads=[psn(5 + half_)], writes=[qn])
                        for qb in range(4):
                            tiles = []
                            for kt in range(4 * qb + 4):
                                dl = 512 * qb - 128 * kt
                                if dl >= 256:
                                    mode, strip = "const", None
                                else:
                                    mode, strip = "eb", SB(h, dl + 384, 512)
                                tiles.append((QP[0:72, 2 + hh, kt * 128:(kt + 1) * 128], [kn], VA[:, kt, hh, :], ["VA"], mode, strip, None,
                                              max(0, -dl), 512))

                            def evac_o(ob, qb=qb, hh=hh, j=j):
                                rb = nxt("e", 2)
                                P.op("scalar", lambda e: e.activation(out=rtmp[0:64, rb, :], in_=psum[64:128, ob, :], func=AF.Ln), reads=[psn(ob)], writes=["rtmp%d" % rb])
                                P.op("scalar", lambda e: e.activation(out=rtmp[0:64, rb, :], in_=rtmp[0:64, rb, :], func=AF.Exp, scale=-1.0), reads=["rtmp%d" % rb], writes=["rtmp%d" % rb])
                                P.op("vector", lambda e: e.tensor_tensor(out=omT[64 * hh:64 * hh + 64, 6 + j, qb * 512:(qb + 1) * 512], in0=psum[0:64, ob, :],
                                                                       in1=rtmp[0:64, rb, :], op=ALU.mult),
                                     reads=[psn(ob), "rtmp%d" % rb], writes=["omT"])
                            attend_block(h, QP[0:72, hh, qb * 512:(qb + 1) * 512], 72, tiles, [qn], evac_o)
                        flush_all()

                P.phase = "B"
                for g in (range(2) if "B" in ST else []):
                    P.phase = "B.compress"
                    slot = nxt("w", 2)
                    for kv, cbase in enumerate((C_BKC, C_BVC)):
                        for rep in range(2):
                            load_w(slot, kv * 128 + rep * 64, W[:, cbase + g * 64:cbase + g * 64 + 64], 64, "kc")
                    for kv in range(2):
                        def evac(tb, b, kv=kv):
                            cp(QP[0:64, kv, tb * 512:(tb + 1) * 512], psum[0:64, b, :], [psn(b)], ["QP%d" % kv])
                            if tb == 0:
                                cp(QP[64:128, kv, 0:511], psum[64:128, b, 1:512], [psn(b)], ["QP%d" % kv])
                            else:
                                cp(QP[64:128, kv, tb * 512 - 1:tb * 512 + 511], psum[64:128, b, :], [psn(b)], ["QP%d" % kv])
                        proj_fm(slot, kv * 128, 128, evac)
                    for kv, (w1d, w2d) in enumerate(((w1k, w2k), (w1v, w2v))):
                        slot = nxt("w", 2)
                        w1s = wsl[:, slot, :, :].rearrange("p c n -> p (c n)").rearrange("p (r m) -> p r m", m=256)
                        P.dma("gpsimd", "ldw%d" % slot, w1s, w1d[l].rearrange("(r p) m -> p r m", p=128), writes=["wsl%d" % slot])
                        P.dma("gpsimd", "ldg", w2s, w2d[l].rearrange("(m p) d -> p m d", p=128), reads=["gT"], writes=["w2s"])
                        P.dma("gpsimd", "ldg", pe2s, pe2[l, kv], writes=["pe2s"])
                        for mc in range(2):
                            b = nxt("m", 2, 5)
                            for r in range(16):
                                P.op("tensor", lambda e, b=b, r=r, mc=mc: e.matmul(psum[:, b, 0:1], lhsT=w1s[:, r, mc * 128:(mc + 1) * 128], rhs=pe2s[:, r:r + 1],
                                                                                 start=(r == 0), stop=(r == 15)),
                                     reads=["wsl%d" % slot, "pe2s"], writes=[psn(b)], inc=(r == 15))
                            P.op("vector", lambda e, b=b, mc=mc: e.tensor_copy(out=hb[:, mc:mc + 1], in_=psum[:, b, 0:1]), reads=[psn(b)], writes=["hb"])
                            b = nxt("m", 2, 5)
                            for r in range(16):
                                P.op("tensor", lambda e, b=b, r=r, mc=mc: e.matmul(psum[:, b, 0:127], lhsT=w1s[:, r, mc * 128:(mc + 1) * 128],
                                                                                 rhs=QP[:, kv, 2 * r:2 * r + 16 * 126 + 1:16], start=(r == 0), stop=(r == 15)),
                                     reads=["wsl%d" % slot, "QP%d" % kv], writes=[psn(b)], inc=(r == 15))
                            P.op("scalar", lambda e, b=b, mc=mc: e.activation(out=gx[:, 0:127], in_=psum[:, b, 0:127], func=AF.Identity, bias=hb[:, mc:mc + 1], scale=1.0),
                                 reads=[psn(b), "hb"], writes=["gx"])
                            P.op("vector", lambda e: e.tensor_tensor(out=gu[:, 0:127], in0=gx[:, 0:127], in1=gx[:, 0:127], op=ALU.mult), reads=["gx"], writes=["gu"])
                            P.op("vector", lambda e: e.tensor_scalar(out=gu[:, 0:127], in0=gu[:, 0:127], scalar1=0.044715, scalar2=1.0, op0=ALU.mult, op1=ALU.add),
                                 reads=["gu"], writes=["gu"])
                            P.op("vector", lambda e: e.tensor_tensor(out=gu[:, 0:127], in0=gu[:, 0:127], in1=gx[:, 0:127], op=ALU.mult), reads=["gu", "gx"], writes=["gu"])
                            P.op("scalar", lambda e: e.activation(out=gu[:, 0:127], in_=gu[:, 0:127], func=AF.Sigmoid, scale=2.0 * math.sqrt(2.0 / math.pi)),
                                 reads=["gu"], writes=["gu"])
                            P.op("vector", lambda e, mc=mc: e.tensor_tensor(out=gT[:, mc, 0:127], in0=gu[:, 0:127], in1=gx[:, 0:127], op=ALU.mult),
                                 reads=["gu", "gx"], writes=["gT"])
                        b = nxt("m", 2, 5)
                        if kv == 0:
                            for mc in range(2):
                                P.op("tensor", lambda e, b=b, mc=mc: e.matmul(psum[0:64, b, 0:128], lhsT=w2s[:, mc, :], rhs=gT[:, mc, :], start=(mc == 0), stop=(mc == 1)),
                                     reads=["w2s", "gT"], writes=[psn(b)], inc=(mc == 1))
                            P.op("vector", lambda e, b=b: e.tensor_copy(out=KCMP[0:64, :], in_=psum[0:64, b, 0:128]), reads=[psn(b)], writes=["KCMP"])
                        else:
                            for mc in range(2):
                                P.op("tensor", lambda e, b=b, mc=mc: e.matmul(psum[:, b, 0:64], lhsT=gT[:, mc, :], rhs=w2s[:, mc, :], start=(mc == 0), stop=(mc == 1)),
                                     reads=["w2s", "gT"], writes=[psn(b)], inc=(mc == 1))
                            P.op("vector", lambda e, b=b: e.tensor_copy(out=VCMP[:, 0:64], in_=psum[:, b, 0:64]), reads=[psn(b)], writes=["VCMP"])
                    P.phase = "B.proj"
                    slot = nxt("w", 2)
                    load_w(slot, 0, W[:, C_BQ + g * 192:C_BQ + g * 192 + 192], 192, "bq")
                    load_w(slot, 192, W[:, C_BKS + g * 64:C_BKS + g * 64 + 64], 64, "bks")
                    load_w(slot, 256, W[:, C_BKW + g * 64:C_BKW + g * 64 + 64], 64, "bkw")
                    load_w(slot, 320, W[:, C_BVS + g * 64:C_BVS + g * 64 + 64], 64, "bvs")
                    load_w(slot, 384, W[:, C_BVW + g * 64:C_BVW + g * 64 + 64], 64, "bvw")
                    load_w(slot, 448, W[:, C_BG + g * 9:C_BG + g * 9 + 9], 9, "bg")
                    for hq in range(3):
                        P.op("vector", lambda e, hq=hq: e.memset(QP[64:96, hq, 0:1024], 0.0), reads=["QP0", "QP1"], writes=["QP%d" % hq])
                    P.dma("gpsimd", "ldg", QP[64:96, 3, :], inds_d, writes=["QP3"])

                    def evac1(tb, b):
                        for hh in range(2):
                            cp(QP[0:64, hh, tb * 512:(tb + 1) * 512], psum[64 * hh:64 * hh + 64, b, :], [psn(b)], ["QP%d" % hh])
                    proj_fm(slot, 0, 128, evac1)

                    def evac2(tb, b):
                        for hh in range(2):
                            cp(QP[0:64, 2 + hh, tb * 512:(tb + 1) * 512], psum[64 * hh:64 * hh + 64, b, :], [psn(b)], ["QP%d" % (2 + hh)])
                    proj_fm(slot, 128, 128, evac2)

                    def evac3(tb, b):
                        cp(QP[0:64, 4, tb * 512:(tb + 1) * 512], psum[0:64, b, :], [psn(b)], ["QP4"])
                    proj_fm(slot, 256, 64, evac3)

                    def evac4(tb, b):
                        P.op("scalar", lambda e: e.activation(out=GT[0:9, tb * 512:(tb + 1) * 512], in_=psum[0:9, b, :], func=AF.Sigmoid),
                             reads=[psn(b)], writes=["GT"])
                    proj_fm(slot, 448, 9, evac4)

                    def evac_v(t, b):
                        cp(VA[:, t, 0:2, 0:64], psum[:, b, 0:128].rearrange("p (h d) -> p h d", h=2), [psn(b)], ["VA"])
                    proj_tm(slot, 320, 128, lambda t, c: hnT[:, c, t * 128:(t + 1) * 128], evac_v)

                    def gate_mult(ob, hq, br, qb, clampden, ph):
                        rb = nxt("e", 2)
                        rn = "rtmp%d" % rb
                        if clampden:
                            P.op("vector", lambda e: e.tensor_scalar(out=rtmp[ph:ph + 64, rb, :], in0=psum[64:128, ob, :], scalar1=1e-30, scalar2=None, op0=ALU.max),
                                 reads=[psn(ob)], writes=[rn])
                            P.op("scalar", lambda e: e.activation(out=rtmp[ph:ph + 64, rb, :], in_=rtmp[ph:ph + 64, rb, :], func=AF.Ln), reads=[rn], writes=[rn])
                        else:
                            P.op("scalar", lambda e: e.activation(out=rtmp[ph:ph + 64, rb, :], in_=psum[64:128, ob, :], func=AF.Ln), reads=[psn(ob)], writes=[rn])
                        P.op("scalar", lambda e: e.activation(out=rtmp[ph:ph + 64, rb, :], in_=rtmp[ph:ph + 64, rb, :], func=AF.Exp, scale=-1.0), reads=[rn], writes=[rn])
                        b = nxt("m", 2, 5)
                        jj = hq * 3 + br
                        P.op("tensor", lambda e: e.matmul(psum[0:64, b, :], lhsT=sel_t[0:9, jj * 64:(jj + 1) * 64], rhs=GT[0:9, qb * 512:(qb + 1) * 512], start=True, stop=True),
                             reads=["sel", "GT"], writes=[psn(b)])
                        P.op("vector", lambda e: e.tensor_tensor(out=rtmp[ph:ph + 64, rb, :], in0=rtmp[ph:ph + 64, rb, :], in1=psum[0:64, b, :], op=ALU.mult),
                             reads=[rn, psn(b)], writes=[rn])
                        return rb

                    P.phase = "B.cmp"
                    for qb in range(4):
                        impb = 7 if qb >= 2 else None
                        for hq in range(3):
                            hb_ = 3 * g + hq
                            h = 6 + hb_
                            ch, ph = 3 + hb_ // 2, 64 * (hb_ % 2)
                            pbs = {}

                            def evac_o(ob, qb=qb, hq=hq, ch=ch, ph=ph):
                                rb = gate_mult(ob, hq, 0, qb, True, ph)
                                P.op("vector", lambda e: e.tensor_tensor(out=omT[ph:ph + 64, ch, qb * 512:(qb + 1) * 512], in0=psum[0:64, ob, :],
                                                                       in1=rtmp[ph:ph + 64, rb, :], op=ALU.mult),
                                     reads=[psn(ob), "rtmp%d" % rb], writes=["omT"])
                            ob = nxt("o", 2, 3)
                            sb_ = nxt("s", 3, 0)
                            s_tile(sb_, KCMP[0:64, :], QP[0:64, hq, qb * 512:(qb + 1) * 512], 64, ["KCMP", "QP%d" % hq])
                            pb = softmax_tile(sb_, "eb", h, strips[:, OFF_CMP + qb * 512:OFF_CMP + (qb + 1) * 512])
                            P.op("tensor", lambda e, ob=ob, pb=pb: e.matmul(PS(ob), lhsT=VCMP[:, :], rhs=Pbuf[:, pb, :], start=True, stop=True),
                                 reads=["VCMP", "P%d" % pb], writes=[psn(ob)])
                            if impb is not None:
                                for ti in range(4):
                                    P.op("tensor", lambda e, pb=pb, ti=ti, hq=hq: e.matmul(psum[:, 7, ti * 128 + hq * 33:ti * 128 + hq * 33 + 33],
                                                                                         lhsT=Pbuf[:, pb, ti * 128:(ti + 1) * 128], rhs=ov_t[:, 0:33], start=True, stop=True),
                                         reads=["P%d" % pb, "ov"], writes=["ps7"], inc=(ti == 3))
                            evac_o(ob)
                        if impb is not None:
                            import os as _os
                            for ti in [int(c) for c in _os.environ.get("TI_ORDER", "0123")]:
                                i = 4 * qb + ti
                                base = ti * 128
                                P.op("vector", lambda e, base=base: e.reciprocal(out=rec3[:, 0:3], in_=psum[:, 7, base + 32:base + 99:33]), reads=["ps7"], writes=["rec3"])
                                P.op("vector", lambda e, base=base: e.tensor_scalar(out=scb[:, :], in0=psum[:, 7, base:base + 32], scalar1=rec3[:, 0:1], scalar2=None, op0=ALU.mult),
                                     reads=["ps7", "rec3"], writes=["scb"])
                                for hq in (1, 2):
                                    P.op("vector", lambda e, base=base, hq=hq: e.scalar_tensor_tensor(out=scb[:, :], in0=psum[:, 7, base + hq * 33:base + hq * 33 + 32],
                                                                                                     scalar=rec3[:, hq:hq + 1], in1=scb[:, :], op0=ALU.mult, op1=ALU.add),
                                         reads=["ps7", "rec3", "scb"], writes=["scb"])
                                P.op("vector", lambda e, i=i: e.tensor_tensor(out=scb[:, :], in0=scb[:, :], in1=tabs[:, 128 + (i - 8) * 32:128 + (i - 8) * 32 + 32], op=ALU.add),
                                     reads=["scb", "tabs"], writes=["scb"])
                                P.op("vector", lambda e: e.max(out=m8[:, 0:8], in_=scb[:, :]), reads=["scb"], writes=["m8"])
                                P.op("vector", lambda e: e.match_replace(out=sctmp[:, :], in_to_replace=m8[:, 0:8], in_values=scb[:, :], imm_value=-1e30),
                                     reads=["scb", "m8"], writes=["sctmp"])
                                P.op("vector", lambda e: e.max(out=m8[:, 8:16], in_=sctmp[:, :]), reads=["sctmp"], writes=["m8"])
                                P.op("vector", lambda e: e.tensor_scalar(out=nmk[:, :], in0=scb[:, :], scalar1=m8[:, 15:16], scalar2=None, op0=ALU.is_ge),
                                     reads=["scb", "m8"], writes=["nmk"])
                                P.op("vector", lambda e: e.tensor_scalar(out=nmk[:, :], in0=nmk[:, :], scalar1=-1.0, scalar2=BIG, op0=ALU.add, op1=ALU.mult),
                                     reads=["nmk"], writes=["nmk"])
                                b2 = nxt("m", 2, 5)
                                P.op("tensor", lambda e, b2=b2: e.transpose(psum[0:32, b2, 0:128], nmk[:, :], ident), reads=["nmk", "tabs"], writes=[psn(b2)])
                                for hq in range(3):
                                    cp(QP[64:96, hq, i * 128:(i + 1) * 128], psum[0:32, b2, 0:128], [psn(b2)], ["QP%d" % hq], eng="scalar")
                    P.phase = "B.slcwin"
                    for hq in range(3):
                        hb_ = 3 * g + hq
                        h = 6 + hb_
                        ch, ph = 3 + hb_ // 2, 64 * (hb_ % 2)
                        qn = "QP%d" % hq
                        for qb in range(4):
                            for br in (1, 2):
                                tiles = []
                                if br == 1:
                                    for kt in range(4 * qb + 4):
                                        dl = 512 * qb - 128 * kt
                                        if dl >= 256:
                                            mode, strip = "const", None
                                        else:
                                            mode, strip = "eb", SB(h, dl + 384, 512)
                                        tiles.append((QP[0:96, 3, kt * 128:(kt + 1) * 128], ["QP3"], VA[:, kt, 0, :], ["VA"], mode, strip, None,
                                                      max(0, -dl), 512))
                                    kd = 96
                                else:
                                    for kt in range(max(0, 4 * qb - 4), 4 * qb + 4):
                                        dl = 512 * qb - 128 * kt
                                        if dl >= 256:
                                            mode, strip, strip2 = "cb", strips[:, OFF_UM + dl:OFF_UM + dl + 512], None
                                        elif dl == 128:
                                            mode, strip, strip2 = "eb", strips[:, OFF_UM + dl:OFF_UM + dl + 512], SB(h, dl + 384, 512)
                                        else:
                                            mode, strip, strip2 = "eb", SB(h, dl + 384, 512), None
                                        c0_ = max(0, -dl)
                                        c1_ = 512 if dl <= 128 else 640 - dl
                                        tiles.append((QP[0:64, 4, kt * 128:(kt + 1) * 128], ["QP4"], VA[:, kt, 1, :], ["VA"], mode, strip, strip2, c0_, c1_))
                                    tiles.sort(key=lambda tl: 0 if (tl[7] == 0 and tl[8] == 512) else 1)
                                    kd = 64

                                def evac_o(ob, qb=qb, hq=hq, ch=ch, ph=ph, br=br):
                                    rb = gate_mult(ob, hq, br, qb, False, ph)
                                    rn = "rtmp%d" % rb
                                    P.op("vector", lambda e: e.tensor_tensor(out=rtmp[ph:ph + 64, rb, :], in0=psum[0:64, ob, :], in1=rtmp[ph:ph + 64, rb, :], op=ALU.mult),
                                         reads=[psn(ob), rn], writes=[rn])
                                    P.op("vector", lambda e: e.tensor_tensor(out=omT[ph:ph + 64, ch, qb * 512:(qb + 1) * 512], in0=omT[ph:ph + 64, ch, qb * 512:(qb + 1) * 512],
                                                                           in1=rtmp[ph:ph + 64, rb, :], op=ALU.add),
                                         reads=["omT", rn], writes=["omT"])
                                attend_block(h, QP[0:kd, hq, qb * 512:(qb + 1) * 512], kd, tiles, [qn], evac_o)
                    flush_all()

                if debug and sq == 0 and l == 0:
                    P.dma("sync", "dbg", dbg["omT"], omT, reads=["omT"])

                P.phase = "merge"
                Gm = dent.rearrange("p (b n) -> p b n", b=4)
                for fc in (range(8) if "merge" in ST else []):
                    slot = nxt("w", 2)
                    load_w(slot, 0, w_br[l][:, fc * 128:(fc + 1) * 128], 128, "wb")
                    for gi in range(3):
                        load_w(slot, 128 + gi * 128, W[:, C_MG + gi * 1024 + fc * 128:C_MG + gi * 1024 + fc * 128 + 128], 128, "mg")
                    for tb in range(4):
                        tsl = slice(tb * 512, (tb + 1) * 512)
                        bs = [5, 6, 7]
                        gb = [nxt("s", 3, 0) for _ in range(3)]
                        rng = ((0, 3), (3, 6), (6, 8))
                        for gi in range(3):
                            for c in range(8):
                                P.op("tensor", lambda e, gi=gi, c=c: e.matmul(PS(gb[gi]), lhsT=wsl[:, slot, c, 128 + gi * 128:256 + gi * 128], rhs=hnT[:, c, tsl],
                                                                            start=(c == 0), stop=(c == 7)),
                                     reads=["wsl%d" % slot, "hnT"], writes=[psn(gb[gi])], inc=(c == 7))
                            P.op("scalar", lambda e, gi=gi: e.activation(out=Gm[:, gi, :], in_=PS(gb[gi]), func=AF.Sigmoid), reads=[psn(gb[gi])], writes=["Gm%d" % gi])
                            k0, k1 = rng[gi]
                            for k in range(k0, k1):
                                P.op("tensor", lambda e, gi=gi, k=k, k0=k0, k1=k1: e.matmul(PS(bs[gi]), lhsT=wsl[:, slot, k, 0:128], rhs=omT[:, k, tsl],
                                                                                          start=(k == k0), stop=(k == k1 - 1)),
                                     reads=["wsl%d" % slot, "omT"], writes=[psn(bs[gi])], inc=(k == k1 - 1))
                        P.op("vector", lambda e: e.tensor_tensor(out=Gm[:, 0, :], in0=Gm[:, 0, :], in1=PS(bs[0]), op=ALU.mult), reads=["Gm0", psn(bs[0])], writes=["Gm0"])
                        P.op("vector", lambda e: e.tensor_tensor(out=Gm[:, 1, :], in0=Gm[:, 1, :], in1=PS(bs[1]), op=ALU.mult), reads=["Gm1", psn(bs[1])], writes=["Gm1"])
                        P.op("vector", lambda e: e.tensor_tensor(out=Gm[:, 2, :], in0=Gm[:, 2, :], in1=PS(bs[2]), op=ALU.mult), reads=["Gm2", psn(bs[2])], writes=["Gm2"])
                        P.op("vector", lambda e: e.tensor_tensor(out=Gm[:, 0, :], in0=Gm[:, 0, :], in1=Gm[:, 1, :], op=ALU.add), reads=["Gm0", "Gm1"], writes=["Gm0"])
                        P.op("vector", lambda e, fc=fc, tsl=tsl: e.tensor_tensor(out=mergedT[:, fc, tsl], in0=Gm[:, 0, :], in1=Gm[:, 2, :], op=ALU.add),
                             reads=["Gm0", "Gm2", "QP0", "QP1", "QP2", "QP3", "QP4", "VA"], writes=["mergedT"])

                P.phase = "outproj"
                wo = wsl_raw[:].rearrange("p (c n) -> p c n", c=8)
                P.dma("gpsimd", "ldw0", wo, w_out[l].rearrange("(c p) n -> p c n", p=128), reads=["wsl1"], writes=["wsl0", "wsl1"])
                prev_post = None
                for t in (range(16) if "outproj" in ST else []):
                    xb = nxt("x", 2)
                    xn = "xt%d" % xb
                    P.dma("sync", "ldx%d" % xb, xt[:, xb, :], src_x(t), reads=["scr"], writes=[xn])
                    for hf in range(2):
                        b = nxt("f", 8, 0)
                        for c in range(8):
                            P.op("tensor", lambda e, b=b, c=c, hf=hf, t=t: e.matmul(PS(b), lhsT=mergedT[:, c, t * 128:(t + 1) * 128], rhs=wo[:, c, hf * 512:(hf + 1) * 512],
                                                                                  start=(c == 0), stop=(c == 7)),
                                 reads=["mergedT", "wsl0", "wsl1"], writes=[psn(b)], inc=(c == 7))
                        P.op("vector", lambda e, b=b, hf=hf: e.tensor_tensor(out=xt[:, xb, hf * 512:(hf + 1) * 512], in0=xt[:, xb, hf * 512:(hf + 1) * 512], in1=PS(b), op=ALU.add),
                             reads=[psn(b), xn], writes=[xn])
                    P.dma("sync", "stx", scr[t * 128:(t + 1) * 128, :], xt[:, xb, :], reads=[xn], writes=["scr_t%d" % t])
                    if debug and sq == 0 and l == 0:
                        P.dma("sync", "dbg", dbg["x1"][t * 128:(t + 1) * 128, :], xt[:, xb, :], reads=[xn])
                    norm_tile(xb, t, (l * 2 + 1) * 8, post=False)
                    if prev_post is not None:
                        norm_post(*prev_post)
                    prev_post = (xb, t, (l * 2 + 1) * 8)
                if prev_post is not None:
                    norm_post(*prev_post)
                P.barrier()

                P.phase = "ffn"
                P.dma("sync", "ldc", cw_t[:], cwt[l], writes=["cw"])
                P.op("vector", lambda e: e.memset(halo[:], 0.0), writes=["halo%d" % q_ for q_ in range(44)])
                wsl4 = wsl_raw[:].rearrange("p (s c n) -> p s c n", s=4, c=8)
                for hf in (range(2) if "ffn" in ST else []):
                    P.phase = "ffn.up"
                    for j in range(22):
                        s4 = nxt("w4", 4)
                        wn = "wf%d" % s4
                        P.dma("gpsimd", "ldf%d" % s4, wsl4[:, s4, :, 0:128], w_up[l][:, j * 128:(j + 1) * 128].rearrange("(c p) n -> p c n", p=128),
                              writes=[wn, "wsl%d" % (s4 // 2)])
                        P.dma("gpsimd", "ldf%d" % s4, wsl4[:, s4, :, 128:256], w_up[l][:, DFF + j * 128:DFF + (j + 1) * 128].rearrange("(c p) n -> p c n", p=128),
                              writes=[wn])
                        par = j % 2
                        for au in range(2):
                            rbuf = rbufs[au][par]
                            obuf = obufs[au][par]
                            jj = au * 22 + j
                            rn, on = "r%d_%d" % (au, par), "o%d_%d" % (au, par)
                            P.op("scalar", lambda e, rbuf=rbuf, jj=jj: e.copy(out=rbuf[:, 0:2], in_=halo[:, jj, :]), reads=["halo%d" % jj], writes=[rn])
                            for tbl in range(2):
                                tb = hf * 2 + tbl
                                b = nxt("f", 8, 0)
                                for c in range(8):
                                    P.op("tensor", lambda e, b=b, c=c, tb=tb, au=au: e.matmul(PS(b), lhsT=wsl4[:, s4, c, au * 128:(au + 1) * 128], rhs=hnT[:, c, tb * 512:(tb + 1) * 512],
                                                                                            start=(c == 0), stop=(c == 7)),
                                         reads=[wn, "hnT"], writes=[psn(b)], inc=(c == 7))
                                P.op("scalar", lambda e, b=b, rbuf=rbuf, tbl=tbl: e.copy(out=rbuf[:, 2 + tbl * 512:2 + (tbl + 1) * 512], in_=PS(b)), reads=[psn(b)], writes=[rn])
                            cwb = jj * 4
                            P.op("scalar", lambda e, rbuf=rbuf, obuf=obuf, cwb=cwb: e.activation(out=obuf[:, :], in_=rbuf[:, 2:1026], func=AF.Identity,
                                                                                               bias=cw_t[:, cwb + 3:cwb + 4], scale=cw_t[:, cwb + 2:cwb + 3]),
                                 reads=[rn, "cw"], writes=[on])
                            P.op("vector", lambda e, rbuf=rbuf, obuf=obuf, cwb=cwb: e.scalar_tensor_tensor(out=obuf[:, :], in0=rbuf[:, 1:1025], scalar=cw_t[:, cwb + 1:cwb + 2],
                                                                                                         in1=obuf[:, :], op0=ALU.mult, op1=ALU.add),
                                 reads=[rn, on, "cw"], writes=[on])
                            P.op("vector", lambda e, rbuf=rbuf, obuf=obuf, cwb=cwb: e.scalar_tensor_tensor(out=obuf[:, :], in0=rbuf[:, 0:1024], scalar=cw_t[:, cwb:cwb + 1],
                                                                                                         in1=obuf[:, :], op0=ALU.mult, op1=ALU.add),
                                 reads=[rn, on, "cw"], writes=[on])
                            if hf == 0:
                                P.op("scalar", lambda e, rbuf=rbuf, jj=jj: e.copy(out=halo[:, jj, :], in_=rbuf[:, 1024:1026]), reads=[rn], writes=["halo%d" % jj])
                        oa_, ou_ = obufs[0][par], obufs[1][par]
                        P.op("scalar", lambda e: e.activation(out=sg[:, :], in_=oa_[:, :], func=AF.Silu), reads=["o0_%d" % par], writes=["sg"])
                        P.op("vector", lambda e, j=j: e.tensor_tensor(out=gTf[:, j, :], in0=sg[:, :], in1=ou_[:, :], op=ALU.mult), reads=["sg", "o1_%d" % par], writes=["gTf%d" % j])
                    P.phase = "ffn.dn"
                    xt4 = xt[:].rearrange("p b (h n) -> p (b h) n", h=2)
                    for chh in range(2):
                        xbufs = {}
                        for tt in range(4):
                            t = hf * 8 + tt
                            xn = "xq%d" % tt
                            P.dma("sync", "ldx4_%d" % tt, xt4[:, tt, :], scr[t * 128:(t + 1) * 128, chh * 512:(chh + 1) * 512],
                                  reads=["scr_t%d" % t], writes=[xn, "xt0", "xt1"])
                            xbufs[tt] = (tt, xn)
                        for j in range(22):
                            sd = nxt("wd", 6)
                            dn = "wd%d" % sd
                            P.dma("gpsimd", "ldd%d" % sd, wdsl[:, sd, :], w_dn[l][j * 128:(j + 1) * 128, chh * 512:(chh + 1) * 512], writes=[dn])
                            for tt in range(8):
                                P.op("tensor", lambda e, j=j, tt=tt: e.matmul(PS(tt), lhsT=gTf[:, j, tt * 128:(tt + 1) * 128], rhs=wdsl[:, sd, :], start=(j == 0), stop=(j == 21)),
                                     reads=["gTf%d" % j, dn], writes=[psn(tt)], inc=(tt == 7))
                        def dn_add(tt):
                            xb4, xn = xbufs[tt]
                            P.op("vector", lambda e: e.tensor_tensor(out=xt4[:, xb4, :], in0=xt4[:, xb4, :], in1=PS(tt), op=ALU.add), reads=[psn(tt), xn], writes=[xn])

                        def dn_store(tt):
                            t = hf * 8 + tt
                            xb4, xn = xbufs[tt]
                            P.dma("sync", "stx", scr[t * 128:(t + 1) * 128, chh * 512:(chh + 1) * 512], xt4[:, xb4, :], reads=[xn], writes=["scr_t%d" % t, "scr"])

                        for tt in range(4):
                            dn_add(tt)
                        for tt in range(4):
                            dn_store(tt)
                        for tt in range(4, 8):
                            t2 = hf * 8 + tt
                            xb4, xn = xbufs[tt - 4]
                            P.dma("sync", "ldx4_%d" % xb4, xt4[:, xb4, :], scr[t2 * 128:(t2 + 1) * 128, chh * 512:(chh + 1) * 512], reads=["scr_t%d" % t2], writes=[xn])
                            xbufs[tt] = (xb4, xn)
                        for tt in range(4, 8):
                            dn_add(tt)
                            dn_store(tt)
                P.barrier()
            P.phase = "final"
            for t in (range(16) if "final" in ST else []):
                xb = nxt("x", 2)
                xn = "xt%d" % xb
                P.dma("sync", "ldx%d" % xb, xt[:, xb, :], scr[t * 128:(t + 1) * 128, :], reads=["scr", "scr_t%d" % t], writes=[xn])
                norm_tile(xb, t, None)
                P.op("vector", lambda e, xb=xb: e.tensor_tensor(out=xt[:, xb, :], in0=xt[:, xb, :], in1=gfin_t[:, :], op=ALU.mult), reads=[xn, "gfin"], writes=[xn])
                final_toks.append(P.dma("sync", "sto", out_d[sq, t * 128:(t + 1) * 128, :], xt[:, xb, :], reads=[xn], writes=["scr"]))
            P.barrier()
        P.finish(final_toks)
    return nc


_CACHE = {}


def _prep_inputs(inputs):
    f = lambda a: np.ascontiguousarray(np.asarray(a, np.float32))
    hc = _host_consts(inputs["rel_bias"])
    gains = np.zeros((128, 32), np.float32)
    for l in range(DEPTH):
        gains[:, (l * 2 + 0) * 8:(l * 2 + 0) * 8 + 8] = f(inputs["norm_mix"])[l].reshape(8, 128).T
        gains[:, (l * 2 + 1) * 8:(l * 2 + 1) * 8 + 8] = f(inputs["norm_ffn"])[l].reshape(8, 128).T
    cwt = np.zeros((DEPTH, 128, 176), np.float32)
    for l in range(DEPTH):
        cw = f(inputs["conv_w"])[l]
        cb = f(inputs["conv_b"])[l]
        for k in range(3):
            cwt[l, :, k::4] = cw[k].reshape(44, 128).T
        cwt[l, :, 3::4] = cb.reshape(44, 128).T
    pe2 = np.zeros((DEPTH, 2, 128, 16), np.float32)
    for l in range(DEPTH):
        for kv, nm in enumerate(("cmp_pe_k", "cmp_pe_v")):
            pe = f(inputs[nm])[l]
            pe2[l, kv] = pe.reshape(16, 2, 64).transpose(1, 2, 0).reshape(128, 16)
    shared = dict(
        w_in=f(inputs["w_in"]), cmp_w1_k=f(inputs["cmp_w1_k"]), cmp_w2_k=f(inputs["cmp_w2_k"]),
        cmp_w1_v=f(inputs["cmp_w1_v"]), cmp_w2_v=f(inputs["cmp_w2_v"]), pe2=pe2,
        w_branch=f(inputs["w_branch"]), w_out=f(inputs["w_out"]), w_up=f(inputs["w_up"]), w_down=f(inputs["w_down"]),
        gains_in=gains, cwt=cwt, norm_final=f(inputs["norm_final"]),
        strips_in=hc["strips"], tabs_in=hc["tabs"], ov_in=hc["ov"], inds=hc["inds"], indm=hc["indm"], sel_in=hc["sel"],
    )
    x = f(inputs["x"])
    return [dict(shared, x=np.ascontiguousarray(x[c * SEQ_PER_CORE:(c + 1) * SEQ_PER_CORE])) for c in range(NCORE)]


def kernel(**inputs):
    if "nc" not in _CACHE:
        _CACHE["nc"] = build_program()
    nc = _CACHE["nc"]
    in_maps = _prep_inputs(inputs)
    res = run_bass_kernel_spmd(nc, in_maps, core_ids=list(range(NCORE)))
    out = np.concatenate([np.asarray(r["out"], np.float32) for r in res.results], axis=0)
    return out
```

```python
import math
from contextlib import ExitStack
import numpy as np
import concourse.bass as bass
import concourse.mybir as mybir
from concourse.bass_utils import run_bass_kernel_spmd

F32 = mybir.dt.float32
BF16 = mybir.dt.bfloat16
AF = mybir.ActivationFunctionType
ALU = mybir.AluOpType
AX = mybir.AxisListType

S = 2048
D = 1024
NCORE = 8
SEQ_PER_CORE = 2
DEPTH = 2
INC = 6162
DFF = 2816
BIG = 240000.0
NEGB = -30000.0
C_AQ, C_AK, C_AV = 0, 384, 768
C_BQ, C_BKC, C_BVC, C_BKS, C_BVS, C_BKW, C_BVW, C_BG = 1152, 1536, 1664, 1792, 1920, 2048, 2176, 2304
C_CQ, C_CK, C_CV, C_MG = 2322, 2578, 2834, 3090
A_DIL = (1, 4, 16)
NSTRIP = 10 * 1024 + 3 * 512 + 1024 + 2048
OFF_B = lambda h: (h - 6) * 1024
OFF_A = lambda g: 10240 + g * 512
OFF_UM = 10240 + 1536
OFF_CMP = OFF_UM + 1024

ENGS = ("tensor", "vector", "scalar", "gpsimd", "sync")


class Unit:
    __slots__ = ("w", "r")

    def __init__(self):
        self.w = None
        self.r = {}


class _Rec:
    def __init__(self):
        self.call = None

    def __getattr__(self, name):
        def f(*a, **kw):
            self.call = (name, a, kw)
            return self
        return f


class Prog:
    def __init__(self, nc, stack):
        self.nc = nc
        self.stack = stack
        self.q = {e: [] for e in ENGS}
        self.sems = {}
        self.cnt = {}
        self.seen = {e: {} for e in ENGS}
        for e in ENGS:
            self._sem("E_" + e)
        self.units = {}
        self.waitmax = {}
        self.limit = None
        self.nrec = 0
        self.log = []
        self.phase = "setup"
        self.pe_phase = []

    def _sem(self, key):
        if key not in self.sems:
            self.sems[key] = self.stack.enter_context(self.nc.semaphore(key))
            self.cnt[key] = 0
        return self.sems[key]

    def unit(self, key):
        u = self.units.get(key)
        if u is None:
            u = Unit()
            self.units[key] = u
        return u

    def _deps(self, eng, reads, writes, same_war=False, own_sem=None):
        need = {}

        def add(tok, kind):
            if tok is None:
                return
            k, v = tok
            if kind == "waw" and own_sem is not None and k == own_sem:
                return
            if k == "E_" + eng:
                if eng == "tensor":
                    return
                if kind == "war" and not same_war:
                    return
            if v > need.get(k, 0):
                need[k] = v

        for u in reads:
            add(u.w, "raw")
        for u in writes:
            add(u.w, "waw")
            for k, v in u.r.items():
                add((k, v), "war")
        out = []
        seen = self.seen[eng]
        for k in list(need):
            if not k.startswith("E_"):
                need[k] = self.cnt[k]
        for k, v in need.items():
            if seen.get(k, 0) < v:
                seen[k] = v
                out.append((k, v))
                if not k.startswith("E_") and v > self.waitmax.get(k, 0):
                    self.waitmax[k] = v
        return out

    def _mark(self, tok, reads, writes):
        for u in reads:
            if u.r.get(tok[0], 0) < tok[1]:
                u.r[tok[0]] = tok[1]
        for u in writes:
            u.w = tok
            u.r = {}

    def op(self, eng, fn, reads=(), writes=(), inc=True):
        self.nrec += 1
        if self.limit is not None and self.nrec > self.limit:
            return None
        if "hnT" in reads:
            reads = list(reads) + ["hnTd"]
        reads = [self.unit(u) for u in reads]
        writes = [self.unit(u) for u in writes]
        waits = self._deps(eng, reads, writes, same_war=(eng != "tensor"))
        k = "E_" + eng
        if inc:
            self.cnt[k] += 1
            tok = (k, self.cnt[k])
        else:
            tok = (k, self.cnt[k] + 1)
        rec = _Rec()
        fn(rec)
        name, a, kw = rec.call
        self.log.append((self.nrec, eng, name, [str(getattr(x, "shape", x)) for x in list(a) + list(kw.values())][:4]))
        if eng == "tensor":
            self.pe_phase.append(self.phase)
        self.q[eng].append((waits, (lambda e: getattr(e, name)(*a, **kw)), (k, 1) if inc else None))
        self._mark(tok, reads, writes)
        return tok

    def dma(self, eng, semkey, out, in_, reads=(), writes=()):
        self.nrec += 1
        if self.limit is not None and self.nrec > self.limit:
            return None
        self.log.append((self.nrec, eng, "dma", semkey))
        reads = [self.unit(u) for u in reads]
        writes = [self.unit(u) for u in writes]
        self._sem(semkey)
        waits = self._deps(eng, reads, writes, same_war=True, own_sem=semkey)
        wm = self.waitmax.get(semkey, 0)
        if wm > self.seen[eng].get(semkey, 0):
            self.seen[eng][semkey] = wm
            waits.append((semkey, wm))
        self.cnt[semkey] += 16
        tok = (semkey, self.cnt[semkey])
        self.q[eng].append((waits, lambda e: e.dma_start(out=out, in_=in_), (semkey, 16)))
        self._mark(tok, reads, writes)
        return tok

    def barrier(self):
        cur = [(k, v) for k, v in self.cnt.items() if v > 0]
        for e in ENGS:
            waits = []
            for k, v in cur:
                if k == "E_" + e:
                    continue
                if self.seen[e].get(k, 0) < v:
                    self.seen[e][k] = v
                    waits.append((k, v))
                    if not k.startswith("E_") and v > self.waitmax.get(k, 0):
                        self.waitmax[k] = v
            if waits:
                self.q[e].append((waits, None, None))

    def finish(self, final_tokens):
        prog = self
        fin = [t for t in final_tokens if t is not None]
        for k, v in self.cnt.items():
            if v > 0 and not k.startswith("E_tensor"):
                fin.append((k, v))

        def emit(engname):
            def body(e):
                for waits, fn, inc in prog.q[engname]:
                    for (k, v) in waits:
                        e.wait_ge(prog.sems[k], v)
                    if fn is None:
                        continue
                    ins = fn(e)
                    if inc is not None:
                        ins.then_inc(prog.sems[inc[0]], inc[1])
                if engname == "sync":
                    for (k, v) in fin:
                        e.wait_ge(prog.sems[k], v)
            return body

        with self.nc.Block() as block:
            block.tensor(emit("tensor"))
            block.vector(emit("vector"))
            block.scalar(emit("scalar"))
            block.gpsimd(emit("gpsimd"))
            block.sync(emit("sync"))


def _bucket(d):
    n = np.maximum(d, 0)
    nf = np.maximum(n, 1).astype(np.float32)
    large = 16 + (np.log(nf / np.float32(16)) / np.float32(math.log(8.0)) * np.float32(16)).astype(np.int32)
    return np.where(n < 16, n, np.minimum(large, 31)).astype(np.int64)


def _host_consts(rel_bias):
    rb = np.asarray(rel_bias, np.float32)
    ki = np.arange(128)[:, None]
    strips = np.zeros((128, NSTRIP), np.float32)
    u = np.arange(1024)[None, :]
    d = u - 384 - ki
    bk = _bucket(d)
    for h in range(6, 16):
        strips[:, OFF_B(h):OFF_B(h) + 1024] = np.where(d >= 0, rb[bk, h], np.float32(NEGB))
    qi = np.arange(128)[None, :]
    for g, dil in enumerate(A_DIL):
        for hh in range(2):
            h = 2 * g + hh
            ds = qi - ki + 128
            prev = np.where(ds <= 128, rb[_bucket(ds * dil), h], np.float32(NEGB))
            ds2 = qi - ki
            diag = np.where(ds2 >= 0, rb[_bucket(ds2 * dil), h], np.float32(NEGB))
            o = OFF_A(g) + hh * 256
            strips[:, o:o + 128] = prev
            strips[:, o + 128:o + 256] = diag
    w = np.arange(1024)[None, :]
    strips[:, OFF_UM:OFF_UM + 1024] = np.where(w - ki <= 511, np.float32(0), np.float32(NEGB))
    q = np.arange(2048)[None, :]
    ok = (16 * ki + 31 <= q) & (ki < 127)
    strips[:, OFF_CMP:OFF_CMP + 2048] = np.where(ok, np.float32(0), np.float32(NEGB))
    tabs = np.zeros((128, 128 + 256 + 64 + 16), np.float32)
    tabs[:, 0:128] = np.eye(128, dtype=np.float32)
    p = np.arange(128)[:, None]
    for i in range(8, 16):
        t = 128 * i + p
        jt = t // 64
        j = np.arange(32)[None, :]
        forced = (j == 0) | (j == jt) | (j == jt - 1)
        fb = np.where(j <= jt, np.where(forced, np.float32(1e4), np.float32(0)), np.float32(-1e30))
        tabs[:, 128 + (i - 8) * 32:128 + (i - 8) * 32 + 32] = fb
        nb = t // 256
        n = np.arange(8)[None, :]
        pm = np.where(n < nb, np.float32(0), np.where(n == nb, np.float32(1e4), np.float32(-1e30)))
        tabs[:, 384 + (i - 8) * 8:384 + (i - 8) * 8 + 8] = pm
    tabs[:, 448:464] = rb[31][None, :]
    ov = np.zeros((128, 64), np.float32)
    c = np.arange(128)[:, None]
    j = np.arange(32)[None, :]
    ov[:, 0:32] = ((16 * c < 64 * j + 64) & (16 * c + 32 > 64 * j) & (c < 127)).astype(np.float32)
    ov[:127, 32] = 1.0
    k = np.arange(2048)[None, :]
    inds = (k // 64 == np.arange(32)[:, None]).astype(np.float32)
    indm = (k // 256 == np.arange(8)[:, None]).astype(np.float32)
    sel = np.zeros((18, 18 * 64), np.float32)
    for jj in range(18):
        sel[jj, jj * 64:(jj + 1) * 64] = 1.0
    return dict(strips=strips, tabs=tabs, ov=ov, inds=inds, indm=indm, sel=sel)


def build_program(debug=False, stages=None, nseq=SEQ_PER_CORE, depth=DEPTH, limit=None):
    ST = stages if stages is not None else {"A", "C", "B", "merge", "outproj", "ffn", "final"}
    nc = bass.Bass("TRN2", target_bir_lowering=False)
    dt_in = lambda name, shape: nc.dram_tensor(name, shape, F32, kind="ExternalInput").ap()
    x_d = dt_in("x", [SEQ_PER_CORE, S, D])
    w_in = dt_in("w_in", [DEPTH, D, INC])
    w1k = dt_in("cmp_w1_k", [DEPTH, 2048, 256])
    w2k = dt_in("cmp_w2_k", [DEPTH, 256, 64])
    w1v = dt_in("cmp_w1_v", [DEPTH, 2048, 256])
    w2v = dt_in("cmp_w2_v", [DEPTH, 256, 64])
    pe2 = dt_in("pe2", [DEPTH, 2, 128, 16])
    w_br = dt_in("w_branch", [DEPTH, D, D])
    w_out = dt_in("w_out", [DEPTH, D, D])
    w_up = dt_in("w_up", [DEPTH, D, 2 * DFF])
    w_dn = dt_in("w_down", [DEPTH, DFF, D])
    gains = dt_in("gains_in", [128, 32])
    cwt = dt_in("cwt", [DEPTH, 128, 176])
    gfin = dt_in("norm_final", [D])
    strips_d = dt_in("strips_in", [128, NSTRIP])
    tabs_d = dt_in("tabs_in", [128, 464])
    ov_d = dt_in("ov_in", [128, 64])
    inds_d = dt_in("inds", [32, 2048])
    indm_d = dt_in("indm", [8, 2048])
    sel_d = dt_in("sel_in", [18, 18 * 64])
    out_d = nc.dram_tensor("out", [SEQ_PER_CORE, S, D], F32, kind="ExternalOutput").ap()
    scr = nc.dram_tensor("xscr", [S, D], F32, kind="Internal").ap()
    dbg = {}
    if debug:
        dbg["omT"] = nc.dram_tensor("dbg_omT", [128, 8, S], BF16, kind="ExternalOutput").ap()
        dbg["x1"] = nc.dram_tensor("dbg_x1", [S, D], F32, kind="ExternalOutput").ap()

    with ExitStack() as st:
        P = Prog(nc, st)
        P.limit = limit
        build_program.P = P
        sb = lambda name, shape, dt: st.enter_context(nc.sbuf_tensor(name, shape, dt))
        hnT = sb("hnT", [128, 8, S], BF16)
        wsl_raw = sb("wsl", [128, 8192], BF16)
        wsl = wsl_raw[:].rearrange("p (s c n) -> p s c n", s=2, c=8)
        strips = sb("strips", [128, NSTRIP], BF16)
        tabs = sb("tabs", [128, 464], F32)
        gains_t = sb("gains", [128, 32], F32)
        cw_t = sb("cw", [128, 176], F32)
        gfin_t = sb("gfin", [128, D], F32)
        ov_t = sb("ov", [128, 64], BF16)
        sel_t = sb("sel", [18, 18 * 64], F32)
        xt = sb("xt", [128, 2, D], F32)
        small = sb("small", [128, 64], F32)
        halo = sb("halo", [128, 44, 2], F32)
        ones_bf = sb("ones_bf", [128, 64], BF16)
        ARENA = 44 * 1024
        arena = sb("arena", [128, ARENA], BF16)
        psum = st.enter_context(nc.psum_tensor("ps", [128, 8, 512], F32))

        ident = tabs[:, 0:128]
        omT = arena[:, 0:16384].rearrange("p (c t) -> p c t", c=8)
        QP = arena[:, 16384:26624].rearrange("p (i t) -> p i t", i=5)
        VA = arena[:, 26624:32768].rearrange("p (t h e) -> p t h e", t=16, h=3)
        mergedT = arena[:, 16384:32768].rearrange("p (c t) -> p c t", c=8)
        dent = arena[:, 32768:36864].bitcast(F32)
        Ebuf = arena[:, 36864:38912].bitcast(F32).rearrange("p (b n) -> p b n", b=2)
        Pbuf = arena[:, 38912:40448].rearrange("p (b n) -> p b n", b=3)
        rtmp = arena[:, 40448:42496].bitcast(F32).rearrange("p (b n) -> p b n", b=2)
        misc = arena[:, 42496:44032].bitcast(F32)
        hb = misc[:, 0:4]
        gx = misc[:, 8:136]
        gu = misc[:, 136:264]
        scb = misc[:, 264:296]
        sctmp = misc[:, 296:328]
        m8 = misc[:, 328:344]
        nmk = misc[:, 344:376]
        rec3 = misc[:, 376:380]
        KCMP = misc[:, 384:448].bitcast(BF16)
        VCMP = misc[:, 448:512].bitcast(BF16)
        gT = misc[:, 512:640].bitcast(BF16).rearrange("p (m c) -> p m c", m=2)
        w2s = misc[:, 640:704].bitcast(BF16).rearrange("p (m d) -> p m d", m=2)
        pe2s = misc[:, 704:712].bitcast(BF16)
        kmT = misc[:, 712:720].bitcast(BF16)
        GT = dent
        gTf = arena[:, 0:22528].rearrange("p (j t) -> p j t", j=22)
        wdsl = arena[:, 22528:25600].rearrange("p (s n) -> p s n", s=6)
        _o = 25600
        rbufs = [[None, None], [None, None]]
        obufs = [[None, None], [None, None]]
        for au_ in range(2):
            for par_ in range(2):
                rbufs[au_][par_] = arena[:, _o:_o + 2056].bitcast(F32)
                _o += 2056
        for au_ in range(2):
            for par_ in range(2):
                obufs[au_][par_] = arena[:, _o:_o + 2048].bitcast(F32)
                _o += 2048
        sg = arena[:, _o:_o + 2048].bitcast(F32)
        assert _o + 2048 <= ARENA

        PS = lambda b: psum[:, b, :]
        psn = lambda b: "ps%d" % b
        rr = {"s": 0, "o": 0, "m": 0, "e": 0, "p": 0, "w": 0, "x": 0, "f": 0, "w4": 0, "wd": 0}

        def nxt(kind, n, base=0):
            v = base + rr[kind] % n
            rr[kind] += 1
            return v

        P.dma("sync", "ldc", tabs[:], tabs_d, writes=["tabs"])
        P.dma("sync", "ldc", gains_t[:], gains, writes=["gains"])
        P.dma("sync", "ldc", sel_t[:], sel_d, writes=["sel"])
        P.dma("sync", "ldc", gfin_t[:], gfin.partition_broadcast(128), writes=["gfin"])
        P.dma("gpsimd", "ldg", ov_t[:], ov_d, writes=["ov"])
        P.op("vector", lambda e: e.memset(ones_bf[:], 1.0), writes=["ones_bf"])
        stg = arena[:, 0:8192].bitcast(F32)
        off = 0
        while off < NSTRIP:
            n = min(4096, NSTRIP - off)
            o_, n_ = off, n
            P.dma("sync", "ldc", stg[:, 0:n_], strips_d[:, o_:o_ + n_], writes=["stg"])
            P.op("scalar", lambda e, o_=o_, n_=n_: e.activation(out=strips[:, o_:o_ + n_], in_=stg[:, 0:n_], func=AF.Exp),
                 reads=["stg"], writes=["strips"])
            off += n
        P.barrier()

        def SB(h, lo, n):
            o = OFF_B(h) + lo
            return strips[:, o:o + n]

        def load_w(slot, col, dram2d, ncols, nm):
            P.dma("gpsimd", "ldw%d" % slot, wsl[:, slot, :, col:col + ncols],
                  dram2d.rearrange("(c p) n -> p c n", p=128), writes=["wsl%d" % slot])

        def proj_fm(slot, c0, M, evac, tbs=range(4)):
            for tb in tbs:
                b = nxt("m", 2, 5)
                for c in range(8):
                    P.op("tensor", lambda e, b=b, c=c, tb=tb: e.matmul(
                        psum[0:M, b, :], lhsT=wsl[:, slot, c, c0:c0 + M], rhs=hnT[:, c, tb * 512:(tb + 1) * 512],
                        start=(c == 0), stop=(c == 7)),
                        reads=["wsl%d" % slot, "hnT"], writes=[psn(b)], inc=(c == 7))
                evac(tb, b)

        def proj_tm(slot, c0, N, tok_ap, evac):
            for t in range(16):
                b = nxt("m", 2, 5)
                for c in range(8):
                    P.op("tensor", lambda e, b=b, c=c, t=t: e.matmul(
                        psum[:, b, 0:N], lhsT=tok_ap(t, c), rhs=wsl[:, slot, c, c0:c0 + N],
                        start=(c == 0), stop=(c == 7)),
                        reads=["wsl%d" % slot, "hnT"], writes=[psn(b)], inc=(c == 7))
                evac(t, b)

        alt = {"i": 0}

        def cp(out, in_, reads, writes, eng=None):
            if eng is None:
                eng = "scalar" if alt["i"] % 2 == 0 else "vector"
                alt["i"] += 1
            if eng == "scalar":
                P.op("scalar", lambda e: e.copy(out=out, in_=in_), reads=reads, writes=writes)
            else:
                P.op("vector", lambda e: e.tensor_copy(out=out, in_=in_), reads=reads, writes=writes)

        def s_tile(sb_, kt_ap, q_ap, kd, reads, c0=0, c1=512):
            P.op("tensor", lambda e: e.matmul(psum[:, sb_, c0:c1], lhsT=kt_ap, rhs=q_ap[:, c0:c1], start=True, stop=True),
                 reads=reads, writes=[psn(sb_)])

        def softmax_tile(sb_, mode, h, strip_ap=None, strip2_ap=None, c0=0, c1=512):
            pb = nxt("p", 3)
            pn = "P%d" % pb
            if mode == "const":
                P.op("scalar", lambda e: e.activation(out=Pbuf[:, pb, c0:c1], in_=psum[:, sb_, c0:c1], func=AF.Exp,
                                                      bias=tabs[:, 448 + h:449 + h], scale=0.125),
                     reads=[psn(sb_), "tabs"], writes=[pn])
                return pb
            eb = nxt("e", 2)
            en = "E%d" % eb
            if mode == "eb":
                P.op("scalar", lambda e: e.activation(out=Ebuf[:, eb, c0:c1], in_=psum[:, sb_, c0:c1], func=AF.Exp, scale=0.125),
                     reads=[psn(sb_)], writes=[en])
            else:
                P.op("scalar", lambda e: e.activation(out=Ebuf[:, eb, c0:c1], in_=psum[:, sb_, c0:c1], func=AF.Exp,
                                                      bias=tabs[:, 448 + h:449 + h], scale=0.125),
                     reads=[psn(sb_), "tabs"], writes=[en])
            if strip2_ap is not None:
                P.op("vector", lambda e: e.tensor_tensor(out=Ebuf[:, eb, c0:c1], in0=Ebuf[:, eb, c0:c1], in1=strip2_ap[:, c0:c1], op=ALU.mult),
                     reads=[en, "strips"], writes=[en])
            P.op("vector", lambda e: e.tensor_tensor(out=Pbuf[:, pb, c0:c1], in0=Ebuf[:, eb, c0:c1], in1=strip_ap[:, c0:c1], op=ALU.mult),
                 reads=[en, "strips"], writes=[pn])
            return pb

        deferred = []
        pend = []

        def flush_evac():
            while deferred:
                deferred.pop(0)()

        def flush_all():
            while pend:
                pend.pop(0)()
            flush_evac()

        def attend_block(h, q_ap, kd, tiles, qreads, evac):
            ob = nxt("o", 2, 3)
            n = len(tiles)
            assert n >= 2 and tiles[0][7] == 0 and tiles[0][8] == 512
            for i, (kt_ap, kreads, va_ap, vreads, mode, strip, strip2, c0, c1) in enumerate(tiles):
                sb_ = nxt("s", 3, 0)
                s_tile(sb_, kt_ap, q_ap, kd, qreads + kreads, c0, c1)
                if len(pend) >= 2:
                    pend.pop(0)()
                pb = softmax_tile(sb_, mode, h, strip, strip2, c0, c1)
                if i == 1:
                    flush_evac()
                pend.append(lambda va_ap=va_ap, vreads=vreads, pb=pb, i=i, c0=c0, c1=c1: P.op(
                    "tensor", lambda e: e.matmul(psum[:, ob, c0:c1], lhsT=va_ap, rhs=Pbuf[:, pb, c0:c1], start=(i == 0), stop=(i == n - 1)),
                    reads=vreads + ["P%d" % pb], writes=[psn(ob)], inc=True))
            deferred.append(lambda: evac(ob))

        eps_t = sb("eps", [128, 1], F32)
        P.op("vector", lambda e: e.memset(eps_t[:], 1e-6), writes=["eps"])

        def norm_tile(xb, t, gcol, post=True):
            xn = "xt%d" % xb
            P.op("scalar", lambda e: e.activation(out=rtmp[:, 0, :], in_=xt[:, xb, 0:512], func=AF.Square, accum_out=small[:, 0:1]),
                 reads=[xn], writes=["rtmp0", "st0"])
            P.op("scalar", lambda e: e.activation(out=rtmp[:, 0, :], in_=xt[:, xb, 512:1024], func=AF.Square, accum_out=small[:, 1:2]),
                 reads=[xn], writes=["rtmp0", "st1"])
            P.op("vector", lambda e: e.tensor_tensor(out=small[:, 2:3], in0=small[:, 0:1], in1=small[:, 1:2], op=ALU.add),
                 reads=["st0", "st1"], writes=["st2"])
            P.op("scalar", lambda e: e.activation(out=small[:, 3:4], in_=small[:, 2:3], func=AF.Sqrt, bias=eps_t[:, 0:1], scale=1.0 / D),
                 reads=["st2", "eps"], writes=["st3"])
            P.op("vector", lambda e: e.reciprocal(out=small[:, 4:5], in_=small[:, 3:4]), reads=["st3"], writes=["st4"])
            P.op("vector", lambda e: e.tensor_scalar(out=xt[:, xb, :], in0=xt[:, xb, :], scalar1=small[:, 4:5], scalar2=None, op0=ALU.mult),
                 reads=[xn, "st4"], writes=[xn])
            if gcol is None or not post:
                return
            norm_post(xb, t, gcol)

        def norm_post(xb, t, gcol):
            xn = "xt%d" % xb
            for half in range(2):
                b = nxt("f", 8, 0)
                for cc in range(4):
                    c = half * 4 + cc
                    P.op("tensor", lambda e, b=b, c=c, cc=cc: e.transpose(psum[:, b, cc * 128:(cc + 1) * 128], xt[:, xb, c * 128:(c + 1) * 128], ident),
                         reads=[xn, "tabs"], writes=[psn(b)], inc=(cc == 3))
                for cc in range(4):
                    c = half * 4 + cc
                    if half == 0:
                        P.op("scalar", lambda e, b=b, c=c, cc=cc: e.activation(out=hnT[:, c, t * 128:(t + 1) * 128], in_=psum[:, b, cc * 128:(cc + 1) * 128],
                                                                              func=AF.Copy, scale=gains_t[:, gcol + c:gcol + c + 1]),
                             reads=[psn(b), "gains"], writes=["hnT"])
                    else:
                        P.op("vector", lambda e, b=b, c=c, cc=cc: e.tensor_scalar(out=hnT[:, c, t * 128:(t + 1) * 128], in0=psum[:, b, cc * 128:(cc + 1) * 128],
                                                                                 scalar1=gains_t[:, gcol + c:gcol + c + 1], scalar2=None, op0=ALU.mult),
                             reads=[psn(b), "gains"], writes=["hnTd"])

        final_toks = []

        for sq in range(nseq):
            for l in range(depth):
                src_x = (lambda t: x_d[sq, t * 128:(t + 1) * 128, :]) if l == 0 else (lambda t: scr[t * 128:(t + 1) * 128, :])
                P.phase = "norm1"
                prev_post = None
                for t in range(16):
                    xb = nxt("x", 2)
                    P.dma("sync", "ldx%d" % xb, xt[:, xb, :], src_x(t), reads=["scr"], writes=["xt%d" % xb])
                    norm_tile(xb, t, (l * 2 + 0) * 8, post=False)
                    if prev_post is not None:
                        norm_post(*prev_post)
                    prev_post = (xb, t, (l * 2 + 0) * 8)
                norm_post(*prev_post)
                W = w_in[l]
                P.op("vector", lambda e: e.memset(VA[:, :, :, 64:128], 1.0), writes=["VA"])
                P.op("vector", lambda e: e.memset(VCMP[:, 64:128], 1.0), writes=["VCMP"])
                P.op("vector", lambda e: e.memset(gT[:], 0.0), writes=["gT"])

                P.phase = "A"
                for g, dil in (enumerate(A_DIL) if "A" in ST else []):
                    slot = nxt("w", 2)
                    L = S // dil
                    load_w(slot, 0, W[:, C_AQ + g * 128:C_AQ + g * 128 + 128], 128, "aq")
                    load_w(slot, 128, W[:, C_AK + g * 128:C_AK + g * 128 + 128], 128, "ak")
                    load_w(slot, 256, W[:, C_AV + g * 128:C_AV + g * 128 + 128], 128, "av")
                    for which in range(2):
                        def evac(tb, b, which=which):
                            dst = QP[:, which, :].rearrange("p (r l) -> p r l", r=dil)[:, :, tb * 512 // dil:(tb + 1) * 512 // dil]
                            src = psum[:, b, :].rearrange("p (l r) -> p r l", r=dil)
                            cp(dst, src, [psn(b)], ["QP%d" % which])
                        proj_fm(slot, which * 128, 128, evac)

                    def tok_ap(t, c):
                        pos0 = t * 128
                        r = pos0 // L
                        l0 = pos0 % L
                        st_ = r + dil * l0
                        return hnT[:, c, st_:st_ + dil * 127 + 1:dil]

                    def evac_v(t, b):
                        cp(VA[:, t, 0:2, 0:64], psum[:, b, 0:128].rearrange("p (h d) -> p h d", h=2), [psn(b)], ["VA"])
                    proj_tm(slot, 256, 128, tok_ap, evac_v)
                    tps = L // 128
                    Sv = lambda b: psum[:, b:b + 2, 0:256]
                    astr = strips[:, OFF_A(g):OFF_A(g) + 512]
                    def a_front(i):
                        has_prev = (i % tps) != 0
                        sb_ = (0, 5)[nxt("s", 2)]
                        spair = [psn(sb_), psn(sb_ + 1)]
                        qc = slice(i * 128, (i + 1) * 128)
                        for hh in range(2):
                            pr = slice(64 * hh, 64 * hh + 64)
                            if has_prev:
                                P.op("tensor", lambda e, hh=hh, pr=pr: e.matmul(psum[:, sb_ + hh, 0:128], lhsT=QP[pr, 1, (i - 1) * 128:i * 128],
                                                                              rhs=QP[pr, 0, qc], start=True, stop=True),
                                     reads=["QP0", "QP1"], writes=spair, inc=False)
                            P.op("tensor", lambda e, hh=hh, pr=pr: e.matmul(psum[:, sb_ + hh, 128:256], lhsT=QP[pr, 1, qc],
                                                                          rhs=QP[pr, 0, qc], start=True, stop=True),
                                 reads=["QP0", "QP1"], writes=spair, inc=(hh == 1))

                        def mid():
                            eb = nxt("e", 2)
                            pb = nxt("p", 3)
                            lo = 0 if has_prev else 128
                            Ev = Ebuf[:, eb, :].rearrange("p (h x) -> p h x", h=2)
                            Pv = Pbuf[:, pb, :].rearrange("p (h x) -> p h x", h=2)
                            Av = astr.rearrange("p (h x) -> p h x", h=2)
                            P.op("scalar", lambda e: e.activation(out=Ev[:, :, lo:256], in_=Sv(sb_)[:, :, lo:256], func=AF.Exp, scale=0.125),
                                 reads=spair, writes=["E%d" % eb])
                            P.op("vector", lambda e: e.tensor_tensor(out=Pv[:, :, lo:256], in0=Ev[:, :, lo:256], in1=Av[:, :, lo:256], op=ALU.mult),
                                 reads=["E%d" % eb, "strips"], writes=["P%d" % pb])
                            return pb

                        def back(pb):
                            ob = nxt("o", 2, 3)
                            for hh in range(2):
                                if has_prev:
                                    P.op("tensor", lambda e, hh=hh: e.matmul(psum[:, ob, hh * 128:(hh + 1) * 128], lhsT=VA[:, i - 1, hh, :],
                                                                           rhs=Pbuf[:, pb, hh * 256:hh * 256 + 128], start=True, stop=False),
                                         reads=["VA", "P%d" % pb], writes=[psn(ob)], inc=False)
                                P.op("tensor", lambda e, hh=hh: e.matmul(psum[:, ob, hh * 128:(hh + 1) * 128], lhsT=VA[:, i, hh, :],
                                                                       rhs=Pbuf[:, pb, hh * 256 + 128:hh * 256 + 256], start=(not has_prev), stop=True),
                                     reads=["VA", "P%d" % pb], writes=[psn(ob)], inc=(hh == 1))
                            pos0 = i * 128
                            r = pos0 // L
                            l0 = pos0 % L
                            st_ = r + dil * l0
                            cols = slice(st_, st_ + dil * 127 + 1, dil)
                            for hh in range(2):
                                P.op("scalar", lambda e, hh=hh: e.copy(out=QP[64 * hh:64 * hh + 64, 2 + g, cols], in_=psum[0:64, ob, hh * 128:(hh + 1) * 128]),
                                     reads=[psn(ob)], writes=["QP%d" % (2 + g)])
                                if g == 0:
                                    P.op("vector", lambda e, hh=hh: e.tensor_copy(out=dent[64 * hh:64 * hh + 64, cols], in_=psum[64:128, ob, hh * 128:(hh + 1) * 128]),
                                         reads=[psn(ob)], writes=["dent"])
                                else:
                                    P.op("vector", lambda e, hh=hh: e.tensor_tensor(out=dent[64 * hh:64 * hh + 64, cols], in0=dent[64 * hh:64 * hh + 64, cols],
                                                                                  in1=psum[64:128, ob, hh * 128:(hh + 1) * 128], op=ALU.add),
                                         reads=[psn(ob), "dent"], writes=["dent"])
                        return mid, back

                    pendA = None
                    for i in range(16):
                        mid, back = a_front(i)
                        if pendA is not None:
                            pendA[0](pendA[1])
                        pb = mid()
                        pendA = (back, pb)
                    pendA[0](pendA[1])
                if "A" in ST:
                    P.op("vector", lambda e: e.reciprocal(out=dent[:, :], in_=dent[:, :]), reads=["dent"], writes=["dent"])
                for g in (range(3) if "A" in ST else []):
                    P.op("vector", lambda e, g=g: e.tensor_tensor(out=omT[:, g, :], in0=QP[:, 2 + g, :], in1=dent[:, :], op=ALU.mult),
                         reads=["QP%d" % (2 + g), "dent"], writes=["omT"])

                P.phase = "C"
                for j in (range(2) if "C" in ST else []):
                    slot = nxt("w", 2)
                    load_w(slot, 0, W[:, C_CQ + j * 128:C_CQ + j * 128 + 128], 128, "cq")
                    load_w(slot, 128, W[:, C_CK + j * 128:C_CK + j * 128 + 128], 128, "ck")
                    load_w(slot, 256, W[:, C_CV + j * 128:C_CV + j * 128 + 128], 128, "cv")
                    for hh in range(2):
                        P.op("vector", lambda e, hh=hh: e.memset(QP[64:72, hh, 0:1024], 0.0), writes=["QP%d" % hh])
                        P.dma("gpsimd", "ldg", QP[64:72, 2 + hh, :], indm_d, writes=["QP%d" % (2 + hh)])
                    for which in range(2):
                        def evac(tb, b, which=which):
                            for hh in range(2):
                                cp(QP[0:64, which * 2 + hh, tb * 512:(tb + 1) * 512], psum[64 * hh:64 * hh + 64, b, :], [psn(b)], ["QP%d" % (which * 2 + hh)])
                        proj_fm(slot, which * 128, 128, evac)

                    def evac_v(t, b):
                        cp(VA[:, t, 0:2, 0:64], psum[:, b, 0:128].rearrange("p (h d) -> p h d", h=2), [psn(b)], ["VA"])
                    proj_tm(slot, 256, 128, lambda t, c: hnT[:, c, t * 128:(t + 1) * 128], evac_v)
                    for hh in range(2):
                        h = 12 + 2 * j + hh
                        qn, kn = "QP%d" % hh, "QP%d" % (2 + hh)
                        P.op("vector", lambda e, hh=hh: e.tensor_reduce(out=gx[0:64, 0:8], in_=QP[0:64, 2 + hh, :].rearrange("p (n k) -> p n k", k=256),
                                                                      axis=AX.X, op=ALU.add),
                             reads=[kn], writes=["gx"])
                        P.op("vector", lambda e, hh=hh: e.tensor_scalar(out=kmT[0:64, hh * 8:hh * 8 + 8], in0=gx[0:64, 0:8], scalar1=1.0 / 256, scalar2=None, op0=ALU.mult),
                             reads=["gx"], writes=["kmT"])
                        sc64, m64, nm64 = gu[:, 0:64], gu[:, 64:128], gx[:, 64:128]
                        for ti in range(8):
                            i = 8 + ti
                            P.op("tensor", lambda e, i=i, ti=ti, hh=hh: e.matmul(psum[:, 7, ti * 8:ti * 8 + 8], lhsT=QP[0:64, hh, i * 128:(i + 1) * 128],
                                                                               rhs=kmT[0:64, hh * 8:hh * 8 + 8], start=True, stop=True),
                                 reads=[qn, "kmT"], writes=["ps7"], inc=(ti == 7))
                        P.op("vector", lambda e: e.tensor_tensor(out=sc64, in0=psum[:, 7, 0:64], in1=tabs[:, 384:448], op=ALU.add),
                             reads=["ps7", "tabs"], writes=["gu"])
                        for ti in range(8):
                            P.op("vector", lambda e, ti=ti: e.max(out=m64[:, ti * 8:ti * 8 + 8], in_=sc64[:, ti * 8:ti * 8 + 8]), reads=["gu"], writes=["gu_m"])
                        for ti in range(8):
                            P.op("vector", lambda e, ti=ti: e.tensor_scalar(out=nm64[:, ti * 8:ti * 8 + 8], in0=sc64[:, ti * 8:ti * 8 + 8],
                                                                            scalar1=m64[:, ti * 8 + 3:ti * 8 + 4], scalar2=None, op0=ALU.is_ge),
                                 reads=["gu", "gu_m"], writes=["gx"])
                        P.op("vector", lambda e: e.tensor_scalar(out=nm64, in0=nm64, scalar1=-1.0, scalar2=BIG, op0=ALU.add, op1=ALU.mult),
                             reads=["gx"], writes=["gx"])
                        for ti in range(8):
                            bt = 5 + ti // 4
                            P.op("tensor", lambda e, ti=ti, bt=bt: e.transpose(psum[0:8, bt, (ti % 4) * 128:(ti % 4 + 1) * 128], nm64[:, ti * 8:ti * 8 + 8], ident),
                                 reads=["gx", "tabs"], writes=[psn(bt)], inc=(ti % 4 == 3))
                        for half_ in range(2):
                            P.op("scalar", lambda e, half_=half_, hh=hh: e.copy(out=QP[64:72, hh, 1024 + half_ * 512:1536 + half_ * 512], in_=psum[0:8, 5 + half_, :]),
                                 reads=[psn(5 + half_)], writes=[qn])
                        for qb in range(4):
                            tiles = []
                            for kt in range(4 * qb + 4):
                                dl = 512 * qb - 128 * kt
                                if dl >= 256:
                                    mode, strip = "const", None
                                else:
                                    mode, strip = "eb", SB(h, dl + 384, 512)
                                tiles.append((QP[0:72, 2 + hh, kt * 128:(kt + 1) * 128], [kn], VA[:, kt, hh, :], ["VA"], mode, strip, None,
                                              max(0, -dl), 512))

                            def evac_o(ob, qb=qb, hh=hh, j=j):
                                rb = nxt("e", 2)
                                P.op("scalar", lambda e: e.activation(out=rtmp[0:64, rb, :], in_=psum[64:128, ob, :], func=AF.Ln), reads=[psn(ob)], writes=["rtmp%d" % rb])
                                P.op("scalar", lambda e: e.activation(out=rtmp[0:64, rb, :], in_=rtmp[0:64, rb, :], func=AF.Exp, scale=-1.0), reads=["rtmp%d" % rb], writes=["rtmp%d" % rb])
                                P.op("vector", lambda e: e.tensor_tensor(out=omT[64 * hh:64 * hh + 64, 6 + j, qb * 512:(qb + 1) * 512], in0=psum[0:64, ob, :],
                                                                       in1=rtmp[0:64, rb, :], op=ALU.mult),
                                     reads=[psn(ob), "rtmp%d" % rb], writes=["omT"])
                            attend_block(h, QP[0:72, hh, qb * 512:(qb + 1) * 512], 72, tiles, [qn], evac_o)
                        flush_all()

                P.phase = "B"
                for g in (range(2) if "B" in ST else []):
                    P.phase = "B.compress"
                    slot = nxt("w", 2)
                    for kv, cbase in enumerate((C_BKC, C_BVC)):
                        for rep in range(2):
                            load_w(slot, kv * 128 + rep * 64, W[:, cbase + g * 64:cbase + g * 64 + 64], 64, "kc")
                    for kv in range(2):
                        def evac(tb, b, kv=kv):
                            cp(QP[0:64, kv, tb * 512:(tb + 1) * 512], psum[0:64, b, :], [psn(b)], ["QP%d" % kv])
                            if tb == 0:
                                cp(QP[64:128, kv, 0:511], psum[64:128, b, 1:512], [psn(b)], ["QP%d" % kv])
                            else:
                                cp(QP[64:128, kv, tb * 512 - 1:tb * 512 + 511], psum[64:128, b, :], [psn(b)], ["QP%d" % kv])
                        proj_fm(slot, kv * 128, 128, evac)
                    for kv, (w1d, w2d) in enumerate(((w1k, w2k), (w1v, w2v))):
                        slot = nxt("w", 2)
                        w1s = wsl[:, slot, :, :].rearrange("p c n -> p (c n)").rearrange("p (r m) -> p r m", m=256)
                        P.dma("gpsimd", "ldw%d" % slot, w1s, w1d[l].rearrange("(r p) m -> p r m", p=128), writes=["wsl%d" % slot])
                        P.dma("gpsimd", "ldg", w2s, w2d[l].rearrange("(m p) d -> p m d", p=128), reads=["gT"], writes=["w2s"])
                        P.dma("gpsimd", "ldg", pe2s, pe2[l, kv], writes=["pe2s"])
                        for mc in range(2):
                            b = nxt("m", 2, 5)
                            for r in range(16):
                                P.op("tensor", lambda e, b=b, r=r, mc=mc: e.matmul(psum[:, b, 0:1], lhsT=w1s[:, r, mc * 128:(mc + 1) * 128], rhs=pe2s[:, r:r + 1],
                                                                                 start=(r == 0), stop=(r == 15)),
                                     reads=["wsl%d" % slot, "pe2s"], writes=[psn(b)], inc=(r == 15))
                            P.op("vector", lambda e, b=b, mc=mc: e.tensor_copy(out=hb[:, mc:mc + 1], in_=psum[:, b, 0:1]), reads=[psn(b)], writes=["hb"])
                            b = nxt("m", 2, 5)
                            for r in range(16):
                                P.op("tensor", lambda e, b=b, r=r, mc=mc: e.matmul(psum[:, b, 0:127], lhsT=w1s[:, r, mc * 128:(mc + 1) * 128],
                                                                                 rhs=QP[:, kv, 2 * r:2 * r + 16 * 126 + 1:16], start=(r == 0), stop=(r == 15)),
                                     reads=["wsl%d" % slot, "QP%d" % kv], writes=[psn(b)], inc=(r == 15))
                            P.op("scalar", lambda e, b=b, mc=mc: e.activation(out=gx[:, 0:127], in_=psum[:, b, 0:127], func=AF.Identity, bias=hb[:, mc:mc + 1], scale=1.0),
                                 reads=[psn(b), "hb"], writes=["gx"])
                            P.op("vector", lambda e: e.tensor_tensor(out=gu[:, 0:127], in0=gx[:, 0:127], in1=gx[:, 0:127], op=ALU.mult), reads=["gx"], writes=["gu"])
                            P.op("vector", lambda e: e.tensor_scalar(out=gu[:, 0:127], in0=gu[:, 0:127], scalar1=0.044715, scalar2=1.0, op0=ALU.mult, op1=ALU.add),
                                 reads=["gu"], writes=["gu"])
                            P.op("vector", lambda e: e.tensor_tensor(out=gu[:, 0:127], in0=gu[:, 0:127], in1=gx[:, 0:127], op=ALU.mult), reads=["gu", "gx"], writes=["gu"])
                            P.op("scalar", lambda e: e.activation(out=gu[:, 0:127], in_=gu[:, 0:127], func=AF.Sigmoid, scale=2.0 * math.sqrt(2.0 / math.pi)),
                                 reads=["gu"], writes=["gu"])
                            P.op("vector", lambda e, mc=mc: e.tensor_tensor(out=gT[:, mc, 0:127], in0=gu[:, 0:127], in1=gx[:, 0:127], op=ALU.mult),
                                 reads=["gu", "gx"], writes=["gT"])
                        b = nxt("m", 2, 5)
                        if kv == 0:
                            for mc in range(2):
                                P.op("tensor", lambda e, b=b, mc=mc: e.matmul(psum[0:64, b, 0:128], lhsT=w2s[:, mc, :], rhs=gT[:, mc, :], start=(mc == 0), stop=(mc == 1)),
                                     reads=["w2s", "gT"], writes=[psn(b)], inc=(mc == 1))
                            P.op("vector", lambda e, b=b: e.tensor_copy(out=KCMP[0:64, :], in_=psum[0:64, b, 0:128]), reads=[psn(b)], writes=["KCMP"])
                        else:
                            for mc in range(2):
                                P.op("tensor", lambda e, b=b, mc=mc: e.matmul(psum[:, b, 0:64], lhsT=gT[:, mc, :], rhs=w2s[:, mc, :], start=(mc == 0), stop=(mc == 1)),
                                     reads=["w2s", "gT"], writes=[psn(b)], inc=(mc == 1))
                            P.op("vector", lambda e, b=b: e.tensor_copy(out=VCMP[:, 0:64], in_=psum[:, b, 0:64]), reads=[psn(b)], writes=["VCMP"])
                    P.phase = "B.proj"
                    slot = nxt("w", 2)
                    load_w(slot, 0, W[:, C_BQ + g * 192:C_BQ + g * 192 + 192], 192, "bq")
                    load_w(slot, 192, W[:, C_BKS + g * 64:C_BKS + g * 64 + 64], 64, "bks")
                    load_w(slot, 256, W[:, C_BKW + g * 64:C_BKW + g * 64 + 64], 64, "bkw")
                    load_w(slot, 320, W[:, C_BVS + g * 64:C_BVS + g * 64 + 64], 64, "bvs")
                    load_w(slot, 384, W[:, C_BVW + g * 64:C_BVW + g * 64 + 64], 64, "bvw")
                    load_w(slot, 448, W[:, C_BG + g * 9:C_BG + g * 9 + 9], 9, "bg")
                    for hq in range(3):
                        P.op("vector", lambda e, hq=hq: e.memset(QP[64:96, hq, 0:1024], 0.0), reads=["QP0", "QP1"], writes=["QP%d" % hq])
                    P.dma("gpsimd", "ldg", QP[64:96, 3, :], inds_d, writes=["QP3"])

                    def evac1(tb, b):
                        for hh in range(2):
                            cp(QP[0:64, hh, tb * 512:(tb + 1) * 512], psum[64 * hh:64 * hh + 64, b, :], [psn(b)], ["QP%d" % hh])
                    proj_fm(slot, 0, 128, evac1)

                    def evac2(tb, b):
                        for hh in range(2):
                            cp(QP[0:64, 2 + hh, tb * 512:(tb + 1) * 512], psum[64 * hh:64 * hh + 64, b, :], [psn(b)], ["QP%d" % (2 + hh)])
                    proj_fm(slot, 128, 128, evac2)

                    def evac3(tb, b):
                        cp(QP[0:64, 4, tb * 512:(tb + 1) * 512], psum[0:64, b, :], [psn(b)], ["QP4"])
                    proj_fm(slot, 256, 64, evac3)

                    def evac4(tb, b):
                        P.op("scalar", lambda e: e.activation(out=GT[0:9, tb * 512:(tb + 1) * 512], in_=psum[0:9, b, :], func=AF.Sigmoid),
                             reads=[psn(b)], writes=["GT"])
                    proj_fm(slot, 448, 9, evac4)

                    def evac_v(t, b):
                        cp(VA[:, t, 0:2, 0:64], psum[:, b, 0:128].rearrange("p (h d) -> p h d", h=2), [psn(b)], ["VA"])
                    proj_tm(slot, 320, 128, lambda t, c: hnT[:, c, t * 128:(t + 1) * 128], evac_v)

                    def gate_mult(ob, hq, br, qb, clampden, ph):
                        rb = nxt("e", 2)
                        rn = "rtmp%d" % rb
                        if clampden:
                            P.op("vector", lambda e: e.tensor_scalar(out=rtmp[ph:ph + 64, rb, :], in0=psum[64:128, ob, :], scalar1=1e-30, scalar2=None, op0=ALU.max),
                                 reads=[psn(ob)], writes=[rn])
                            P.op("scalar", lambda e: e.activation(out=rtmp[ph:ph + 64, rb, :], in_=rtmp[ph:ph + 64, rb, :], func=AF.Ln), reads=[rn], writes=[rn])
                        else:
                            P.op("scalar", lambda e: e.activation(out=rtmp[ph:ph + 64, rb, :], in_=psum[64:128, ob, :], func=AF.Ln), reads=[psn(ob)], writes=[rn])
                        P.op("scalar", lambda e: e.activation(out=rtmp[ph:ph + 64, rb, :], in_=rtmp[ph:ph + 64, rb, :], func=AF.Exp, scale=-1.0), reads=[rn], writes=[rn])
                        b = nxt("m", 2, 5)
                        jj = hq * 3 + br
                        P.op("tensor", lambda e: e.matmul(psum[0:64, b, :], lhsT=sel_t[0:9, jj * 64:(jj + 1) * 64], rhs=GT[0:9, qb * 512:(qb + 1) * 512], start=True, stop=True),
                             reads=["sel", "GT"], writes=[psn(b)])
                        P.op("vector", lambda e: e.tensor_tensor(out=rtmp[ph:ph + 64, rb, :], in0=rtmp[ph:ph + 64, rb, :], in1=psum[0:64, b, :], op=ALU.mult),
                             reads=[rn, psn(b)], writes=[rn])
                        return rb

                    P.phase = "B.cmp"
                    for qb in range(4):
                        impb = 7 if qb >= 2 else None
                        for hq in range(3):
                            hb_ = 3 * g + hq
                            h = 6 + hb_
                            ch, ph = 3 + hb_ // 2, 64 * (hb_ % 2)
                            pbs = {}

                            def evac_o(ob, qb=qb, hq=hq, ch=ch, ph=ph):
                                rb = gate_mult(ob, hq, 0, qb, True, ph)
                                P.op("vector", lambda e: e.tensor_tensor(out=omT[ph:ph + 64, ch, qb * 512:(qb + 1) * 512], in0=psum[0:64, ob, :],
                                                                       in1=rtmp[ph:ph + 64, rb, :], op=ALU.mult),
                                     reads=[psn(ob), "rtmp%d" % rb], writes=["omT"])
                            ob = nxt("o", 2, 3)
                            sb_ = nxt("s", 3, 0)
                            s_tile(sb_, KCMP[0:64, :], QP[0:64, hq, qb * 512:(qb + 1) * 512], 64, ["KCMP", "QP%d" % hq])
                            pb = softmax_tile(sb_, "eb", h, strips[:, OFF_CMP + qb * 512:OFF_CMP + (qb + 1) * 512])
                            P.op("tensor", lambda e, ob=ob, pb=pb: e.matmul(PS(ob), lhsT=VCMP[:, :], rhs=Pbuf[:, pb, :], start=True, stop=True),
                                 reads=["VCMP", "P%d" % pb], writes=[psn(ob)])
                            if impb is not None:
                                for ti in range(4):
                                    P.op("tensor", lambda e, pb=pb, ti=ti, hq=hq: e.matmul(psum[:, 7, ti * 128 + hq * 33:ti * 128 + hq * 33 + 33],
                                                                                         lhsT=Pbuf[:, pb, ti * 128:(ti + 1) * 128], rhs=ov_t[:, 0:33], start=True, stop=True),
                                         reads=["P%d" % pb, "ov"], writes=["ps7"], inc=(ti == 3))
                            flush_evac()
                            deferred.append(lambda evac_o=evac_o, ob=ob: evac_o(ob))
                        if impb is not None:
                            import os as _os
                            for ti in [int(c) for c in _os.environ.get("TI_ORDER", "0123")]:
                                i = 4 * qb + ti
                                base = ti * 128
                                P.op("vector", lambda e, base=base: e.reciprocal(out=rec3[:, 0:3], in_=psum[:, 7, base + 32:base + 99:33]), reads=["ps7"], writes=["rec3"])
                                P.op("vector", lambda e, base=base: e.tensor_scalar(out=scb[:, :], in0=psum[:, 7, base:base + 32], scalar1=rec3[:, 0:1], scalar2=None, op0=ALU.mult),
                                     reads=["ps7", "rec3"], writes=["scb"])
                                for hq in (1, 2):
                                    P.op("vector", lambda e, base=base, hq=hq: e.scalar_tensor_tensor(out=scb[:, :], in0=psum[:, 7, base + hq * 33:base + hq * 33 + 32],
                                                                                                     scalar=rec3[:, hq:hq + 1], in1=scb[:, :], op0=ALU.mult, op1=ALU.add),
                                         reads=["ps7", "rec3", "scb"], writes=["scb"])
                                P.op("vector", lambda e, i=i: e.tensor_tensor(out=scb[:, :], in0=scb[:, :], in1=tabs[:, 128 + (i - 8) * 32:128 + (i - 8) * 32 + 32], op=ALU.add),
                                     reads=["scb", "tabs"], writes=["scb"])
                                P.op("vector", lambda e: e.max(out=m8[:, 0:8], in_=scb[:, :]), reads=["scb"], writes=["m8"])
                                P.op("vector", lambda e: e.match_replace(out=sctmp[:, :], in_to_replace=m8[:, 0:8], in_values=scb[:, :], imm_value=-1e30),
                                     reads=["scb", "m8"], writes=["sctmp"])
                                P.op("vector", lambda e: e.max(out=m8[:, 8:16], in_=sctmp[:, :]), reads=["sctmp"], writes=["m8"])
                                P.op("vector", lambda e: e.tensor_scalar(out=nmk[:, :], in0=scb[:, :], scalar1=m8[:, 15:16], scalar2=None, op0=ALU.is_ge),
                                     reads=["scb", "m8"], writes=["nmk"])
                                P.op("vector", lambda e: e.tensor_scalar(out=nmk[:, :], in0=nmk[:, :], scalar1=-1.0, scalar2=BIG, op0=ALU.add, op1=ALU.mult),
                                     reads=["nmk"], writes=["nmk"])
                                b2 = nxt("m", 2, 5)
                                P.op("tensor", lambda e, b2=b2: e.transpose(psum[0:32, b2, 0:128], nmk[:, :], ident), reads=["nmk", "tabs"], writes=[psn(b2)])
                                for hq in range(3):
                                    cp(QP[64:96, hq, i * 128:(i + 1) * 128], psum[0:32, b2, 0:128], [psn(b2)], ["QP%d" % hq], eng="scalar")
                    flush_evac()
                    P.phase = "B.slcwin"
                    for hq in range(3):
                        hb_ = 3 * g + hq
                        h = 6 + hb_
                        ch, ph = 3 + hb_ // 2, 64 * (hb_ % 2)
                        qn = "QP%d" % hq
                        for qb in range(4):
                            for br in (1, 2):
                                tiles = []
                                if br == 1:
                                    for kt in range(4 * qb + 4):
                                        dl = 512 * qb - 128 * kt
                                        if dl >= 256:
                                            mode, strip = "const", None
                                        else:
                                            mode, strip = "eb", SB(h, dl + 384, 512)
                                        tiles.append((QP[0:96, 3, kt * 128:(kt + 1) * 128], ["QP3"], VA[:, kt, 0, :], ["VA"], mode, strip, None,
                                                      max(0, -dl), 512))
                                    kd = 96
                                else:
                                    for kt in range(max(0, 4 * qb - 4), 4 * qb + 4):
                                        dl = 512 * qb - 128 * kt
                                        if dl >= 256:
                                            mode, strip, strip2 = "cb", strips[:, OFF_UM + dl:OFF_UM + dl + 512], None
                                        elif dl == 128:
                                            mode, strip, strip2 = "eb", strips[:, OFF_UM + dl:OFF_UM + dl + 512], SB(h, dl + 384, 512)
                                        else:
                                            mode, strip, strip2 = "eb", SB(h, dl + 384, 512), None
                                        c0_ = max(0, -dl)
                                        c1_ = 512 if dl <= 128 else 640 - dl
                                        tiles.append((QP[0:64, 4, kt * 128:(kt + 1) * 128], ["QP4"], VA[:, kt, 1, :], ["VA"], mode, strip, strip2, c0_, c1_))
                                    tiles.sort(key=lambda tl: 0 if (tl[7] == 0 and tl[8] == 512) else 1)
                                    kd = 64

                                def evac_o(ob, qb=qb, hq=hq, ch=ch, ph=ph, br=br):
                                    rb = gate_mult(ob, hq, br, qb, False, ph)
                                    rn = "rtmp%d" % rb
                                    P.op("vector", lambda e: e.tensor_tensor(out=rtmp[ph:ph + 64, rb, :], in0=psum[0:64, ob, :], in1=rtmp[ph:ph + 64, rb, :], op=ALU.mult),
                                         reads=[psn(ob), rn], writes=[rn])
                                    P.op("vector", lambda e: e.tensor_tensor(out=omT[ph:ph + 64, ch, qb * 512:(qb + 1) * 512], in0=omT[ph:ph + 64, ch, qb * 512:(qb + 1) * 512],
                                                                           in1=rtmp[ph:ph + 64, rb, :], op=ALU.add),
                                         reads=["omT", rn], writes=["omT"])
                                attend_block(h, QP[0:kd, hq, qb * 512:(qb + 1) * 512], kd, tiles, [qn], evac_o)
                    flush_all()

                if debug and sq == 0 and l == 0:
                    P.dma("sync", "dbg", dbg["omT"], omT, reads=["omT"])

                P.phase = "merge"
                Gm = dent.rearrange("p (b n) -> p b n", b=4)
                for fc in (range(8) if "merge" in ST else []):
                    slot = nxt("w", 2)
                    load_w(slot, 0, w_br[l][:, fc * 128:(fc + 1) * 128], 128, "wb")
                    for gi in range(3):
                        load_w(slot, 128 + gi * 128, W[:, C_MG + gi * 1024 + fc * 128:C_MG + gi * 1024 + fc * 128 + 128], 128, "mg")
                    for tb in range(4):
                        tsl = slice(tb * 512, (tb + 1) * 512)
                        bs = [5, 6, 7]
                        gb = [nxt("s", 3, 0) for _ in range(3)]
                        rng = ((0, 3), (3, 6), (6, 8))
                        for gi in range(3):
                            for c in range(8):
                                P.op("tensor", lambda e, gi=gi, c=c: e.matmul(PS(gb[gi]), lhsT=wsl[:, slot, c, 128 + gi * 128:256 + gi * 128], rhs=hnT[:, c, tsl],
                                                                            start=(c == 0), stop=(c == 7)),
                                     reads=["wsl%d" % slot, "hnT"], writes=[psn(gb[gi])], inc=(c == 7))
                            P.op("scalar", lambda e, gi=gi: e.activation(out=Gm[:, gi, :], in_=PS(gb[gi]), func=AF.Sigmoid), reads=[psn(gb[gi])], writes=["Gm%d" % gi])
                            k0, k1 = rng[gi]
                            for k in range(k0, k1):
                                P.op("tensor", lambda e, gi=gi, k=k, k0=k0, k1=k1: e.matmul(PS(bs[gi]), lhsT=wsl[:, slot, k, 0:128], rhs=omT[:, k, tsl],
                                                                                          start=(k == k0), stop=(k == k1 - 1)),
                                     reads=["wsl%d" % slot, "omT"], writes=[psn(bs[gi])], inc=(k == k1 - 1))
                        P.op("vector", lambda e: e.tensor_tensor(out=Gm[:, 0, :], in0=Gm[:, 0, :], in1=PS(bs[0]), op=ALU.mult), reads=["Gm0", psn(bs[0])], writes=["Gm0"])
                        P.op("vector", lambda e: e.tensor_tensor(out=Gm[:, 1, :], in0=Gm[:, 1, :], in1=PS(bs[1]), op=ALU.mult), reads=["Gm1", psn(bs[1])], writes=["Gm1"])
                        P.op("vector", lambda e: e.tensor_tensor(out=Gm[:, 2, :], in0=Gm[:, 2, :], in1=PS(bs[2]), op=ALU.mult), reads=["Gm2", psn(bs[2])], writes=["Gm2"])
                        P.op("vector", lambda e: e.tensor_tensor(out=Gm[:, 0, :], in0=Gm[:, 0, :], in1=Gm[:, 1, :], op=ALU.add), reads=["Gm0", "Gm1"], writes=["Gm0"])
                        P.op("vector", lambda e, fc=fc, tsl=tsl: e.tensor_tensor(out=mergedT[:, fc, tsl], in0=Gm[:, 0, :], in1=Gm[:, 2, :], op=ALU.add),
                             reads=["Gm0", "Gm2", "QP0", "QP1", "QP2", "QP3", "QP4", "VA"], writes=["mergedT"])

                P.phase = "outproj"
                wo = wsl_raw[:].rearrange("p (c n) -> p c n", c=8)
                P.dma("gpsimd", "ldw0", wo, w_out[l].rearrange("(c p) n -> p c n", p=128), reads=["wsl1"], writes=["wsl0", "wsl1"])
                prev_post = None
                for t in (range(16) if "outproj" in ST else []):
                    xb = nxt("x", 2)
                    xn = "xt%d" % xb
                    P.dma("sync", "ldx%d" % xb, xt[:, xb, :], src_x(t), reads=["scr"], writes=[xn])
                    for hf in range(2):
                        b = nxt("f", 8, 0)
                        for c in range(8):
                            P.op("tensor", lambda e, b=b, c=c, hf=hf, t=t: e.matmul(PS(b), lhsT=mergedT[:, c, t * 128:(t + 1) * 128], rhs=wo[:, c, hf * 512:(hf + 1) * 512],
                                                                                  start=(c == 0), stop=(c == 7)),
                                 reads=["mergedT", "wsl0", "wsl1"], writes=[psn(b)], inc=(c == 7))
                        P.op("vector", lambda e, b=b, hf=hf: e.tensor_tensor(out=xt[:, xb, hf * 512:(hf + 1) * 512], in0=xt[:, xb, hf * 512:(hf + 1) * 512], in1=PS(b), op=ALU.add),
                             reads=[psn(b), xn], writes=[xn])
                    P.dma("sync", "stx", scr[t * 128:(t + 1) * 128, :], xt[:, xb, :], reads=[xn], writes=["scr_t%d" % t])
                    if debug and sq == 0 and l == 0:
                        P.dma("sync", "dbg", dbg["x1"][t * 128:(t + 1) * 128, :], xt[:, xb, :], reads=[xn])
                    norm_tile(xb, t, (l * 2 + 1) * 8, post=False)
                    if prev_post is not None:
                        norm_post(*prev_post)
                    prev_post = (xb, t, (l * 2 + 1) * 8)
                if prev_post is not None:
                    norm_post(*prev_post)
                P.barrier()

                P.phase = "ffn"
                P.dma("sync", "ldc", cw_t[:], cwt[l], writes=["cw"])
                P.op("vector", lambda e: e.memset(halo[:], 0.0), writes=["halo%d" % q_ for q_ in range(44)])
                wsl4 = wsl_raw[:].rearrange("p (s c n) -> p s c n", s=4, c=8)
                for hf in (range(2) if "ffn" in ST else []):
                    P.phase = "ffn.up"
                    for j in range(22):
                        s4 = nxt("w4", 4)
                        wn = "wf%d" % s4
                        P.dma("gpsimd", "ldf%d" % s4, wsl4[:, s4, :, 0:128], w_up[l][:, j * 128:(j + 1) * 128].rearrange("(c p) n -> p c n", p=128),
                              writes=[wn, "wsl%d" % (s4 // 2)])
                        P.dma("gpsimd", "ldf%d" % s4, wsl4[:, s4, :, 128:256], w_up[l][:, DFF + j * 128:DFF + (j + 1) * 128].rearrange("(c p) n -> p c n", p=128),
                              writes=[wn])
                        par = j % 2
                        for au in range(2):
                            rbuf = rbufs[au][par]
                            obuf = obufs[au][par]
                            jj = au * 22 + j
                            rn, on = "r%d_%d" % (au, par), "o%d_%d" % (au, par)
                            P.op("scalar", lambda e, rbuf=rbuf, jj=jj: e.copy(out=rbuf[:, 0:2], in_=halo[:, jj, :]), reads=["halo%d" % jj], writes=[rn])
                            for tbl in range(2):
                                tb = hf * 2 + tbl
                                b = nxt("f", 8, 0)
                                for c in range(8):
                                    P.op("tensor", lambda e, b=b, c=c, tb=tb, au=au: e.matmul(PS(b), lhsT=wsl4[:, s4, c, au * 128:(au + 1) * 128], rhs=hnT[:, c, tb * 512:(tb + 1) * 512],
                                                                                            start=(c == 0), stop=(c == 7)),
                                         reads=[wn, "hnT"], writes=[psn(b)], inc=(c == 7))
                                P.op("scalar", lambda e, b=b, rbuf=rbuf, tbl=tbl: e.copy(out=rbuf[:, 2 + tbl * 512:2 + (tbl + 1) * 512], in_=PS(b)), reads=[psn(b)], writes=[rn])
                            cwb = jj * 4
                            P.op("scalar", lambda e, rbuf=rbuf, obuf=obuf, cwb=cwb: e.activation(out=obuf[:, :], in_=rbuf[:, 2:1026], func=AF.Identity,
                                                                                               bias=cw_t[:, cwb + 3:cwb + 4], scale=cw_t[:, cwb + 2:cwb + 3]),
                                 reads=[rn, "cw"], writes=[on])
                            P.op("vector", lambda e, rbuf=rbuf, obuf=obuf, cwb=cwb: e.scalar_tensor_tensor(out=obuf[:, :], in0=rbuf[:, 1:1025], scalar=cw_t[:, cwb + 1:cwb + 2],
                                                                                                         in1=obuf[:, :], op0=ALU.mult, op1=ALU.add),
                                 reads=[rn, on, "cw"], writes=[on])
                            P.op("vector", lambda e, rbuf=rbuf, obuf=obuf, cwb=cwb: e.scalar_tensor_tensor(out=obuf[:, :], in0=rbuf[:, 0:1024], scalar=cw_t[:, cwb:cwb + 1],
                                                                                                         in1=obuf[:, :], op0=ALU.mult, op1=ALU.add),
                                 reads=[rn, on, "cw"], writes=[on])
                            if hf == 0:
                                P.op("scalar", lambda e, rbuf=rbuf, jj=jj: e.copy(out=halo[:, jj, :], in_=rbuf[:, 1024:1026]), reads=[rn], writes=["halo%d" % jj])
                        oa_, ou_ = obufs[0][par], obufs[1][par]
                        P.op("scalar", lambda e: e.activation(out=sg[:, :], in_=oa_[:, :], func=AF.Silu), reads=["o0_%d" % par], writes=["sg"])
                        P.op("vector", lambda e, j=j: e.tensor_tensor(out=gTf[:, j, :], in0=sg[:, :], in1=ou_[:, :], op=ALU.mult), reads=["sg", "o1_%d" % par], writes=["gTf%d" % j])
                    P.phase = "ffn.dn"
                    xt4 = xt[:].rearrange("p b (h n) -> p (b h) n", h=2)
                    for chh in range(2):
                        xbufs = {}
                        for tt in range(4):
                            t = hf * 8 + tt
                            xn = "xq%d" % tt
                            P.dma("sync", "ldx4_%d" % tt, xt4[:, tt, :], scr[t * 128:(t + 1) * 128, chh * 512:(chh + 1) * 512],
                                  reads=["scr_t%d" % t], writes=[xn, "xt0", "xt1"])
                            xbufs[tt] = (tt, xn)
                        for j in range(22):
                            sd = nxt("wd", 6)
                            dn = "wd%d" % sd
                            P.dma("gpsimd", "ldd%d" % sd, wdsl[:, sd, :], w_dn[l][j * 128:(j + 1) * 128, chh * 512:(chh + 1) * 512], writes=[dn])
                            for tt in range(8):
                                P.op("tensor", lambda e, j=j, tt=tt: e.matmul(PS(tt), lhsT=gTf[:, j, tt * 128:(tt + 1) * 128], rhs=wdsl[:, sd, :], start=(j == 0), stop=(j == 21)),
                                     reads=["gTf%d" % j, dn], writes=[psn(tt)], inc=(tt == 7))
                        def dn_add(tt):
                            xb4, xn = xbufs[tt]
                            P.op("vector", lambda e: e.tensor_tensor(out=xt4[:, xb4, :], in0=xt4[:, xb4, :], in1=PS(tt), op=ALU.add), reads=[psn(tt), xn], writes=[xn])

                        def dn_store(tt):
                            t = hf * 8 + tt
                            xb4, xn = xbufs[tt]
                            P.dma("sync", "stx", scr[t * 128:(t + 1) * 128, chh * 512:(chh + 1) * 512], xt4[:, xb4, :], reads=[xn], writes=["scr_t%d" % t, "scr"])

                        for tt in range(4):
                            dn_add(tt)
                        for tt in range(4):
                            dn_store(tt)
                        for tt in range(4, 8):
                            t2 = hf * 8 + tt
                            xb4, xn = xbufs[tt - 4]
                            P.dma("sync", "ldx4_%d" % xb4, xt4[:, xb4, :], scr[t2 * 128:(t2 + 1) * 128, chh * 512:(chh + 1) * 512], reads=["scr_t%d" % t2], writes=[xn])
                            xbufs[tt] = (xb4, xn)
                        for tt in range(4, 8):
                            dn_add(tt)
                            dn_store(tt)
                P.barrier()
            P.phase = "final"
            for t in (range(16) if "final" in ST else []):
                xb = nxt("x", 2)
                xn = "xt%d" % xb
                P.dma("sync", "ldx%d" % xb, xt[:, xb, :], scr[t * 128:(t + 1) * 128, :], reads=["scr", "scr_t%d" % t], writes=[xn])
                norm_tile(xb, t, None)
                P.op("vector", lambda e, xb=xb: e.tensor_tensor(out=xt[:, xb, :], in0=xt[:, xb, :], in1=gfin_t[:, :], op=ALU.mult), reads=[xn, "gfin"], writes=[xn])
                final_toks.append(P.dma("sync", "sto", out_d[sq, t * 128:(t + 1) * 128, :], xt[:, xb, :], reads=[xn], writes=["scr"]))
            P.barrier()
        P.finish(final_toks)
    return nc


_CACHE = {}


def _prep_inputs(inputs):
    f = lambda a: np.ascontiguousarray(np.asarray(a, np.float32))
    hc = _host_consts(inputs["rel_bias"])
    gains = np.zeros((128, 32), np.float32)
    for l in range(DEPTH):
        gains[:, (l * 2 + 0) * 8:(l * 2 + 0) * 8 + 8] = f(inputs["norm_mix"])[l].reshape(8, 128).T
        gains[:, (l * 2 + 1) * 8:(l * 2 + 1) * 8 + 8] = f(inputs["norm_ffn"])[l].reshape(8, 128).T
    cwt = np.zeros((DEPTH, 128, 176), np.float32)
    for l in range(DEPTH):
        cw = f(inputs["conv_w"])[l]
        cb = f(inputs["conv_b"])[l]
        for k in range(3):
            cwt[l, :, k::4] = cw[k].reshape(44, 128).T
        cwt[l, :, 3::4] = cb.reshape(44, 128).T
    pe2 = np.zeros((DEPTH, 2, 128, 16), np.float32)
    for l in range(DEPTH):
        for kv, nm in enumerate(("cmp_pe_k", "cmp_pe_v")):
            pe = f(inputs[nm])[l]
            pe2[l, kv] = pe.reshape(16, 2, 64).transpose(1, 2, 0).reshape(128, 16)
    shared = dict(
        w_in=f(inputs["w_in"]), cmp_w1_k=f(inputs["cmp_w1_k"]), cmp_w2_k=f(inputs["cmp_w2_k"]),
        cmp_w1_v=f(inputs["cmp_w1_v"]), cmp_w2_v=f(inputs["cmp_w2_v"]), pe2=pe2,
        w_branch=f(inputs["w_branch"]), w_out=f(inputs["w_out"]), w_up=f(inputs["w_up"]), w_down=f(inputs["w_down"]),
        gains_in=gains, cwt=cwt, norm_final=f(inputs["norm_final"]),
        strips_in=hc["strips"], tabs_in=hc["tabs"], ov_in=hc["ov"], inds=hc["inds"], indm=hc["indm"], sel_in=hc["sel"],
    )
    x = f(inputs["x"])
    return [dict(shared, x=np.ascontiguousarray(x[c * SEQ_PER_CORE:(c + 1) * SEQ_PER_CORE])) for c in range(NCORE)]


def kernel(**inputs):
    if "nc" not in _CACHE:
        _CACHE["nc"] = build_program()
    nc = _CACHE["nc"]
    in_maps = _prep_inputs(inputs)
    res = run_bass_kernel_spmd(nc, in_maps, core_ids=list(range(NCORE)))
    out = np.concatenate([np.asarray(r["out"], np.float32) for r in res.results], axis=0)
    return out
```

```python
import math
from contextlib import ExitStack
import numpy as np
import concourse.bass as bass
import concourse.mybir as mybir
from concourse.bass_utils import run_bass_kernel_spmd

F32 = mybir.dt.float32
BF16 = mybir.dt.bfloat16
AF = mybir.ActivationFunctionType
ALU = mybir.AluOpType
AX = mybir.AxisListType

S = 2048
D = 1024
NCORE = 8
SEQ_PER_CORE = 2
DEPTH = 2
INC = 6162
DFF = 2816
BIG = 240000.0
NEGB = -30000.0
C_AQ, C_AK, C_AV = 0, 384, 768
C_BQ, C_BKC, C_BVC, C_BKS, C_BVS, C_BKW, C_BVW, C_BG = 1152, 1536, 1664, 1792, 1920, 2048, 2176, 2304
C_CQ, C_CK, C_CV, C_MG = 2322, 2578, 2834, 3090
A_DIL = (1, 4, 16)
NSTRIP = 10 * 1024 + 3 * 512 + 1024 + 2048
OFF_B = lambda h: (h - 6) * 1024
OFF_A = lambda g: 10240 + g * 512
OFF_UM = 10240 + 1536
OFF_CMP = OFF_UM + 1024

ENGS = ("tensor", "vector", "scalar", "gpsimd", "sync")


class Unit:
    __slots__ = ("w", "r")

    def __init__(self):
        self.w = None
        self.r = {}


class _Rec:
    def __init__(self):
        self.call = None

    def __getattr__(self, name):
        def f(*a, **kw):
            self.call = (name, a, kw)
            return self
        return f


class Prog:
    def __init__(self, nc, stack):
        self.nc = nc
        self.stack = stack
        self.q = {e: [] for e in ENGS}
        self.sems = {}
        self.cnt = {}
        self.seen = {e: {} for e in ENGS}
        for e in ENGS:
            self._sem("E_" + e)
        self.units = {}
        self.waitmax = {}
        self.limit = None
        self.nrec = 0
        self.log = []
        self.phase = "setup"
        self.pe_phase = []

    def _sem(self, key):
        if key not in self.sems:
            self.sems[key] = self.stack.enter_context(self.nc.semaphore(key))
            self.cnt[key] = 0
        return self.sems[key]

    def unit(self, key):
        u = self.units.get(key)
        if u is None:
            u = Unit()
            self.units[key] = u
        return u

    def _deps(self, eng, reads, writes, same_war=False, own_sem=None):
        need = {}

        def add(tok, kind):
            if tok is None:
                return
            k, v = tok
            if kind == "waw" and own_sem is not None and k == own_sem:
                return
            if k == "E_" + eng:
                if eng == "tensor":
                    return
                if kind == "war" and not same_war:
                    return
            if v > need.get(k, 0):
                need[k] = v

        for u in reads:
            add(u.w, "raw")
        for u in writes:
            add(u.w, "waw")
            for k, v in u.r.items():
                add((k, v), "war")
        out = []
        seen = self.seen[eng]
        for k in list(need):
            if not k.startswith("E_"):
                need[k] = self.cnt[k]
        for k, v in need.items():
            if seen.get(k, 0) < v:
                seen[k] = v
                out.append((k, v))
                if not k.startswith("E_") and v > self.waitmax.get(k, 0):
                    self.waitmax[k] = v
        return out

    def _mark(self, tok, reads, writes):
        for u in reads:
            if u.r.get(tok[0], 0) < tok[1]:
                u.r[tok[0]] = tok[1]
        for u in writes:
            u.w = tok
            u.r = {}

    def op(self, eng, fn, reads=(), writes=(), inc=True):
        self.nrec += 1
        if self.limit is not None and self.nrec > self.limit:
            return None
        if "hnT" in reads:
            reads = list(reads) + ["hnTd"]
        reads = [self.unit(u) for u in reads]
        writes = [self.unit(u) for u in writes]
        waits = self._deps(eng, reads, writes, same_war=(eng != "tensor"))
        k = "E_" + eng
        if inc:
            self.cnt[k] += 1
            tok = (k, self.cnt[k])
        else:
            tok = (k, self.cnt[k] + 1)
        rec = _Rec()
        fn(rec)
        name, a, kw = rec.call
        self.log.append((self.nrec, eng, name, [str(getattr(x, "shape", x)) for x in list(a) + list(kw.values())][:4]))
        if eng == "tensor":
            self.pe_phase.append(self.phase)
        self.q[eng].append((waits, (lambda e: getattr(e, name)(*a, **kw)), (k, 1) if inc else None))
        self._mark(tok, reads, writes)
        return tok

    def dma(self, eng, semkey, out, in_, reads=(), writes=()):
        self.nrec += 1
        if self.limit is not None and self.nrec > self.limit:
            return None
        self.log.append((self.nrec, eng, "dma", semkey))
        reads = [self.unit(u) for u in reads]
        writes = [self.unit(u) for u in writes]
        self._sem(semkey)
        waits = self._deps(eng, reads, writes, same_war=True, own_sem=semkey)
        wm = self.waitmax.get(semkey, 0)
        if wm > self.seen[eng].get(semkey, 0):
            self.seen[eng][semkey] = wm
            waits.append((semkey, wm))
        self.cnt[semkey] += 16
        tok = (semkey, self.cnt[semkey])
        self.q[eng].append((waits, lambda e: e.dma_start(out=out, in_=in_), (semkey, 16)))
        self._mark(tok, reads, writes)
        return tok

    def barrier(self):
        cur = [(k, v) for k, v in self.cnt.items() if v > 0]
        for e in ENGS:
            waits = []
            for k, v in cur:
                if k == "E_" + e:
                    continue
                if self.seen[e].get(k, 0) < v:
                    self.seen[e][k] = v
                    waits.append((k, v))
                    if not k.startswith("E_") and v > self.waitmax.get(k, 0):
                        self.waitmax[k] = v
            if waits:
                self.q[e].append((waits, None, None))

    def finish(self, final_tokens):
        prog = self
        fin = [t for t in final_tokens if t is not None]
        for k, v in self.cnt.items():
            if v > 0 and not k.startswith("E_tensor"):
                fin.append((k, v))

        def emit(engname):
            def body(e):
                for waits, fn, inc in prog.q[engname]:
                    for (k, v) in waits:
                        e.wait_ge(prog.sems[k], v)
                    if fn is None:
                        continue
                    ins = fn(e)
                    if inc is not None:
                        ins.then_inc(prog.sems[inc[0]], inc[1])
                if engname == "sync":
                    for (k, v) in fin:
                        e.wait_ge(prog.sems[k], v)
            return body

        with self.nc.Block() as block:
            block.tensor(emit("tensor"))
            block.vector(emit("vector"))
            block.scalar(emit("scalar"))
            block.gpsimd(emit("gpsimd"))
            block.sync(emit("sync"))


def _bucket(d):
    n = np.maximum(d, 0)
    nf = np.maximum(n, 1).astype(np.float32)
    large = 16 + (np.log(nf / np.float32(16)) / np.float32(math.log(8.0)) * np.float32(16)).astype(np.int32)
    return np.where(n < 16, n, np.minimum(large, 31)).astype(np.int64)


def _host_consts(rel_bias):
    rb = np.asarray(rel_bias, np.float32)
    ki = np.arange(128)[:, None]
    strips = np.zeros((128, NSTRIP), np.float32)
    u = np.arange(1024)[None, :]
    d = u - 384 - ki
    bk = _bucket(d)
    for h in range(6, 16):
        strips[:, OFF_B(h):OFF_B(h) + 1024] = np.where(d >= 0, rb[bk, h], np.float32(NEGB))
    qi = np.arange(128)[None, :]
    for g, dil in enumerate(A_DIL):
        for hh in range(2):
            h = 2 * g + hh
            ds = qi - ki + 128
            prev = np.where(ds <= 128, rb[_bucket(ds * dil), h], np.float32(NEGB))
            ds2 = qi - ki
            diag = np.where(ds2 >= 0, rb[_bucket(ds2 * dil), h], np.float32(NEGB))
            o = OFF_A(g) + hh * 256
            strips[:, o:o + 128] = prev
            strips[:, o + 128:o + 256] = diag
    w = np.arange(1024)[None, :]
    strips[:, OFF_UM:OFF_UM + 1024] = np.where(w - ki <= 511, np.float32(0), np.float32(NEGB))
    q = np.arange(2048)[None, :]
    ok = (16 * ki + 31 <= q) & (ki < 127)
    strips[:, OFF_CMP:OFF_CMP + 2048] = np.where(ok, np.float32(0), np.float32(NEGB))
    tabs = np.zeros((128, 128 + 256 + 64 + 16), np.float32)
    tabs[:, 0:128] = np.eye(128, dtype=np.float32)
    p = np.arange(128)[:, None]
    for i in range(8, 16):
        t = 128 * i + p
        jt = t // 64
        j = np.arange(32)[None, :]
        forced = (j == 0) | (j == jt) | (j == jt - 1)
        fb = np.where(j <= jt, np.where(forced, np.float32(1e4), np.float32(0)), np.float32(-1e30))
        tabs[:, 128 + (i - 8) * 32:128 + (i - 8) * 32 + 32] = fb
        nb = t // 256
        n = np.arange(8)[None, :]
        pm = np.where(n < nb, np.float32(0), np.where(n == nb, np.float32(1e4), np.float32(-1e30)))
        tabs[:, 384 + (i - 8) * 8:384 + (i - 8) * 8 + 8] = pm
    tabs[:, 448:464] = rb[31][None, :]
    ov = np.zeros((128, 64), np.float32)
    c = np.arange(128)[:, None]
    j = np.arange(32)[None, :]
    ov[:, 0:32] = ((16 * c < 64 * j + 64) & (16 * c + 32 > 64 * j) & (c < 127)).astype(np.float32)
    ov[:127, 32] = 1.0
    k = np.arange(2048)[None, :]
    inds = (k // 64 == np.arange(32)[:, None]).astype(np.float32)
    indm = (k // 256 == np.arange(8)[:, None]).astype(np.float32)
    sel = np.zeros((18, 18 * 64), np.float32)
    for jj in range(18):
        sel[jj, jj * 64:(jj + 1) * 64] = 1.0
    return dict(strips=strips, tabs=tabs, ov=ov, inds=inds, indm=indm, sel=sel)


def build_program(debug=False, stages=None, nseq=SEQ_PER_CORE, depth=DEPTH, limit=None):
    ST = stages if stages is not None else {"A", "C", "B", "merge", "outproj", "ffn", "final"}
    nc = bass.Bass("TRN2", target_bir_lowering=False)
    dt_in = lambda name, shape: nc.dram_tensor(name, shape, F32, kind="ExternalInput").ap()
    x_d = dt_in("x", [SEQ_PER_CORE, S, D])
    w_in = dt_in("w_in", [DEPTH, D, INC])
    w1k = dt_in("cmp_w1_k", [DEPTH, 2048, 256])
    w2k = dt_in("cmp_w2_k", [DEPTH, 256, 64])
    w1v = dt_in("cmp_w1_v", [DEPTH, 2048, 256])
    w2v = dt_in("cmp_w2_v", [DEPTH, 256, 64])
    pe2 = dt_in("pe2", [DEPTH, 2, 128, 16])
    w_br = dt_in("w_branch", [DEPTH, D, D])
    w_out = dt_in("w_out", [DEPTH, D, D])
    w_up = dt_in("w_up", [DEPTH, D, 2 * DFF])
    w_dn = dt_in("w_down", [DEPTH, DFF, D])
    gains = dt_in("gains_in", [128, 32])
    cwt = dt_in("cwt", [DEPTH, 128, 176])
    gfin = dt_in("norm_final", [D])
    strips_d = dt_in("strips_in", [128, NSTRIP])
    tabs_d = dt_in("tabs_in", [128, 464])
    ov_d = dt_in("ov_in", [128, 64])
    inds_d = dt_in("inds", [32, 2048])
    indm_d = dt_in("indm", [8, 2048])
    sel_d = dt_in("sel_in", [18, 18 * 64])
    out_d = nc.dram_tensor("out", [SEQ_PER_CORE, S, D], F32, kind="ExternalOutput").ap()
    scr = nc.dram_tensor("xscr", [S, D], F32, kind="Internal").ap()
    dbg = {}
    if debug:
        dbg["omT"] = nc.dram_tensor("dbg_omT", [128, 8, S], BF16, kind="ExternalOutput").ap()
        dbg["x1"] = nc.dram_tensor("dbg_x1", [S, D], F32, kind="ExternalOutput").ap()

    with ExitStack() as st:
        P = Prog(nc, st)
        P.limit = limit
        build_program.P = P
        sb = lambda name, shape, dt: st.enter_context(nc.sbuf_tensor(name, shape, dt))
        hnT = sb("hnT", [128, 8, S], BF16)
        wsl_raw = sb("wsl", [128, 8192], BF16)
        wsl = wsl_raw[:].rearrange("p (s c n) -> p s c n", s=2, c=8)
        strips = sb("strips", [128, NSTRIP], BF16)
        tabs = sb("tabs", [128, 464], F32)
        gains_t = sb("gains", [128, 32], F32)
        cw_t = sb("cw", [128, 176], F32)
        gfin_t = sb("gfin", [128, D], F32)
        ov_t = sb("ov", [128, 64], BF16)
        sel_t = sb("sel", [18, 18 * 64], F32)
        xt = sb("xt", [128, 2, D], F32)
        small = sb("small", [128, 64], F32)
        halo = sb("halo", [128, 44, 2], F32)
        ones_bf = sb("ones_bf", [128, 64], BF16)
        ARENA = 44 * 1024
        arena = sb("arena", [128, ARENA], BF16)
        psum = st.enter_context(nc.psum_tensor("ps", [128, 8, 512], F32))

        ident = tabs[:, 0:128]
        omT = arena[:, 0:16384].rearrange("p (c t) -> p c t", c=8)
        QP = arena[:, 16384:26624].rearrange("p (i t) -> p i t", i=5)
        VA = arena[:, 26624:32768].rearrange("p (t h e) -> p t h e", t=16, h=3)
        mergedT = arena[:, 16384:32768].rearrange("p (c t) -> p c t", c=8)
        dent = arena[:, 32768:36864].bitcast(F32)
        Ebuf = arena[:, 36864:38912].bitcast(F32).rearrange("p (b n) -> p b n", b=2)
        Pbuf = arena[:, 38912:40448].rearrange("p (b n) -> p b n", b=3)
        rtmp = arena[:, 40448:42496].bitcast(F32).rearrange("p (b n) -> p b n", b=2)
        misc = arena[:, 42496:44032].bitcast(F32)
        hb = misc[:, 0:4]
        gx = misc[:, 8:136]
        gu = misc[:, 136:264]
        scb = misc[:, 264:296]
        sctmp = misc[:, 296:328]
        m8 = misc[:, 328:344]
        nmk = misc[:, 344:376]
        rec3 = misc[:, 376:380]
        KCMP = misc[:, 384:448].bitcast(BF16)
        VCMP = misc[:, 448:512].bitcast(BF16)
        gT = misc[:, 512:640].bitcast(BF16).rearrange("p (m c) -> p m c", m=2)
        w2s = misc[:, 640:704].bitcast(BF16).rearrange("p (m d) -> p m d", m=2)
        pe2s = misc[:, 704:712].bitcast(BF16)
        kmT = misc[:, 712:720].bitcast(BF16)
        GT = dent
        gTf = arena[:, 0:22528].rearrange("p (j t) -> p j t", j=22)
        wdsl = arena[:, 22528:25600].rearrange("p (s n) -> p s n", s=6)
        _o = 25600
        rbufs = [[None, None], [None, None]]
        obufs = [[None, None], [None, None]]
        for au_ in range(2):
            for par_ in range(2):
                rbufs[au_][par_] = arena[:, _o:_o + 2056].bitcast(F32)
                _o += 2056
        for au_ in range(2):
            for par_ in range(2):
                obufs[au_][par_] = arena[:, _o:_o + 2048].bitcast(F32)
                _o += 2048
        sg = arena[:, _o:_o + 2048].bitcast(F32)
        assert _o + 2048 <= ARENA

        PS = lambda b: psum[:, b, :]
        psn = lambda b: "ps%d" % b
        rr = {"s": 0, "o": 0, "m": 0, "e": 0, "p": 0, "w": 0, "x": 0, "f": 0, "w4": 0, "wd": 0}

        def nxt(kind, n, base=0):
            v = base + rr[kind] % n
            rr[kind] += 1
            return v

        P.dma("sync", "ldc", tabs[:], tabs_d, writes=["tabs"])
        P.dma("sync", "ldc", gains_t[:], gains, writes=["gains"])
        P.dma("sync", "ldc", sel_t[:], sel_d, writes=["sel"])
        P.dma("sync", "ldc", gfin_t[:], gfin.partition_broadcast(128), writes=["gfin"])
        P.dma("gpsimd", "ldg", ov_t[:], ov_d, writes=["ov"])
        P.op("vector", lambda e: e.memset(ones_bf[:], 1.0), writes=["ones_bf"])
        stg = arena[:, 0:8192].bitcast(F32)
        off = 0
        while off < NSTRIP:
            n = min(4096, NSTRIP - off)
            o_, n_ = off, n
            P.dma("sync", "ldc", stg[:, 0:n_], strips_d[:, o_:o_ + n_], writes=["stg"])
            P.op("scalar", lambda e, o_=o_, n_=n_: e.activation(out=strips[:, o_:o_ + n_], in_=stg[:, 0:n_], func=AF.Exp),
                 reads=["stg"], writes=["strips"])
            off += n
        P.barrier()

        def SB(h, lo, n):
            o = OFF_B(h) + lo
            return strips[:, o:o + n]

        def load_w(slot, col, dram2d, ncols, nm):
            P.dma("gpsimd", "ldw%d" % slot, wsl[:, slot, :, col:col + ncols],
                  dram2d.rearrange("(c p) n -> p c n", p=128), writes=["wsl%d" % slot])

        def proj_fm(slot, c0, M, evac, tbs=range(4)):
            for tb in tbs:
                b = nxt("m", 2, 5)
                for c in range(8):
                    P.op("tensor", lambda e, b=b, c=c, tb=tb: e.matmul(
                        psum[0:M, b, :], lhsT=wsl[:, slot, c, c0:c0 + M], rhs=hnT[:, c, tb * 512:(tb + 1) * 512],
                        start=(c == 0), stop=(c == 7)),
                        reads=["wsl%d" % slot, "hnT"], writes=[psn(b)], inc=(c == 7))
                evac(tb, b)

        def proj_tm(slot, c0, N, tok_ap, evac):
            for t in range(16):
                b = nxt("m", 2, 5)
                for c in range(8):
                    P.op("tensor", lambda e, b=b, c=c, t=t: e.matmul(
                        psum[:, b, 0:N], lhsT=tok_ap(t, c), rhs=wsl[:, slot, c, c0:c0 + N],
                        start=(c == 0), stop=(c == 7)),
                        reads=["wsl%d" % slot, "hnT"], writes=[psn(b)], inc=(c == 7))
                evac(t, b)

        alt = {"i": 0}

        def cp(out, in_, reads, writes, eng=None):
            if eng is None:
                eng = "scalar" if alt["i"] % 2 == 0 else "vector"
                alt["i"] += 1
            if eng == "scalar":
                P.op("scalar", lambda e: e.copy(out=out, in_=in_), reads=reads, writes=writes)
            else:
                P.op("vector", lambda e: e.tensor_copy(out=out, in_=in_), reads=reads, writes=writes)

        def s_tile(sb_, kt_ap, q_ap, kd, reads, c0=0, c1=512):
            P.op("tensor", lambda e: e.matmul(psum[:, sb_, c0:c1], lhsT=kt_ap, rhs=q_ap[:, c0:c1], start=True, stop=True),
                 reads=reads, writes=[psn(sb_)])

        def softmax_tile(sb_, mode, h, strip_ap=None, strip2_ap=None, c0=0, c1=512):
            pb = nxt("p", 3)
            pn = "P%d" % pb
            if mode == "const":
                P.op("scalar", lambda e: e.activation(out=Pbuf[:, pb, c0:c1], in_=psum[:, sb_, c0:c1], func=AF.Exp,
                                                      bias=tabs[:, 448 + h:449 + h], scale=0.125),
                     reads=[psn(sb_), "tabs"], writes=[pn])
                return pb
            eb = nxt("e", 2)
            en = "E%d" % eb
            if mode == "eb":
                P.op("scalar", lambda e: e.activation(out=Ebuf[:, eb, c0:c1], in_=psum[:, sb_, c0:c1], func=AF.Exp, scale=0.125),
                     reads=[psn(sb_)], writes=[en])
            else:
                P.op("scalar", lambda e: e.activation(out=Ebuf[:, eb, c0:c1], in_=psum[:, sb_, c0:c1], func=AF.Exp,
                                                      bias=tabs[:, 448 + h:449 + h], scale=0.125),
                     reads=[psn(sb_), "tabs"], writes=[en])
            if strip2_ap is not None:
                P.op("vector", lambda e: e.tensor_tensor(out=Ebuf[:, eb, c0:c1], in0=Ebuf[:, eb, c0:c1], in1=strip2_ap[:, c0:c1], op=ALU.mult),
                     reads=[en, "strips"], writes=[en])
            P.op("vector", lambda e: e.tensor_tensor(out=Pbuf[:, pb, c0:c1], in0=Ebuf[:, eb, c0:c1], in1=strip_ap[:, c0:c1], op=ALU.mult),
                 reads=[en, "strips"], writes=[pn])
            return pb

        deferred = []
        pend = []

        def flush_evac():
            while deferred:
                deferred.pop(0)()

        def flush_all():
            while pend:
                pend.pop(0)()
            flush_evac()

        def attend_block(h, q_ap, kd, tiles, qreads, evac):
            ob = nxt("o", 2, 3)
            n = len(tiles)
            assert n >= 2 and tiles[0][7] == 0 and tiles[0][8] == 512
            for i, (kt_ap, kreads, va_ap, vreads, mode, strip, strip2, c0, c1) in enumerate(tiles):
                sb_ = nxt("s", 3, 0)
                s_tile(sb_, kt_ap, q_ap, kd, qreads + kreads, c0, c1)
                if len(pend) >= 2:
                    pend.pop(0)()
                pb = softmax_tile(sb_, mode, h, strip, strip2, c0, c1)
                if i == 1:
                    flush_evac()
                pend.append(lambda va_ap=va_ap, vreads=vreads, pb=pb, i=i, c0=c0, c1=c1: P.op(
                    "tensor", lambda e: e.matmul(psum[:, ob, c0:c1], lhsT=va_ap, rhs=Pbuf[:, pb, c0:c1], start=(i == 0), stop=(i == n - 1)),
                    reads=vreads + ["P%d" % pb], writes=[psn(ob)], inc=True))
            deferred.append(lambda: evac(ob))

        eps_t = sb("eps", [128, 1], F32)
        P.op("vector", lambda e: e.memset(eps_t[:], 1e-6), writes=["eps"])

        def norm_tile(xb, t, gcol, post=True):
            xn = "xt%d" % xb
            P.op("scalar", lambda e: e.activation(out=rtmp[:, 0, :], in_=xt[:, xb, 0:512], func=AF.Square, accum_out=small[:, 0:1]),
                 reads=[xn], writes=["rtmp0", "st0"])
            P.op("scalar", lambda e: e.activation(out=rtmp[:, 0, :], in_=xt[:, xb, 512:1024], func=AF.Square, accum_out=small[:, 1:2]),
                 reads=[xn], writes=["rtmp0", "st1"])
            P.op("vector", lambda e: e.tensor_tensor(out=small[:, 2:3], in0=small[:, 0:1], in1=small[:, 1:2], op=ALU.add),
                 reads=["st0", "st1"], writes=["st2"])
            P.op("scalar", lambda e: e.activation(out=small[:, 3:4], in_=small[:, 2:3], func=AF.Sqrt, bias=eps_t[:, 0:1], scale=1.0 / D),
                 reads=["st2", "eps"], writes=["st3"])
            P.op("vector", lambda e: e.reciprocal(out=small[:, 4:5], in_=small[:, 3:4]), reads=["st3"], writes=["st4"])
            P.op("vector", lambda e: e.tensor_scalar(out=xt[:, xb, :], in0=xt[:, xb, :], scalar1=small[:, 4:5], scalar2=None, op0=ALU.mult),
                 reads=[xn, "st4"], writes=[xn])
            if gcol is None or not post:
                return
            norm_post(xb, t, gcol)

        def norm_post(xb, t, gcol):
            xn = "xt%d" % xb
            for half in range(2):
                b = nxt("f", 8, 0)
                for cc in range(4):
                    c = half * 4 + cc
                    P.op("tensor", lambda e, b=b, c=c, cc=cc: e.transpose(psum[:, b, cc * 128:(cc + 1) * 128], xt[:, xb, c * 128:(c + 1) * 128], ident),
                         reads=[xn, "tabs"], writes=[psn(b)], inc=(cc == 3))
                for cc in range(4):
                    c = half * 4 + cc
                    if half == 0:
                        P.op("scalar", lambda e, b=b, c=c, cc=cc: e.activation(out=hnT[:, c, t * 128:(t + 1) * 128], in_=psum[:, b, cc * 128:(cc + 1) * 128],
                                                                              func=AF.Copy, scale=gains_t[:, gcol + c:gcol + c + 1]),
                             reads=[psn(b), "gains"], writes=["hnT"])
                    else:
                        P.op("vector", lambda e, b=b, c=c, cc=cc: e.tensor_scalar(out=hnT[:, c, t * 128:(t + 1) * 128], in0=psum[:, b, cc * 128:(cc + 1) * 128],
                                                                                 scalar1=gains_t[:, gcol + c:gcol + c + 1], scalar2=None, op0=ALU.mult),
                             reads=[psn(b), "gains"], writes=["hnTd"])

        final_toks = []

        for sq in range(nseq):
            for l in range(depth):
                src_x = (lambda t: x_d[sq, t * 128:(t + 1) * 128, :]) if l == 0 else (lambda t: scr[t * 128:(t + 1) * 128, :])
                P.phase = "norm1"
                prev_post = None
                for t in range(16):
                    xb = nxt("x", 2)
                    P.dma("sync", "ldx%d" % xb, xt[:, xb, :], src_x(t), reads=["scr"], writes=["xt%d" % xb])
                    norm_tile(xb, t, (l * 2 + 0) * 8, post=False)
                    if prev_post is not None:
                        norm_post(*prev_post)
                    prev_post = (xb, t, (l * 2 + 0) * 8)
                norm_post(*prev_post)
                W = w_in[l]
                P.op("vector", lambda e: e.memset(VA[:, :, :, 64:128], 1.0), writes=["VA"])
                P.op("vector", lambda e: e.memset(VCMP[:, 64:128], 1.0), writes=["VCMP"])
                P.op("vector", lambda e: e.memset(gT[:], 0.0), writes=["gT"])

                P.phase = "A"
                for g, dil in (enumerate(A_DIL) if "A" in ST else []):
                    slot = nxt("w", 2)
                    L = S // dil
                    load_w(slot, 0, W[:, C_AQ + g * 128:C_AQ + g * 128 + 128], 128, "aq")
                    load_w(slot, 128, W[:, C_AK + g * 128:C_AK + g * 128 + 128], 128, "ak")
                    load_w(slot, 256, W[:, C_AV + g * 128:C_AV + g * 128 + 128], 128, "av")
                    for which in range(2):
                        def evac(tb, b, which=which):
                            dst = QP[:, which, :].rearrange("p (r l) -> p r l", r=dil)[:, :, tb * 512 // dil:(tb + 1) * 512 // dil]
                            src = psum[:, b, :].rearrange("p (l r) -> p r l", r=dil)
                            cp(dst, src, [psn(b)], ["QP%d" % which])
                        proj_fm(slot, which * 128, 128, evac)

                    def tok_ap(t, c):
                        pos0 = t * 128
                        r = pos0 // L
                        l0 = pos0 % L
                        st_ = r + dil * l0
                        return hnT[:, c, st_:st_ + dil * 127 + 1:dil]

                    def evac_v(t, b):
                        cp(VA[:, t, 0:2, 0:64], psum[:, b, 0:128].rearrange("p (h d) -> p h d", h=2), [psn(b)], ["VA"])
                    proj_tm(slot, 256, 128, tok_ap, evac_v)
                    tps = L // 128
                    Sv = lambda b: psum[:, b:b + 2, 0:256]
                    astr = strips[:, OFF_A(g):OFF_A(g) + 512]
                    def a_front(i):
                        has_prev = (i % tps) != 0
                        sb_ = (0, 5)[nxt("s", 2)]
                        spair = [psn(sb_), psn(sb_ + 1)]
                        qc = slice(i * 128, (i + 1) * 128)
                        for hh in range(2):
                            pr = slice(64 * hh, 64 * hh + 64)
                            if has_prev:
                                P.op("tensor", lambda e, hh=hh, pr=pr: e.matmul(psum[:, sb_ + hh, 0:128], lhsT=QP[pr, 1, (i - 1) * 128:i * 128],
                                                                              rhs=QP[pr, 0, qc], start=True, stop=True),
                                     reads=["QP0", "QP1"], writes=spair, inc=False)
                            P.op("tensor", lambda e, hh=hh, pr=pr: e.matmul(psum[:, sb_ + hh, 128:256], lhsT=QP[pr, 1, qc],
                                                                          rhs=QP[pr, 0, qc], start=True, stop=True),
                                 reads=["QP0", "QP1"], writes=spair, inc=(hh == 1))

                        def mid():
                            eb = nxt("e", 2)
                            pb = nxt("p", 3)
                            lo = 0 if has_prev else 128
                            Ev = Ebuf[:, eb, :].rearrange("p (h x) -> p h x", h=2)
                            Pv = Pbuf[:, pb, :].rearrange("p (h x) -> p h x", h=2)
                            Av = astr.rearrange("p (h x) -> p h x", h=2)
                            P.op("scalar", lambda e: e.activation(out=Ev[:, :, lo:256], in_=Sv(sb_)[:, :, lo:256], func=AF.Exp, scale=0.125),
                                 reads=spair, writes=["E%d" % eb])
                            P.op("vector", lambda e: e.tensor_tensor(out=Pv[:, :, lo:256], in0=Ev[:, :, lo:256], in1=Av[:, :, lo:256], op=ALU.mult),
                                 reads=["E%d" % eb, "strips"], writes=["P%d" % pb])
                            return pb

                        def back(pb):
                            ob = nxt("o", 2, 3)
                            for hh in range(2):
                                if has_prev:
                                    P.op("tensor", lambda e, hh=hh: e.matmul(psum[:, ob, hh * 128:(hh + 1) * 128], lhsT=VA[:, i - 1, hh, :],
                                                                           rhs=Pbuf[:, pb, hh * 256:hh * 256 + 128], start=True, stop=False),
                                         reads=["VA", "P%d" % pb], writes=[psn(ob)], inc=False)
                                P.op("tensor", lambda e, hh=hh: e.matmul(psum[:, ob, hh * 128:(hh + 1) * 128], lhsT=VA[:, i, hh, :],
                                                                       rhs=Pbuf[:, pb, hh * 256 + 128:hh * 256 + 256], start=(not has_prev), stop=True),
                                     reads=["VA", "P%d" % pb], writes=[psn(ob)], inc=(hh == 1))
                            pos0 = i * 128
                            r = pos0 // L
                            l0 = pos0 % L
                            st_ = r + dil * l0
                            cols = slice(st_, st_ + dil * 127 + 1, dil)
                            for hh in range(2):
                                P.op("scalar", lambda e, hh=hh: e.copy(out=QP[64 * hh:64 * hh + 64, 2 + g, cols], in_=psum[0:64, ob, hh * 128:(hh + 1) * 128]),
                                     reads=[psn(ob)], writes=["QP%d" % (2 + g)])
                                if g == 0:
                                    P.op("vector", lambda e, hh=hh: e.tensor_copy(out=dent[64 * hh:64 * hh + 64, cols], in_=psum[64:128, ob, hh * 128:(hh + 1) * 128]),
                                         reads=[psn(ob)], writes=["dent"])
                                else:
                                    P.op("vector", lambda e, hh=hh: e.tensor_tensor(out=dent[64 * hh:64 * hh + 64, cols], in0=dent[64 * hh:64 * hh + 64, cols],
                                                                                  in1=psum[64:128, ob, hh * 128:(hh + 1) * 128], op=ALU.add),
                                         reads=[psn(ob), "dent"], writes=["dent"])
                        return mid, back

                    pendA = None
                    for i in range(16):
                        mid, back = a_front(i)
                        pb = mid()
                        if pendA is not None:
                            pendA[0](pendA[1])
                        pendA = (back, pb)
                    pendA[0](pendA[1])
                if "A" in ST:
                    P.op("vector", lambda e: e.reciprocal(out=dent[:, :], in_=dent[:, :]), reads=["dent"], writes=["dent"])
                for g in (range(3) if "A" in ST else []):
                    P.op("vector", lambda e, g=g: e.tensor_tensor(out=omT[:, g, :], in0=QP[:, 2 + g, :], in1=dent[:, :], op=ALU.mult),
                         reads=["QP%d" % (2 + g), "dent"], writes=["omT"])

                P.phase = "C"
                for j in (range(2) if "C" in ST else []):
                    slot = nxt("w", 2)
                    load_w(slot, 0, W[:, C_CQ + j * 128:C_CQ + j * 128 + 128], 128, "cq")
                    load_w(slot, 128, W[:, C_CK + j * 128:C_CK + j * 128 + 128], 128, "ck")
                    load_w(slot, 256, W[:, C_CV + j * 128:C_CV + j * 128 + 128], 128, "cv")
                    for hh in range(2):
                        P.op("vector", lambda e, hh=hh: e.memset(QP[64:72, hh, 0:1024], 0.0), writes=["QP%d" % hh])
                        P.dma("gpsimd", "ldg", QP[64:72, 2 + hh, :], indm_d, writes=["QP%d" % (2 + hh)])
                    for which in range(2):
                        def evac(tb, b, which=which):
                            for hh in range(2):
                                cp(QP[0:64, which * 2 + hh, tb * 512:(tb + 1) * 512], psum[64 * hh:64 * hh + 64, b, :], [psn(b)], ["QP%d" % (which * 2 + hh)])
                        proj_fm(slot, which * 128, 128, evac)

                    def evac_v(t, b):
                        cp(VA[:, t, 0:2, 0:64], psum[:, b, 0:128].rearrange("p (h d) -> p h d", h=2), [psn(b)], ["VA"])
                    proj_tm(slot, 256, 128, lambda t, c: hnT[:, c, t * 128:(t + 1) * 128], evac_v)
                    for hh in range(2):
                        h = 12 + 2 * j + hh
                        qn, kn = "QP%d" % hh, "QP%d" % (2 + hh)
                        P.op("vector", lambda e, hh=hh: e.tensor_reduce(out=gx[0:64, 0:8], in_=QP[0:64, 2 + hh, :].rearrange("p (n k) -> p n k", k=256),
                                                                      axis=AX.X, op=ALU.add),
                             reads=[kn], writes=["gx"])
                        P.op("vector", lambda e, hh=hh: e.tensor_scalar(out=kmT[0:64, hh * 8:hh * 8 + 8], in0=gx[0:64, 0:8], scalar1=1.0 / 256, scalar2=None, op0=ALU.mult),
                             reads=["gx"], writes=["kmT"])
                        sc64, m64, nm64 = gu[:, 0:64], gu[:, 64:128], gx[:, 64:128]
                        for ti in range(8):
                            i = 8 + ti
                            P.op("tensor", lambda e, i=i, ti=ti, hh=hh: e.matmul(psum[:, 7, ti * 8:ti * 8 + 8], lhsT=QP[0:64, hh, i * 128:(i + 1) * 128],
                                                                               rhs=kmT[0:64, hh * 8:hh * 8 + 8], start=True, stop=True),
                                 reads=[qn, "kmT"], writes=["ps7"], inc=(ti == 7))
                        P.op("vector", lambda e: e.tensor_tensor(out=sc64, in0=psum[:, 7, 0:64], in1=tabs[:, 384:448], op=ALU.add),
                             reads=["ps7", "tabs"], writes=["gu"])
                        for ti in range(8):
                            P.op("vector", lambda e, ti=ti: e.max(out=m64[:, ti * 8:ti * 8 + 8], in_=sc64[:, ti * 8:ti * 8 + 8]), reads=["gu"], writes=["gu_m"])
                        for ti in range(8):
                            P.op("vector", lambda e, ti=ti: e.tensor_scalar(out=nm64[:, ti * 8:ti * 8 + 8], in0=sc64[:, ti * 8:ti * 8 + 8],
                                                                            scalar1=m64[:, ti * 8 + 3:ti * 8 + 4], scalar2=None, op0=ALU.is_ge),
                                 reads=["gu", "gu_m"], writes=["gx"])
                        P.op("vector", lambda e: e.tensor_scalar(out=nm64, in0=nm64, scalar1=-1.0, scalar2=BIG, op0=ALU.add, op1=ALU.mult),
                             reads=["gx"], writes=["gx"])
                        for ti in range(8):
                            bt = 5 + ti // 4
                            P.op("tensor", lambda e, ti=ti, bt=bt: e.transpose(psum[0:8, bt, (ti % 4) * 128:(ti % 4 + 1) * 128], nm64[:, ti * 8:ti * 8 + 8], ident),
                                 reads=["gx", "tabs"], writes=[psn(bt)], inc=(ti % 4 == 3))
                        for half_ in range(2):
                            P.op("scalar", lambda e, half_=half_, hh=hh: e.copy(out=QP[64:72, hh, 1024 + half_ * 512:1536 + half_ * 512], in_=psum[0:8, 5 + half_, :]),
                                 reads=[psn(5 + half_)], writes=[qn])
                        for qb in range(4):
                            tiles = []
                            for kt in range(4 * qb + 4):
                                dl = 512 * qb - 128 * kt
                                if dl >= 256:
                                    mode, strip = "const", None
                                else:
                                    mode, strip = "eb", SB(h, dl + 384, 512)
                                tiles.append((QP[0:72, 2 + hh, kt * 128:(kt + 1) * 128], [kn], VA[:, kt, hh, :], ["VA"], mode, strip, None,
                                              max(0, -dl), 512))

                            def evac_o(ob, qb=qb, hh=hh, j=j):
                                rb = nxt("e", 2)
                                P.op("scalar", lambda e: e.activation(out=rtmp[0:64, rb, :], in_=psum[64:128, ob, :], func=AF.Ln), reads=[psn(ob)], writes=["rtmp%d" % rb])
                                P.op("scalar", lambda e: e.activation(out=rtmp[0:64, rb, :], in_=rtmp[0:64, rb, :], func=AF.Exp, scale=-1.0), reads=["rtmp%d" % rb], writes=["rtmp%d" % rb])
                                P.op("vector", lambda e: e.tensor_tensor(out=omT[64 * hh:64 * hh + 64, 6 + j, qb * 512:(qb + 1) * 512], in0=psum[0:64, ob, :],
                                                                       in1=rtmp[0:64, rb, :], op=ALU.mult),
                                     reads=[psn(ob), "rtmp%d" % rb], writes=["omT"])
                            attend_block(h, QP[0:72, hh, qb * 512:(qb + 1) * 512], 72, tiles, [qn], evac_o)
                        flush_all()

                P.phase = "B"
                for g in (range(2) if "B" in ST else []):
                    P.phase = "B.compress"
                    slot = nxt("w", 2)
                    for kv, cbase in enumerate((C_BKC, C_BVC)):
                        for rep in range(2):
                            load_w(slot, kv * 128 + rep * 64, W[:, cbase + g * 64:cbase + g * 64 + 64], 64, "kc")
                    for kv in range(2):
                        def evac(tb, b, kv=kv):
                            cp(QP[0:64, kv, tb * 512:(tb + 1) * 512], psum[0:64, b, :], [psn(b)], ["QP%d" % kv])
                            if tb == 0:
                                cp(QP[64:128, kv, 0:511], psum[64:128, b, 1:512], [psn(b)], ["QP%d" % kv])
                            else:
                                cp(QP[64:128, kv, tb * 512 - 1:tb * 512 + 511], psum[64:128, b, :], [psn(b)], ["QP%d" % kv])
                        proj_fm(slot, kv * 128, 128, evac)
                    for kv, (w1d, w2d) in enumerate(((w1k, w2k), (w1v, w2v))):
                        slot = nxt("w", 2)
                        w1s = wsl[:, slot, :, :].rearrange("p c n -> p (c n)").rearrange("p (r m) -> p r m", m=256)
                        P.dma("gpsimd", "ldw%d" % slot, w1s, w1d[l].rearrange("(r p) m -> p r m", p=128), writes=["wsl%d" % slot])
                        P.dma("gpsimd", "ldg", w2s, w2d[l].rearrange("(m p) d -> p m d", p=128), reads=["gT"], writes=["w2s"])
                        P.dma("gpsimd", "ldg", pe2s, pe2[l, kv], writes=["pe2s"])
                        for mc in range(2):
                            b = nxt("m", 2, 5)
                            for r in range(16):
                                P.op("tensor", lambda e, b=b, r=r, mc=mc: e.matmul(psum[:, b, 0:1], lhsT=w1s[:, r, mc * 128:(mc + 1) * 128], rhs=pe2s[:, r:r + 1],
                                                                                 start=(r == 0), stop=(r == 15)),
                                     reads=["wsl%d" % slot, "pe2s"], writes=[psn(b)], inc=(r == 15))
                            P.op("vector", lambda e, b=b, mc=mc: e.tensor_copy(out=hb[:, mc:mc + 1], in_=psum[:, b, 0:1]), reads=[psn(b)], writes=["hb"])
                            b = nxt("m", 2, 5)
                            for r in range(16):
                                P.op("tensor", lambda e, b=b, r=r, mc=mc: e.matmul(psum[:, b, 0:127], lhsT=w1s[:, r, mc * 128:(mc + 1) * 128],
                                                                                 rhs=QP[:, kv, 2 * r:2 * r + 16 * 126 + 1:16], start=(r == 0), stop=(r == 15)),
                                     reads=["wsl%d" % slot, "QP%d" % kv], writes=[psn(b)], inc=(r == 15))
                            P.op("scalar", lambda e, b=b, mc=mc: e.activation(out=gx[:, 0:127], in_=psum[:, b, 0:127], func=AF.Identity, bias=hb[:, mc:mc + 1], scale=1.0),
                                 reads=[psn(b), "hb"], writes=["gx"])
                            P.op("vector", lambda e: e.tensor_tensor(out=gu[:, 0:127], in0=gx[:, 0:127], in1=gx[:, 0:127], op=ALU.mult), reads=["gx"], writes=["gu"])
                            P.op("vector", lambda e: e.tensor_scalar(out=gu[:, 0:127], in0=gu[:, 0:127], scalar1=0.044715, scalar2=1.0, op0=ALU.mult, op1=ALU.add),
                                 reads=["gu"], writes=["gu"])
                            P.op("vector", lambda e: e.tensor_tensor(out=gu[:, 0:127], in0=gu[:, 0:127], in1=gx[:, 0:127], op=ALU.mult), reads=["gu", "gx"], writes=["gu"])
                            P.op("scalar", lambda e: e.activation(out=gu[:, 0:127], in_=gu[:, 0:127], func=AF.Sigmoid, scale=2.0 * math.sqrt(2.0 / math.pi)),
                                 reads=["gu"], writes=["gu"])
                            P.op("vector", lambda e, mc=mc: e.tensor_tensor(out=gT[:, mc, 0:127], in0=gu[:, 0:127], in1=gx[:, 0:127], op=ALU.mult),
                                 reads=["gu", "gx"], writes=["gT"])
                        b = nxt("m", 2, 5)
                        if kv == 0:
                            for mc in range(2):
                                P.op("tensor", lambda e, b=b, mc=mc: e.matmul(psum[0:64, b, 0:128], lhsT=w2s[:, mc, :], rhs=gT[:, mc, :], start=(mc == 0), stop=(mc == 1)),
                                     reads=["w2s", "gT"], writes=[psn(b)], inc=(mc == 1))
                            P.op("vector", lambda e, b=b: e.tensor_copy(out=KCMP[0:64, :], in_=psum[0:64, b, 0:128]), reads=[psn(b)], writes=["KCMP"])
                        else:
                            for mc in range(2):
                                P.op("tensor", lambda e, b=b, mc=mc: e.matmul(psum[:, b, 0:64], lhsT=gT[:, mc, :], rhs=w2s[:, mc, :], start=(mc == 0), stop=(mc == 1)),
                                     reads=["w2s", "gT"], writes=[psn(b)], inc=(mc == 1))
                            P.op("vector", lambda e, b=b: e.tensor_copy(out=VCMP[:, 0:64], in_=psum[:, b, 0:64]), reads=[psn(b)], writes=["VCMP"])
                    P.phase = "B.proj"
                    slot = nxt("w", 2)
                    load_w(slot, 0, W[:, C_BQ + g * 192:C_BQ + g * 192 + 192], 192, "bq")
                    load_w(slot, 192, W[:, C_BKS + g * 64:C_BKS + g * 64 + 64], 64, "bks")
                    load_w(slot, 256, W[:, C_BKW + g * 64:C_BKW + g * 64 + 64], 64, "bkw")
                    load_w(slot, 320, W[:, C_BVS + g * 64:C_BVS + g * 64 + 64], 64, "bvs")
                    load_w(slot, 384, W[:, C_BVW + g * 64:C_BVW + g * 64 + 64], 64, "bvw")
                    load_w(slot, 448, W[:, C_BG + g * 9:C_BG + g * 9 + 9], 9, "bg")
                    for hq in range(3):
                        P.op("vector", lambda e, hq=hq: e.memset(QP[64:96, hq, 0:1024], 0.0), reads=["QP0", "QP1"], writes=["QP%d" % hq])
                    P.dma("gpsimd", "ldg", QP[64:96, 3, :], inds_d, writes=["QP3"])

                    def evac1(tb, b):
                        for hh in range(2):
                            cp(QP[0:64, hh, tb * 512:(tb + 1) * 512], psum[64 * hh:64 * hh + 64, b, :], [psn(b)], ["QP%d" % hh])
                    proj_fm(slot, 0, 128, evac1)

                    def evac2(tb, b):
                        for hh in range(2):
                            cp(QP[0:64, 2 + hh, tb * 512:(tb + 1) * 512], psum[64 * hh:64 * hh + 64, b, :], [psn(b)], ["QP%d" % (2 + hh)])
                    proj_fm(slot, 128, 128, evac2)

                    def evac3(tb, b):
                        cp(QP[0:64, 4, tb * 512:(tb + 1) * 512], psum[0:64, b, :], [psn(b)], ["QP4"])
                    proj_fm(slot, 256, 64, evac3)

                    def evac4(tb, b):
                        P.op("scalar", lambda e: e.activation(out=GT[0:9, tb * 512:(tb + 1) * 512], in_=psum[0:9, b, :], func=AF.Sigmoid),
                             reads=[psn(b)], writes=["GT"])
                    proj_fm(slot, 448, 9, evac4)

                    def evac_v(t, b):
                        cp(VA[:, t, 0:2, 0:64], psum[:, b, 0:128].rearrange("p (h d) -> p h d", h=2), [psn(b)], ["VA"])
                    proj_tm(slot, 320, 128, lambda t, c: hnT[:, c, t * 128:(t + 1) * 128], evac_v)

                    def gate_mult(ob, hq, br, qb, clampden, ph):
                        rb = nxt("e", 2)
                        rn = "rtmp%d" % rb
                        if clampden:
                            P.op("vector", lambda e: e.tensor_scalar(out=rtmp[ph:ph + 64, rb, :], in0=psum[64:128, ob, :], scalar1=1e-30, scalar2=None, op0=ALU.max),
                                 reads=[psn(ob)], writes=[rn])
                            P.op("scalar", lambda e: e.activation(out=rtmp[ph:ph + 64, rb, :], in_=rtmp[ph:ph + 64, rb, :], func=AF.Ln), reads=[rn], writes=[rn])
                        else:
                            P.op("scalar", lambda e: e.activation(out=rtmp[ph:ph + 64, rb, :], in_=psum[64:128, ob, :], func=AF.Ln), reads=[psn(ob)], writes=[rn])
                        P.op("scalar", lambda e: e.activation(out=rtmp[ph:ph + 64, rb, :], in_=rtmp[ph:ph + 64, rb, :], func=AF.Exp, scale=-1.0), reads=[rn], writes=[rn])
                        b = nxt("m", 2, 5)
                        jj = hq * 3 + br
                        P.op("tensor", lambda e: e.matmul(psum[0:64, b, :], lhsT=sel_t[0:9, jj * 64:(jj + 1) * 64], rhs=GT[0:9, qb * 512:(qb + 1) * 512], start=True, stop=True),
                             reads=["sel", "GT"], writes=[psn(b)])
                        P.op("vector", lambda e: e.tensor_tensor(out=rtmp[ph:ph + 64, rb, :], in0=rtmp[ph:ph + 64, rb, :], in1=psum[0:64, b, :], op=ALU.mult),
                             reads=[rn, psn(b)], writes=[rn])
                        return rb

                    P.phase = "B.cmp"
                    for qb in range(4):
                        impb = 7 if qb >= 2 else None
                        for hq in range(3):
                            hb_ = 3 * g + hq
                            h = 6 + hb_
                            ch, ph = 3 + hb_ // 2, 64 * (hb_ % 2)
                            pbs = {}

                            def evac_o(ob, qb=qb, hq=hq, ch=ch, ph=ph):
                                rb = gate_mult(ob, hq, 0, qb, True, ph)
                                P.op("vector", lambda e: e.tensor_tensor(out=omT[ph:ph + 64, ch, qb * 512:(qb + 1) * 512], in0=psum[0:64, ob, :],
                                                                       in1=rtmp[ph:ph + 64, rb, :], op=ALU.mult),
                                     reads=[psn(ob), "rtmp%d" % rb], writes=["omT"])
                            ob = nxt("o", 2, 3)
                            sb_ = nxt("s", 3, 0)
                            s_tile(sb_, KCMP[0:64, :], QP[0:64, hq, qb * 512:(qb + 1) * 512], 64, ["KCMP", "QP%d" % hq])
                            pb = softmax_tile(sb_, "eb", h, strips[:, OFF_CMP + qb * 512:OFF_CMP + (qb + 1) * 512])
                            P.op("tensor", lambda e, ob=ob, pb=pb: e.matmul(PS(ob), lhsT=VCMP[:, :], rhs=Pbuf[:, pb, :], start=True, stop=True),
                                 reads=["VCMP", "P%d" % pb], writes=[psn(ob)])
                            if impb is not None:
                                for ti in range(4):
                                    P.op("tensor", lambda e, pb=pb, ti=ti, hq=hq: e.matmul(psum[:, 7, ti * 128 + hq * 33:ti * 128 + hq * 33 + 33],
                                                                                         lhsT=Pbuf[:, pb, ti * 128:(ti + 1) * 128], rhs=ov_t[:, 0:33], start=True, stop=True),
                                         reads=["P%d" % pb, "ov"], writes=["ps7"], inc=(ti == 3))
                            flush_evac()
                            deferred.append(lambda evac_o=evac_o, ob=ob: evac_o(ob))
                        if impb is not None:
                            import os as _os
                            for ti in [int(c) for c in _os.environ.get("TI_ORDER", "0123")]:
                                i = 4 * qb + ti
                                base = ti * 128
                                P.op("vector", lambda e, base=base: e.reciprocal(out=rec3[:, 0:3], in_=psum[:, 7, base + 32:base + 99:33]), reads=["ps7"], writes=["rec3"])
                                P.op("vector", lambda e, base=base: e.tensor_scalar(out=scb[:, :], in0=psum[:, 7, base:base + 32], scalar1=rec3[:, 0:1], scalar2=None, op0=ALU.mult),
                                     reads=["ps7", "rec3"], writes=["scb"])
                                for hq in (1, 2):
                                    P.op("vector", lambda e, base=base, hq=hq: e.scalar_tensor_tensor(out=scb[:, :], in0=psum[:, 7, base + hq * 33:base + hq * 33 + 32],
                                                                                                     scalar=rec3[:, hq:hq + 1], in1=scb[:, :], op0=ALU.mult, op1=ALU.add),
                                         reads=["ps7", "rec3", "scb"], writes=["scb"])
                                P.op("vector", lambda e, i=i: e.tensor_tensor(out=scb[:, :], in0=scb[:, :], in1=tabs[:, 128 + (i - 8) * 32:128 + (i - 8) * 32 + 32], op=ALU.add),
                                     reads=["scb", "tabs"], writes=["scb"])
                                P.op("vector", lambda e: e.max(out=m8[:, 0:8], in_=scb[:, :]), reads=["scb"], writes=["m8"])
                                P.op("vector", lambda e: e.match_replace(out=sctmp[:, :], in_to_replace=m8[:, 0:8], in_values=scb[:, :], imm_value=-1e30),
                                     reads=["scb", "m8"], writes=["sctmp"])
                                P.op("vector", lambda e: e.max(out=m8[:, 8:16], in_=sctmp[:, :]), reads=["sctmp"], writes=["m8"])
                                P.op("vector", lambda e: e.tensor_scalar(out=nmk[:, :], in0=scb[:, :], scalar1=m8[:, 15:16], scalar2=None, op0=ALU.is_ge),
                                     reads=["scb", "m8"], writes=["nmk"])
                                P.op("vector", lambda e: e.tensor_scalar(out=nmk[:, :], in0=nmk[:, :], scalar1=-1.0, scalar2=BIG, op0=ALU.add, op1=ALU.mult),
                                     reads=["nmk"], writes=["nmk"])
                                b2 = nxt("m", 2, 5)
                                P.op("tensor", lambda e, b2=b2: e.transpose(psum[0:32, b2, 0:128], nmk[:, :], ident), reads=["nmk", "tabs"], writes=[psn(b2)])
                                for hq in range(3):
                                    cp(QP[64:96, hq, i * 128:(i + 1) * 128], psum[0:32, b2, 0:128], [psn(b2)], ["QP%d" % hq], eng="scalar")
                    flush_evac()
                    P.phase = "B.slcwin"
                    for hq in range(3):
                        hb_ = 3 * g + hq
                        h = 6 + hb_
                        ch, ph = 3 + hb_ // 2, 64 * (hb_ % 2)
                        qn = "QP%d" % hq
                        for qb in range(4):
                            for br in (1, 2):
                                tiles = []
                                if br == 1:
                                    for kt in range(4 * qb + 4):
                                        dl = 512 * qb - 128 * kt
                                        if dl >= 256:
                                            mode, strip = "const", None
                                        else:
                                            mode, strip = "eb", SB(h, dl + 384, 512)
                                        tiles.append((QP[0:96, 3, kt * 128:(kt + 1) * 128], ["QP3"], VA[:, kt, 0, :], ["VA"], mode, strip, None,
                                                      max(0, -dl), 512))
                                    kd = 96
                                else:
                                    for kt in range(max(0, 4 * qb - 4), 4 * qb + 4):
                                        dl = 512 * qb - 128 * kt
                                        if dl >= 256:
                                            mode, strip, strip2 = "cb", strips[:, OFF_UM + dl:OFF_UM + dl + 512], None
                                        elif dl == 128:
                                            mode, strip, strip2 = "eb", strips[:, OFF_UM + dl:OFF_UM + dl + 512], SB(h, dl + 384, 512)
                                        else:
                                            mode, strip, strip2 = "eb", SB(h, dl + 384, 512), None
                                        c0_ = max(0, -dl)
                                        c1_ = 512 if dl <= 128 else 640 - dl
                                        tiles.append((QP[0:64, 4, kt * 128:(kt + 1) * 128], ["QP4"], VA[:, kt, 1, :], ["VA"], mode, strip, strip2, c0_, c1_))
                                    tiles.sort(key=lambda tl: 0 if (tl[7] == 0 and tl[8] == 512) else 1)
                                    kd = 64

                                def evac_o(ob, qb=qb, hq=hq, ch=ch, ph=ph, br=br):
                                    rb = gate_mult(ob, hq, br, qb, False, ph)
                                    rn = "rtmp%d" % rb
                                    P.op("vector", lambda e: e.tensor_tensor(out=rtmp[ph:ph + 64, rb, :], in0=psum[0:64, ob, :], in1=rtmp[ph:ph + 64, rb, :], op=ALU.mult),
                                         reads=[psn(ob), rn], writes=[rn])
                                    P.op("vector", lambda e: e.tensor_tensor(out=omT[ph:ph + 64, ch, qb * 512:(qb + 1) * 512], in0=omT[ph:ph + 64, ch, qb * 512:(qb + 1) * 512],
                                                                           in1=rtmp[ph:ph + 64, rb, :], op=ALU.add),
                                         reads=["omT", rn], writes=["omT"])
                                attend_block(h, QP[0:kd, hq, qb * 512:(qb + 1) * 512], kd, tiles, [qn], evac_o)
                    flush_all()

                if debug and sq == 0 and l == 0:
                    P.dma("sync", "dbg", dbg["omT"], omT, reads=["omT"])

                P.phase = "merge"
                Gm = dent.rearrange("p (b n) -> p b n", b=4)
                for fc in (range(8) if "merge" in ST else []):
                    slot = nxt("w", 2)
                    load_w(slot, 0, w_br[l][:, fc * 128:(fc + 1) * 128], 128, "wb")
                    for gi in range(3):
                        load_w(slot, 128 + gi * 128, W[:, C_MG + gi * 1024 + fc * 128:C_MG + gi * 1024 + fc * 128 + 128], 128, "mg")
                    for tb in range(4):
                        tsl = slice(tb * 512, (tb + 1) * 512)
                        bs = [5, 6, 7]
                        gb = [nxt("s", 3, 0) for _ in range(3)]
                        rng = ((0, 3), (3, 6), (6, 8))
                        for gi in range(3):
                            for c in range(8):
                                P.op("tensor", lambda e, gi=gi, c=c: e.matmul(PS(gb[gi]), lhsT=wsl[:, slot, c, 128 + gi * 128:256 + gi * 128], rhs=hnT[:, c, tsl],
                                                                            start=(c == 0), stop=(c == 7)),
                                     reads=["wsl%d" % slot, "hnT"], writes=[psn(gb[gi])], inc=(c == 7))
                            P.op("scalar", lambda e, gi=gi: e.activation(out=Gm[:, gi, :], in_=PS(gb[gi]), func=AF.Sigmoid), reads=[psn(gb[gi])], writes=["Gm%d" % gi])
                            k0, k1 = rng[gi]
                            for k in range(k0, k1):
                                P.op("tensor", lambda e, gi=gi, k=k, k0=k0, k1=k1: e.matmul(PS(bs[gi]), lhsT=wsl[:, slot, k, 0:128], rhs=omT[:, k, tsl],
                                                                                          start=(k == k0), stop=(k == k1 - 1)),
                                     reads=["wsl%d" % slot, "omT"], writes=[psn(bs[gi])], inc=(k == k1 - 1))
                        P.op("vector", lambda e: e.tensor_tensor(out=Gm[:, 0, :], in0=Gm[:, 0, :], in1=PS(bs[0]), op=ALU.mult), reads=["Gm0", psn(bs[0])], writes=["Gm0"])
                        P.op("vector", lambda e: e.tensor_tensor(out=Gm[:, 1, :], in0=Gm[:, 1, :], in1=PS(bs[1]), op=ALU.mult), reads=["Gm1", psn(bs[1])], writes=["Gm1"])
                        P.op("vector", lambda e: e.tensor_tensor(out=Gm[:, 2, :], in0=Gm[:, 2, :], in1=PS(bs[2]), op=ALU.mult), reads=["Gm2", psn(bs[2])], writes=["Gm2"])
                        P.op("vector", lambda e: e.tensor_tensor(out=Gm[:, 0, :], in0=Gm[:, 0, :], in1=Gm[:, 1, :], op=ALU.add), reads=["Gm0", "Gm1"], writes=["Gm0"])
                        P.op("vector", lambda e, fc=fc, tsl=tsl: e.tensor_tensor(out=mergedT[:, fc, tsl], in0=Gm[:, 0, :], in1=Gm[:, 2, :], op=ALU.add),
                             reads=["Gm0", "Gm2", "QP0", "QP1", "QP2", "QP3", "QP4", "VA"], writes=["mergedT"])

                P.phase = "outproj"
                wo = wsl_raw[:].rearrange("p (c n) -> p c n", c=8)
                P.dma("gpsimd", "ldw0", wo, w_out[l].rearrange("(c p) n -> p c n", p=128), reads=["wsl1"], writes=["wsl0", "wsl1"])
                prev_post = None
                for t in (range(16) if "outproj" in ST else []):
                    xb = nxt("x", 2)
                    xn = "xt%d" % xb
                    P.dma("sync", "ldx%d" % xb, xt[:, xb, :], src_x(t), reads=["scr"], writes=[xn])
                    for hf in range(2):
                        b = nxt("f", 8, 0)
                        for c in range(8):
                            P.op("tensor", lambda e, b=b, c=c, hf=hf, t=t: e.matmul(PS(b), lhsT=mergedT[:, c, t * 128:(t + 1) * 128], rhs=wo[:, c, hf * 512:(hf + 1) * 512],
                                                                                  start=(c == 0), stop=(c == 7)),
                                 reads=["mergedT", "wsl0", "wsl1"], writes=[psn(b)], inc=(c == 7))
                        P.op("vector", lambda e, b=b, hf=hf: e.tensor_tensor(out=xt[:, xb, hf * 512:(hf + 1) * 512], in0=xt[:, xb, hf * 512:(hf + 1) * 512], in1=PS(b), op=ALU.add),
                             reads=[psn(b), xn], writes=[xn])
                    P.dma("sync", "stx", scr[t * 128:(t + 1) * 128, :], xt[:, xb, :], reads=[xn], writes=["scr_t%d" % t])
                    if debug and sq == 0 and l == 0:
                        P.dma("sync", "dbg", dbg["x1"][t * 128:(t + 1) * 128, :], xt[:, xb, :], reads=[xn])
                    norm_tile(xb, t, (l * 2 + 1) * 8, post=False)
                    if prev_post is not None:
                        norm_post(*prev_post)
                    prev_post = (xb, t, (l * 2 + 1) * 8)
                if prev_post is not None:
                    norm_post(*prev_post)
                P.barrier()

                P.phase = "ffn"
                P.dma("sync", "ldc", cw_t[:], cwt[l], writes=["cw"])
                P.op("vector", lambda e: e.memset(halo[:], 0.0), writes=["halo%d" % q_ for q_ in range(44)])
                wsl4 = wsl_raw[:].rearrange("p (s c n) -> p s c n", s=4, c=8)
                for hf in (range(2) if "ffn" in ST else []):
                    P.phase = "ffn.up"
                    for j in range(22):
                        s4 = nxt("w4", 4)
                        wn = "wf%d" % s4
                        P.dma("gpsimd", "ldf%d" % s4, wsl4[:, s4, :, 0:128], w_up[l][:, j * 128:(j + 1) * 128].rearrange("(c p) n -> p c n", p=128),
                              writes=[wn, "wsl%d" % (s4 // 2)])
                        P.dma("gpsimd", "ldf%d" % s4, wsl4[:, s4, :, 128:256], w_up[l][:, DFF + j * 128:DFF + (j + 1) * 128].rearrange("(c p) n -> p c n", p=128),
                              writes=[wn])
                        par = j % 2
                        for au in range(2):
                            rbuf = rbufs[au][par]
                            obuf = obufs[au][par]
                            jj = au * 22 + j
                            rn, on = "r%d_%d" % (au, par), "o%d_%d" % (au, par)
                            P.op("scalar", lambda e, rbuf=rbuf, jj=jj: e.copy(out=rbuf[:, 0:2], in_=halo[:, jj, :]), reads=["halo%d" % jj], writes=[rn])
                            for tbl in range(2):
                                tb = hf * 2 + tbl
                                b = nxt("f", 8, 0)
                                for c in range(8):
                                    P.op("tensor", lambda e, b=b, c=c, tb=tb, au=au: e.matmul(PS(b), lhsT=wsl4[:, s4, c, au * 128:(au + 1) * 128], rhs=hnT[:, c, tb * 512:(tb + 1) * 512],
                                                                                            start=(c == 0), stop=(c == 7)),
                                         reads=[wn, "hnT"], writes=[psn(b)], inc=(c == 7))
                                P.op("scalar", lambda e, b=b, rbuf=rbuf, tbl=tbl: e.copy(out=rbuf[:, 2 + tbl * 512:2 + (tbl + 1) * 512], in_=PS(b)), reads=[psn(b)], writes=[rn])
                            cwb = jj * 4
                            P.op("scalar", lambda e, rbuf=rbuf, obuf=obuf, cwb=cwb: e.activation(out=obuf[:, :], in_=rbuf[:, 2:1026], func=AF.Identity,
                                                                                               bias=cw_t[:, cwb + 3:cwb + 4], scale=cw_t[:, cwb + 2:cwb + 3]),
                                 reads=[rn, "cw"], writes=[on])
                            P.op("vector", lambda e, rbuf=rbuf, obuf=obuf, cwb=cwb: e.scalar_tensor_tensor(out=obuf[:, :], in0=rbuf[:, 1:1025], scalar=cw_t[:, cwb + 1:cwb + 2],
                                                                                                         in1=obuf[:, :], op0=ALU.mult, op1=ALU.add),
                                 reads=[rn, on, "cw"], writes=[on])
                            P.op("vector", lambda e, rbuf=rbuf, obuf=obuf, cwb=cwb: e.scalar_tensor_tensor(out=obuf[:, :], in0=rbuf[:, 0:1024], scalar=cw_t[:, cwb:cwb + 1],
                                                                                                         in1=obuf[:, :], op0=ALU.mult, op1=ALU.add),
                                 reads=[rn, on, "cw"], writes=[on])
                            if hf == 0:
                                P.op("scalar", lambda e, rbuf=rbuf, jj=jj: e.copy(out=halo[:, jj, :], in_=rbuf[:, 1024:1026]), reads=[rn], writes=["halo%d" % jj])
                        oa_, ou_ = obufs[0][par], obufs[1][par]
                        P.op("scalar", lambda e: e.activation(out=sg[:, :], in_=oa_[:, :], func=AF.Silu), reads=["o0_%d" % par], writes=["sg"])
                        P.op("vector", lambda e, j=j: e.tensor_tensor(out=gTf[:, j, :], in0=sg[:, :], in1=ou_[:, :], op=ALU.mult), reads=["sg", "o1_%d" % par], writes=["gTf%d" % j])
                    P.phase = "ffn.dn"
                    xt4 = xt[:].rearrange("p b (h n) -> p (b h) n", h=2)
                    for chh in range(2):
                        xbufs = {}
                        for tt in range(4):
                            t = hf * 8 + tt
                            xn = "xq%d" % tt
                            P.dma("sync", "ldx4_%d" % tt, xt4[:, tt, :], scr[t * 128:(t + 1) * 128, chh * 512:(chh + 1) * 512],
                                  reads=["scr_t%d" % t], writes=[xn, "xt0", "xt1"])
                            xbufs[tt] = (tt, xn)
                        for j in range(22):
                            sd = nxt("wd", 6)
                            dn = "wd%d" % sd
                            P.dma("gpsimd", "ldd%d" % sd, wdsl[:, sd, :], w_dn[l][j * 128:(j + 1) * 128, chh * 512:(chh + 1) * 512], writes=[dn])
                            for tt in range(8):
                                P.op("tensor", lambda e, j=j, tt=tt: e.matmul(PS(tt), lhsT=gTf[:, j, tt * 128:(tt + 1) * 128], rhs=wdsl[:, sd, :], start=(j == 0), stop=(j == 21)),
                                     reads=["gTf%d" % j, dn], writes=[psn(tt)], inc=(tt == 7))
                        def dn_add(tt):
                            xb4, xn = xbufs[tt]
                            P.op("vector", lambda e: e.tensor_tensor(out=xt4[:, xb4, :], in0=xt4[:, xb4, :], in1=PS(tt), op=ALU.add), reads=[psn(tt), xn], writes=[xn])

                        def dn_store(tt):
                            t = hf * 8 + tt
                            xb4, xn = xbufs[tt]
                            P.dma("sync", "stx", scr[t * 128:(t + 1) * 128, chh * 512:(chh + 1) * 512], xt4[:, xb4, :], reads=[xn], writes=["scr_t%d" % t, "scr"])

                        for tt in range(4):
                            dn_add(tt)
                        for tt in range(4):
                            dn_store(tt)
                        for tt in range(4, 8):
                            t2 = hf * 8 + tt
                            xb4, xn = xbufs[tt - 4]
                            P.dma("sync", "ldx4_%d" % xb4, xt4[:, xb4, :], scr[t2 * 128:(t2 + 1) * 128, chh * 512:(chh + 1) * 512], reads=["scr_t%d" % t2], writes=[xn])
                            xbufs[tt] = (xb4, xn)
                        for tt in range(4, 8):
                            dn_add(tt)
                            dn_store(tt)
                P.barrier()
            P.phase = "final"
            for t in (range(16) if "final" in ST else []):
                xb = nxt("x", 2)
                xn = "xt%d" % xb
                P.dma("sync", "ldx%d" % xb, xt[:, xb, :], scr[t * 128:(t + 1) * 128, :], reads=["scr", "scr_t%d" % t], writes=[xn])
                norm_tile(xb, t, None)
                P.op("vector", lambda e, xb=xb: e.tensor_tensor(out=xt[:, xb, :], in0=xt[:, xb, :], in1=gfin_t[:, :], op=ALU.mult), reads=[xn, "gfin"], writes=[xn])
                final_toks.append(P.dma("sync", "sto", out_d[sq, t * 128:(t + 1) * 128, :], xt[:, xb, :], reads=[xn], writes=["scr"]))
            P.barrier()
        P.finish(final_toks)
    return nc


_CACHE = {}


def _prep_inputs(inputs):
    f = lambda a: np.ascontiguousarray(np.asarray(a, np.float32))
    hc = _host_consts(inputs["rel_bias"])
    gains = np.zeros((128, 32), np.float32)
    for l in range(DEPTH):
        gains[:, (l * 2 + 0) * 8:(l * 2 + 0) * 8 + 8] = f(inputs["norm_mix"])[l].reshape(8, 128).T
        gains[:, (l * 2 + 1) * 8:(l * 2 + 1) * 8 + 8] = f(inputs["norm_ffn"])[l].reshape(8, 128).T
    cwt = np.zeros((DEPTH, 128, 176), np.float32)
    for l in range(DEPTH):
        cw = f(inputs["conv_w"])[l]
        cb = f(inputs["conv_b"])[l]
        for k in range(3):
            cwt[l, :, k::4] = cw[k].reshape(44, 128).T
        cwt[l, :, 3::4] = cb.reshape(44, 128).T
    pe2 = np.zeros((DEPTH, 2, 128, 16), np.float32)
    for l in range(DEPTH):
        for kv, nm in enumerate(("cmp_pe_k", "cmp_pe_v")):
            pe = f(inputs[nm])[l]
            pe2[l, kv] = pe.reshape(16, 2, 64).transpose(1, 2, 0).reshape(128, 16)
    shared = dict(
        w_in=f(inputs["w_in"]), cmp_w1_k=f(inputs["cmp_w1_k"]), cmp_w2_k=f(inputs["cmp_w2_k"]),
        cmp_w1_v=f(inputs["cmp_w1_v"]), cmp_w2_v=f(inputs["cmp_w2_v"]), pe2=pe2,
        w_branch=f(inputs["w_branch"]), w_out=f(inputs["w_out"]), w_up=f(inputs["w_up"]), w_down=f(inputs["w_down"]),
        gains_in=gains, cwt=cwt, norm_final=f(inputs["norm_final"]),
        strips_in=hc["strips"], tabs_in=hc["tabs"], ov_in=hc["ov"], inds=hc["inds"], indm=hc["indm"], sel_in=hc["sel"],
    )
    x = f(inputs["x"])
    return [dict(shared, x=np.ascontiguousarray(x[c * SEQ_PER_CORE:(c + 1) * SEQ_PER_CORE])) for c in range(NCORE)]


def kernel(**inputs):
    if "nc" not in _CACHE:
        _CACHE["nc"] = build_program()
    nc = _CACHE["nc"]
    in_maps = _prep_inputs(inputs)
    res = run_bass_kernel_spmd(nc, in_maps, core_ids=list(range(NCORE)))
    out = np.concatenate([np.asarray(r["out"], np.float32) for r in res.results], axis=0)
    return out
```

```python
import math
from contextlib import ExitStack
import numpy as np
import concourse.bass as bass
import concourse.mybir as mybir
from concourse.bass_utils import run_bass_kernel_spmd

F32 = mybir.dt.float32
BF16 = mybir.dt.bfloat16
AF = mybir.ActivationFunctionType
ALU = mybir.AluOpType
AX = mybir.AxisListType

S = 2048
D = 1024
NCORE = 8
SEQ_PER_CORE = 2
DEPTH = 2
INC = 6162
DFF = 2816
BIG = 240000.0
NEGB = -30000.0
C_AQ, C_AK, C_AV = 0, 384, 768
C_BQ, C_BKC, C_BVC, C_BKS, C_BVS, C_BKW, C_BVW, C_BG = 1152, 1536, 1664, 1792, 1920, 2048, 2176, 2304
C_CQ, C_CK, C_CV, C_MG = 2322, 2578, 2834, 3090
A_DIL = (1, 4, 16)
NSTRIP = 10 * 1024 + 3 * 512 + 1024 + 2048
OFF_B = lambda h: (h - 6) * 1024
OFF_A = lambda g: 10240 + g * 512
OFF_UM = 10240 + 1536
OFF_CMP = OFF_UM + 1024

ENGS = ("tensor", "vector", "scalar", "gpsimd", "sync")


class Unit:
    __slots__ = ("w", "r")

    def __init__(self):
        self.w = None
        self.r = {}


class _Rec:
    def __init__(self):
        self.call = None

    def __getattr__(self, name):
        def f(*a, **kw):
            self.call = (name, a, kw)
            return self
        return f


class Prog:
    def __init__(self, nc, stack):
        self.nc = nc
        self.stack = stack
        self.q = {e: [] for e in ENGS}
        self.sems = {}
        self.cnt = {}
        self.seen = {e: {} for e in ENGS}
        for e in ENGS:
            self._sem("E_" + e)
        self.units = {}
        self.waitmax = {}
        self.limit = None
        self.nrec = 0
        self.log = []
        self.phase = "setup"
        self.pe_phase = []

    def _sem(self, key):
        if key not in self.sems:
            self.sems[key] = self.stack.enter_context(self.nc.semaphore(key))
            self.cnt[key] = 0
        return self.sems[key]

    def unit(self, key):
        u = self.units.get(key)
        if u is None:
            u = Unit()
            self.units[key] = u
        return u

    def _deps(self, eng, reads, writes, same_war=False, own_sem=None):
        need = {}

        def add(tok, kind):
            if tok is None:
                return
            k, v = tok
            if kind == "waw" and own_sem is not None and k == own_sem:
                return
            if k == "E_" + eng:
                if eng == "tensor":
                    return
                if kind == "war" and not same_war:
                    return
            if v > need.get(k, 0):
                need[k] = v

        for u in reads:
            add(u.w, "raw")
        for u in writes:
            add(u.w, "waw")
            for k, v in u.r.items():
                add((k, v), "war")
        out = []
        seen = self.seen[eng]
        for k in list(need):
            if not k.startswith("E_"):
                need[k] = self.cnt[k]
        for k, v in need.items():
            if seen.get(k, 0) < v:
                seen[k] = v
                out.append((k, v))
                if not k.startswith("E_") and v > self.waitmax.get(k, 0):
                    self.waitmax[k] = v
        return out

    def _mark(self, tok, reads, writes):
        for u in reads:
            if u.r.get(tok[0], 0) < tok[1]:
                u.r[tok[0]] = tok[1]
        for u in writes:
            u.w = tok
            u.r = {}

    def op(self, eng, fn, reads=(), writes=(), inc=True):
        self.nrec += 1
        if self.limit is not None and self.nrec > self.limit:
            return None
        if "hnT" in reads:
            reads = list(reads) + ["hnTd"]
        reads = [self.unit(u) for u in reads]
        writes = [self.unit(u) for u in writes]
        waits = self._deps(eng, reads, writes, same_war=(eng != "tensor"))
        k = "E_" + eng
        if inc:
            self.cnt[k] += 1
            tok = (k, self.cnt[k])
        else:
            tok = (k, self.cnt[k] + 1)
        rec = _Rec()
        fn(rec)
        name, a, kw = rec.call
        if eng == "tensor":
            self.pe_phase.append(self.phase)
        self.q[eng].append((waits, (lambda e: getattr(e, name)(*a, **kw)), (k, 1) if inc else None))
        self._mark(tok, reads, writes)
        return tok

    def dma(self, eng, semkey, out, in_, reads=(), writes=()):
        self.nrec += 1
        if self.limit is not None and self.nrec > self.limit:
            return None
        reads = [self.unit(u) for u in reads]
        writes = [self.unit(u) for u in writes]
        self._sem(semkey)
        waits = self._deps(eng, reads, writes, same_war=True, own_sem=semkey)
        wm = self.waitmax.get(semkey, 0)
        if wm > self.seen[eng].get(semkey, 0):
            self.seen[eng][semkey] = wm
            waits.append((semkey, wm))
        self.cnt[semkey] += 16
        tok = (semkey, self.cnt[semkey])
        self.q[eng].append((waits, lambda e: e.dma_start(out=out, in_=in_), (semkey, 16)))
        self._mark(tok, reads, writes)
        return tok

    def barrier(self):
        cur = [(k, v) for k, v in self.cnt.items() if v > 0]
        for e in ENGS:
            waits = []
            for k, v in cur:
                if k == "E_" + e:
                    continue
                if self.seen[e].get(k, 0) < v:
                    self.seen[e][k] = v
                    waits.append((k, v))
                    if not k.startswith("E_") and v > self.waitmax.get(k, 0):
                        self.waitmax[k] = v
            if waits:
                self.q[e].append((waits, None, None))

    def finish(self, final_tokens):
        prog = self
        fin = [t for t in final_tokens if t is not None]
        for k, v in self.cnt.items():
            if v > 0 and not k.startswith("E_tensor"):
                fin.append((k, v))

        def emit(engname):
            def body(e):
                for waits, fn, inc in prog.q[engname]:
                    for (k, v) in waits:
                        e.wait_ge(prog.sems[k], v)
                    if fn is None:
                        continue
                    ins = fn(e)
                    if inc is not None:
                        ins.then_inc(prog.sems[inc[0]], inc[1])
                if engname == "sync":
                    for (k, v) in fin:
                        e.wait_ge(prog.sems[k], v)
            return body

        with self.nc.Block() as block:
            block.tensor(emit("tensor"))
            block.vector(emit("vector"))
            block.scalar(emit("scalar"))
            block.gpsimd(emit("gpsimd"))
            block.sync(emit("sync"))


def _bucket(d):
    n = np.maximum(d, 0)
    nf = np.maximum(n, 1).astype(np.float32)
    large = 16 + (np.log(nf / np.float32(16)) / np.float32(math.log(8.0)) * np.float32(16)).astype(np.int32)
    return np.where(n < 16, n, np.minimum(large, 31)).astype(np.int64)


def _host_consts(rel_bias):
    rb = np.asarray(rel_bias, np.float32)
    ki = np.arange(128)[:, None]
    strips = np.zeros((128, NSTRIP), np.float32)
    u = np.arange(1024)[None, :]
    d = u - 384 - ki
    bk = _bucket(d)
    for h in range(6, 16):
        strips[:, OFF_B(h):OFF_B(h) + 1024] = np.where(d >= 0, rb[bk, h], np.float32(NEGB))
    qi = np.arange(128)[None, :]
    for g, dil in enumerate(A_DIL):
        for hh in range(2):
            h = 2 * g + hh
            ds = qi - ki + 128
            prev = np.where(ds <= 128, rb[_bucket(ds * dil), h], np.float32(NEGB))
            ds2 = qi - ki
            diag = np.where(ds2 >= 0, rb[_bucket(ds2 * dil), h], np.float32(NEGB))
            o = OFF_A(g) + hh * 256
            strips[:, o:o + 128] = prev
            strips[:, o + 128:o + 256] = diag
    w = np.arange(1024)[None, :]
    strips[:, OFF_UM:OFF_UM + 1024] = np.where(w - ki <= 511, np.float32(0), np.float32(NEGB))
    q = np.arange(2048)[None, :]
    ok = (16 * ki + 31 <= q) & (ki < 127)
    strips[:, OFF_CMP:OFF_CMP + 2048] = np.where(ok, np.float32(0), np.float32(NEGB))
    tabs = np.zeros((128, 128 + 256 + 64 + 16), np.float32)
    tabs[:, 0:128] = np.eye(128, dtype=np.float32)
    p = np.arange(128)[:, None]
    for i in range(8, 16):
        t = 128 * i + p
        jt = t // 64
        j = np.arange(32)[None, :]
        forced = (j == 0) | (j == jt) | (j == jt - 1)
        fb = np.where(j <= jt, np.where(forced, np.float32(1e4), np.float32(0)), np.float32(-1e30))
        tabs[:, 128 + (i - 8) * 32:128 + (i - 8) * 32 + 32] = fb
        nb = t // 256
        n = np.arange(8)[None, :]
        pm = np.where(n < nb, np.float32(0), np.where(n == nb, np.float32(1e4), np.float32(-1e30)))
        tabs[:, 384 + (i - 8) * 8:384 + (i - 8) * 8 + 8] = pm
    tabs[:, 448:464] = rb[31][None, :]
    ov = np.zeros((128, 64), np.float32)
    c = np.arange(128)[:, None]
    j = np.arange(32)[None, :]
    ov[:, 0:32] = ((16 * c < 64 * j + 64) & (16 * c + 32 > 64 * j) & (c < 127)).astype(np.float32)
    ov[:127, 32] = 1.0
    k = np.arange(2048)[None, :]
    inds = (k // 64 == np.arange(32)[:, None]).astype(np.float32)
    indm = (k // 256 == np.arange(8)[:, None]).astype(np.float32)
    sel = np.zeros((18, 18 * 64), np.float32)
    for jj in range(18):
        sel[jj, jj * 64:(jj + 1) * 64] = 1.0
    return dict(strips=strips, tabs=tabs, ov=ov, inds=inds, indm=indm, sel=sel)


def build_program(debug=False, stages=None, nseq=SEQ_PER_CORE, depth=DEPTH, limit=None):
    ST = stages if stages is not None else {"A", "C", "B", "merge", "outproj", "ffn", "final"}
    nc = bass.Bass("TRN2", target_bir_lowering=False)
    dt_in = lambda name, shape: nc.dram_tensor(name, shape, F32, kind="ExternalInput").ap()
    x_d = dt_in("x", [SEQ_PER_CORE, S, D])
    w_in = dt_in("w_in", [DEPTH, D, INC])
    w1k = dt_in("cmp_w1_k", [DEPTH, 2048, 256])
    w2k = dt_in("cmp_w2_k", [DEPTH, 256, 64])
    w1v = dt_in("cmp_w1_v", [DEPTH, 2048, 256])
    w2v = dt_in("cmp_w2_v", [DEPTH, 256, 64])
    pe2 = dt_in("pe2", [DEPTH, 2, 128, 16])
    w_br = dt_in("w_branch", [DEPTH, D, D])
    w_out = dt_in("w_out", [DEPTH, D, D])
    w_up = dt_in("w_up", [DEPTH, D, 2 * DFF])
    w_dn = dt_in("w_down", [DEPTH, DFF, D])
    gains = dt_in("gains_in", [128, 32])
    cwt = dt_in("cwt", [DEPTH, 128, 176])
    gfin = dt_in("norm_final", [D])
    strips_d = dt_in("strips_in", [128, NSTRIP])
    tabs_d = dt_in("tabs_in", [128, 464])
    ov_d = dt_in("ov_in", [128, 64])
    inds_d = dt_in("inds", [32, 2048])
    indm_d = dt_in("indm", [8, 2048])
    sel_d = dt_in("sel_in", [18, 18 * 64])
    out_d = nc.dram_tensor("out", [SEQ_PER_CORE, S, D], F32, kind="ExternalOutput").ap()
    scr = nc.dram_tensor("xscr", [S, D], F32, kind="Internal").ap()
    dbg = {}
    if debug:
        dbg["omT"] = nc.dram_tensor("dbg_omT", [128, 8, S], BF16, kind="ExternalOutput").ap()
        dbg["x1"] = nc.dram_tensor("dbg_x1", [S, D], F32, kind="ExternalOutput").ap()

    with ExitStack() as st:
        P = Prog(nc, st)
        P.limit = limit
        build_program.P = P
        sb = lambda name, shape, dt: st.enter_context(nc.sbuf_tensor(name, shape, dt))
        hnT = sb("hnT", [128, 8, S], BF16)
        wsl_raw = sb("wsl", [128, 8192], BF16)
        wsl = wsl_raw[:].rearrange("p (s c n) -> p s c n", s=2, c=8)
        strips = sb("strips", [128, NSTRIP], BF16)
        tabs = sb("tabs", [128, 464], F32)
        gains_t = sb("gains", [128, 32], F32)
        cw_t = sb("cw", [128, 176], F32)
        gfin_t = sb("gfin", [128, D], F32)
        ov_t = sb("ov", [128, 64], BF16)
        sel_t = sb("sel", [18, 18 * 64], F32)
        xt = sb("xt", [128, 2, D], F32)
        small = sb("small", [128, 64], F32)
        halo = sb("halo", [128, 44, 2], F32)
        ones_bf = sb("ones_bf", [128, 64], BF16)
        ARENA = 44 * 1024
        arena = sb("arena", [128, ARENA], BF16)
        psum = st.enter_context(nc.psum_tensor("ps", [128, 8, 512], F32))

        ident = tabs[:, 0:128]
        omT = arena[:, 0:16384].rearrange("p (c t) -> p c t", c=8)
        QP = arena[:, 16384:26624].rearrange("p (i t) -> p i t", i=5)
        VA = arena[:, 26624:32768].rearrange("p (t h e) -> p t h e", t=16, h=3)
        mergedT = arena[:, 16384:32768].rearrange("p (c t) -> p c t", c=8)
        dent = arena[:, 32768:36864].bitcast(F32)
        Ebuf = arena[:, 36864:38912].bitcast(F32).rearrange("p (b n) -> p b n", b=2)
        Pbuf = arena[:, 38912:40448].rearrange("p (b n) -> p b n", b=3)
        rtmp = arena[:, 40448:42496].bitcast(F32).rearrange("p (b n) -> p b n", b=2)
        misc = arena[:, 42496:44032].bitcast(F32)
        hb = misc[:, 0:4]
        gx = misc[:, 8:136]
        gu = misc[:, 136:264]
        scb = misc[:, 264:296]
        sctmp = misc[:, 296:328]
        m8 = misc[:, 328:344]
        nmk = misc[:, 344:376]
        rec3 = misc[:, 376:380]
        KCMP = misc[:, 384:448].bitcast(BF16)
        VCMP = misc[:, 448:512].bitcast(BF16)
        gT = misc[:, 512:640].bitcast(BF16).rearrange("p (m c) -> p m c", m=2)
        w2s = misc[:, 640:704].bitcast(BF16).rearrange("p (m d) -> p m d", m=2)
        pe2s = misc[:, 704:712].bitcast(BF16)
        kmT = misc[:, 712:720].bitcast(BF16)
        GT = dent
        gTf = arena[:, 0:22528].rearrange("p (j t) -> p j t", j=22)
        wdsl = arena[:, 22528:25600].rearrange("p (s n) -> p s n", s=6)
        _o = 25600
        rbufs = [[None, None], [None, None]]
        obufs = [[None, None], [None, None]]
        for au_ in range(2):
            for par_ in range(2):
                rbufs[au_][par_] = arena[:, _o:_o + 2056].bitcast(F32)
                _o += 2056
        for au_ in range(2):
            for par_ in range(2):
                obufs[au_][par_] = arena[:, _o:_o + 2048].bitcast(F32)
                _o += 2048
        sg = arena[:, _o:_o + 2048].bitcast(F32)
        assert _o + 2048 <= ARENA

        PS = lambda b: psum[:, b, :]
        psn = lambda b: "ps%d" % b
        rr = {"s": 0, "o": 0, "m": 0, "e": 0, "p": 0, "w": 0, "x": 0, "f": 0, "w4": 0, "wd": 0}

        def nxt(kind, n, base=0):
            v = base + rr[kind] % n
            rr[kind] += 1
            return v

        P.dma("sync", "ldc", tabs[:], tabs_d, writes=["tabs"])
        P.dma("sync", "ldc", gains_t[:], gains, writes=["gains"])
        P.dma("sync", "ldc", sel_t[:], sel_d, writes=["sel"])
        P.dma("sync", "ldc", gfin_t[:], gfin.partition_broadcast(128), writes=["gfin"])
        P.dma("gpsimd", "ldg", ov_t[:], ov_d, writes=["ov"])
        P.op("vector", lambda e: e.memset(ones_bf[:], 1.0), writes=["ones_bf"])
        stg = arena[:, 0:8192].bitcast(F32)
        off = 0
        while off < NSTRIP:
            n = min(4096, NSTRIP - off)
            o_, n_ = off, n
            P.dma("sync", "ldc", stg[:, 0:n_], strips_d[:, o_:o_ + n_], writes=["stg"])
            P.op("scalar", lambda e, o_=o_, n_=n_: e.activation(out=strips[:, o_:o_ + n_], in_=stg[:, 0:n_], func=AF.Exp),
                 reads=["stg"], writes=["strips"])
            off += n
        P.barrier()

        def SB(h, lo, n):
            o = OFF_B(h) + lo
            return strips[:, o:o + n]

        def load_w(slot, col, dram2d, ncols, nm):
            P.dma("gpsimd", "ldw%d" % slot, wsl[:, slot, :, col:col + ncols],
                  dram2d.rearrange("(c p) n -> p c n", p=128), writes=["wsl%d" % slot])

        def proj_fm(slot, c0, M, evac, tbs=range(4)):
            for tb in tbs:
                b = nxt("m", 2, 5)
                for c in range(8):
                    P.op("tensor", lambda e, b=b, c=c, tb=tb: e.matmul(
                        psum[0:M, b, :], lhsT=wsl[:, slot, c, c0:c0 + M], rhs=hnT[:, c, tb * 512:(tb + 1) * 512],
                        start=(c == 0), stop=(c == 7)),
                        reads=["wsl%d" % slot, "hnT"], writes=[psn(b)], inc=(c == 7))
                evac(tb, b)

        def proj_tm(slot, c0, N, tok_ap, evac):
            for t in range(16):
                b = nxt("m", 2, 5)
                for c in range(8):
                    P.op("tensor", lambda e, b=b, c=c, t=t: e.matmul(
                        psum[:, b, 0:N], lhsT=tok_ap(t, c), rhs=wsl[:, slot, c, c0:c0 + N],
                        start=(c == 0), stop=(c == 7)),
                        reads=["wsl%d" % slot, "hnT"], writes=[psn(b)], inc=(c == 7))
                evac(t, b)

        alt = {"i": 0}

        def cp(out, in_, reads, writes, eng=None):
            if eng is None:
                eng = "scalar" if alt["i"] % 2 == 0 else "vector"
                alt["i"] += 1
            if eng == "scalar":
                P.op("scalar", lambda e: e.copy(out=out, in_=in_), reads=reads, writes=writes)
            else:
                P.op("vector", lambda e: e.tensor_copy(out=out, in_=in_), reads=reads, writes=writes)

        def s_tile(sb_, kt_ap, q_ap, kd, reads, c0=0, c1=512):
            P.op("tensor", lambda e: e.matmul(psum[:, sb_, c0:c1], lhsT=kt_ap, rhs=q_ap[:, c0:c1], start=True, stop=True),
                 reads=reads, writes=[psn(sb_)])

        def softmax_tile(sb_, mode, h, strip_ap=None, strip2_ap=None, c0=0, c1=512):
            pb = nxt("p", 3)
            pn = "P%d" % pb
            if mode == "const":
                P.op("scalar", lambda e: e.activation(out=Pbuf[:, pb, c0:c1], in_=psum[:, sb_, c0:c1], func=AF.Exp,
                                                      bias=tabs[:, 448 + h:449 + h], scale=0.125),
                     reads=[psn(sb_), "tabs"], writes=[pn])
                return pb
            eb = nxt("e", 2)
            en = "E%d" % eb
            if mode == "eb":
                P.op("scalar", lambda e: e.activation(out=Ebuf[:, eb, c0:c1], in_=psum[:, sb_, c0:c1], func=AF.Exp, scale=0.125),
                     reads=[psn(sb_)], writes=[en])
            else:
                P.op("scalar", lambda e: e.activation(out=Ebuf[:, eb, c0:c1], in_=psum[:, sb_, c0:c1], func=AF.Exp,
                                                      bias=tabs[:, 448 + h:449 + h], scale=0.125),
                     reads=[psn(sb_), "tabs"], writes=[en])
            if strip2_ap is not None:
                P.op("vector", lambda e: e.tensor_tensor(out=Ebuf[:, eb, c0:c1], in0=Ebuf[:, eb, c0:c1], in1=strip2_ap[:, c0:c1], op=ALU.mult),
                     reads=[en, "strips"], writes=[en])
            P.op("vector", lambda e: e.tensor_tensor(out=Pbuf[:, pb, c0:c1], in0=Ebuf[:, eb, c0:c1], in1=strip_ap[:, c0:c1], op=ALU.mult),
                 reads=[en, "strips"], writes=[pn])
            return pb

        deferred = []
        pend = []

        def flush_evac():
            while deferred:
                deferred.pop(0)()

        def flush_all():
            while pend:
                pend.pop(0)()
            flush_evac()

        def attend_block(h, q_ap, kd, tiles, qreads, evac):
            ob = nxt("o", 2, 3)
            n = len(tiles)
            assert n >= 2 and tiles[0][7] == 0 and tiles[0][8] == 512
            for i, (kt_ap, kreads, va_ap, vreads, mode, strip, strip2, c0, c1) in enumerate(tiles):
                sb_ = nxt("s", 3, 0)
                s_tile(sb_, kt_ap, q_ap, kd, qreads + kreads, c0, c1)
                if len(pend) >= 2:
                    pend.pop(0)()
                pb = softmax_tile(sb_, mode, h, strip, strip2, c0, c1)
                if i == 1:
                    flush_evac()
                pend.append(lambda va_ap=va_ap, vreads=vreads, pb=pb, i=i, c0=c0, c1=c1: P.op(
                    "tensor", lambda e: e.matmul(psum[:, ob, c0:c1], lhsT=va_ap, rhs=Pbuf[:, pb, c0:c1], start=(i == 0), stop=(i == n - 1)),
                    reads=vreads + ["P%d" % pb], writes=[psn(ob)], inc=True))
            deferred.append(lambda: evac(ob))

        eps_t = sb("eps", [128, 1], F32)
        P.op("vector", lambda e: e.memset(eps_t[:], 1e-6), writes=["eps"])

        def norm_tile(xb, t, gcol, post=True):
            xn = "xt%d" % xb
            P.op("scalar", lambda e: e.activation(out=rtmp[:, 0, :], in_=xt[:, xb, 0:512], func=AF.Square, accum_out=small[:, 0:1]),
                 reads=[xn], writes=["rtmp0", "st0"])
            P.op("scalar", lambda e: e.activation(out=rtmp[:, 0, :], in_=xt[:, xb, 512:1024], func=AF.Square, accum_out=small[:, 1:2]),
                 reads=[xn], writes=["rtmp0", "st1"])
            P.op("vector", lambda e: e.tensor_tensor(out=small[:, 2:3], in0=small[:, 0:1], in1=small[:, 1:2], op=ALU.add),
                 reads=["st0", "st1"], writes=["st2"])
            P.op("scalar", lambda e: e.activation(out=small[:, 3:4], in_=small[:, 2:3], func=AF.Sqrt, bias=eps_t[:, 0:1], scale=1.0 / D),
                 reads=["st2", "eps"], writes=["st3"])
            P.op("vector", lambda e: e.reciprocal(out=small[:, 4:5], in_=small[:, 3:4]), reads=["st3"], writes=["st4"])
            P.op("vector", lambda e: e.tensor_scalar(out=xt[:, xb, :], in0=xt[:, xb, :], scalar1=small[:, 4:5], scalar2=None, op0=ALU.mult),
                 reads=[xn, "st4"], writes=[xn])
            if gcol is None or not post:
                return
            norm_post(xb, t, gcol)

        def norm_post(xb, t, gcol):
            xn = "xt%d" % xb
            for half in range(2):
                b = nxt("f", 8, 0)
                for cc in range(4):
                    c = half * 4 + cc
                    P.op("tensor", lambda e, b=b, c=c, cc=cc: e.transpose(psum[:, b, cc * 128:(cc + 1) * 128], xt[:, xb, c * 128:(c + 1) * 128], ident),
                         reads=[xn, "tabs"], writes=[psn(b)], inc=(cc == 3))
                for cc in range(4):
                    c = half * 4 + cc
                    if half == 0:
                        P.op("scalar", lambda e, b=b, c=c, cc=cc: e.activation(out=hnT[:, c, t * 128:(t + 1) * 128], in_=psum[:, b, cc * 128:(cc + 1) * 128],
                                                                              func=AF.Copy, scale=gains_t[:, gcol + c:gcol + c + 1]),
                             reads=[psn(b), "gains"], writes=["hnT"])
                    else:
                        P.op("vector", lambda e, b=b, c=c, cc=cc: e.tensor_scalar(out=hnT[:, c, t * 128:(t + 1) * 128], in0=psum[:, b, cc * 128:(cc + 1) * 128],
                                                                                 scalar1=gains_t[:, gcol + c:gcol + c + 1], scalar2=None, op0=ALU.mult),
                             reads=[psn(b), "gains"], writes=["hnTd"])

        final_toks = []

        for sq in range(nseq):
            for l in range(depth):
                src_x = (lambda t: x_d[sq, t * 128:(t + 1) * 128, :]) if l == 0 else (lambda t: scr[t * 128:(t + 1) * 128, :])
                P.phase = "norm1"
                prev_post = None
                for t in range(16):
                    xb = nxt("x", 2)
                    P.dma("sync", "ldx%d" % xb, xt[:, xb, :], src_x(t), reads=["scr"], writes=["xt%d" % xb])
                    norm_tile(xb, t, (l * 2 + 0) * 8, post=False)
                    if prev_post is not None:
                        norm_post(*prev_post)
                    prev_post = (xb, t, (l * 2 + 0) * 8)
                norm_post(*prev_post)
                W = w_in[l]
                P.op("vector", lambda e: e.memset(VA[:, :, :, 64:128], 1.0), writes=["VA"])
                P.op("vector", lambda e: e.memset(VCMP[:, 64:128], 1.0), writes=["VCMP"])
                P.op("vector", lambda e: e.memset(gT[:], 0.0), writes=["gT"])

                P.phase = "A"
                for g, dil in (enumerate(A_DIL) if "A" in ST else []):
                    slot = nxt("w", 2)
                    L = S // dil
                    load_w(slot, 0, W[:, C_AQ + g * 128:C_AQ + g * 128 + 128], 128, "aq")
                    load_w(slot, 128, W[:, C_AK + g * 128:C_AK + g * 128 + 128], 128, "ak")
                    load_w(slot, 256, W[:, C_AV + g * 128:C_AV + g * 128 + 128], 128, "av")
                    for which in range(2):
                        def evac(tb, b, which=which):
                            dst = QP[:, which, :].rearrange("p (r l) -> p r l", r=dil)[:, :, tb * 512 // dil:(tb + 1) * 512 // dil]
                            src = psum[:, b, :].rearrange("p (l r) -> p r l", r=dil)
                            cp(dst, src, [psn(b)], ["QP%d" % which])
                        proj_fm(slot, which * 128, 128, evac)

                    def tok_ap(t, c):
                        pos0 = t * 128
                        r = pos0 // L
                        l0 = pos0 % L
                        st_ = r + dil * l0
                        return hnT[:, c, st_:st_ + dil * 127 + 1:dil]

                    def evac_v(t, b):
                        cp(VA[:, t, 0:2, 0:64], psum[:, b, 0:128].rearrange("p (h d) -> p h d", h=2), [psn(b)], ["VA"])
                    proj_tm(slot, 256, 128, tok_ap, evac_v)
                    tps = L // 128
                    Sv = lambda b: psum[:, b:b + 2, 0:256]
                    astr = strips[:, OFF_A(g):OFF_A(g) + 512]
                    def a_front(i):
                        has_prev = (i % tps) != 0
                        sb_ = (0, 5)[nxt("s", 2)]
                        spair = [psn(sb_), psn(sb_ + 1)]
                        qc = slice(i * 128, (i + 1) * 128)
                        for hh in range(2):
                            pr = slice(64 * hh, 64 * hh + 64)
                            if has_prev:
                                P.op("tensor", lambda e, hh=hh, pr=pr: e.matmul(psum[:, sb_ + hh, 0:128], lhsT=QP[pr, 1, (i - 1) * 128:i * 128],
                                                                              rhs=QP[pr, 0, qc], start=True, stop=True),
                                     reads=["QP0", "QP1"], writes=spair, inc=False)
                            P.op("tensor", lambda e, hh=hh, pr=pr: e.matmul(psum[:, sb_ + hh, 128:256], lhsT=QP[pr, 1, qc],
                                                                          rhs=QP[pr, 0, qc], start=True, stop=True),
                                 reads=["QP0", "QP1"], writes=spair, inc=(hh == 1))

                        def mid():
                            eb = nxt("e", 2)
                            pb = nxt("p", 3)
                            lo = 0 if has_prev else 128
                            Ev = Ebuf[:, eb, :].rearrange("p (h x) -> p h x", h=2)
                            Pv = Pbuf[:, pb, :].rearrange("p (h x) -> p h x", h=2)
                            Av = astr.rearrange("p (h x) -> p h x", h=2)
                            P.op("scalar", lambda e: e.activation(out=Ev[:, :, lo:256], in_=Sv(sb_)[:, :, lo:256], func=AF.Exp, scale=0.125),
                                 reads=spair, writes=["E%d" % eb])
                            P.op("vector", lambda e: e.tensor_tensor(out=Pv[:, :, lo:256], in0=Ev[:, :, lo:256], in1=Av[:, :, lo:256], op=ALU.mult),
                                 reads=["E%d" % eb, "strips"], writes=["P%d" % pb])
                            return pb

                        def back(pb):
                            ob = nxt("o", 2, 3)
                            for hh in range(2):
                                if has_prev:
                                    P.op("tensor", lambda e, hh=hh: e.matmul(psum[:, ob, hh * 128:(hh + 1) * 128], lhsT=VA[:, i - 1, hh, :],
                                                                           rhs=Pbuf[:, pb, hh * 256:hh * 256 + 128], start=True, stop=False),
                                         reads=["VA", "P%d" % pb], writes=[psn(ob)], inc=False)
                                P.op("tensor", lambda e, hh=hh: e.matmul(psum[:, ob, hh * 128:(hh + 1) * 128], lhsT=VA[:, i, hh, :],
                                                                       rhs=Pbuf[:, pb, hh * 256 + 128:hh * 256 + 256], start=(not has_prev), stop=True),
                                     reads=["VA", "P%d" % pb], writes=[psn(ob)], inc=(hh == 1))
                            pos0 = i * 128
                            r = pos0 // L
                            l0 = pos0 % L
                            st_ = r + dil * l0
                            cols = slice(st_, st_ + dil * 127 + 1, dil)
                            for hh in range(2):
                                P.op("scalar", lambda e, hh=hh: e.copy(out=QP[64 * hh:64 * hh + 64, 2 + g, cols], in_=psum[0:64, ob, hh * 128:(hh + 1) * 128]),
                                     reads=[psn(ob)], writes=["QP%d" % (2 + g)])
                                if g == 0:
                                    P.op("vector", lambda e, hh=hh: e.tensor_copy(out=dent[64 * hh:64 * hh + 64, cols], in_=psum[64:128, ob, hh * 128:(hh + 1) * 128]),
                                         reads=[psn(ob)], writes=["dent"])
                                else:
                                    P.op("vector", lambda e, hh=hh: e.tensor_tensor(out=dent[64 * hh:64 * hh + 64, cols], in0=dent[64 * hh:64 * hh + 64, cols],
                                                                                  in1=psum[64:128, ob, hh * 128:(hh + 1) * 128], op=ALU.add),
                                         reads=[psn(ob), "dent"], writes=["dent"])
                        return mid, back

                    pendA = None
                    for i in range(16):
                        mid, back = a_front(i)
                        pb = mid()
                        if pendA is not None:
                            pendA[0](pendA[1])
                        pendA = (back, pb)
                    pendA[0](pendA[1])
                if "A" in ST:
                    P.op("vector", lambda e: e.reciprocal(out=dent[:, :], in_=dent[:, :]), reads=["dent"], writes=["dent"])
                for g in (range(3) if "A" in ST else []):
                    P.op("vector", lambda e, g=g: e.tensor_tensor(out=omT[:, g, :], in0=QP[:, 2 + g, :], in1=dent[:, :], op=ALU.mult),
                         reads=["QP%d" % (2 + g), "dent"], writes=["omT"])

                P.phase = "C"
                for j in (range(2) if "C" in ST else []):
                    slot = nxt("w", 2)
                    load_w(slot, 0, W[:, C_CQ + j * 128:C_CQ + j * 128 + 128], 128, "cq")
                    load_w(slot, 128, W[:, C_CK + j * 128:C_CK + j * 128 + 128], 128, "ck")
                    load_w(slot, 256, W[:, C_CV + j * 128:C_CV + j * 128 + 128], 128, "cv")
                    for hh in range(2):
                        P.op("vector", lambda e, hh=hh: e.memset(QP[64:72, hh, 0:1024], 0.0), writes=["QP%d" % hh])
                        P.dma("gpsimd", "ldg", QP[64:72, 2 + hh, :], indm_d, writes=["QP%d" % (2 + hh)])
                    for which in range(2):
                        def evac(tb, b, which=which):
                            for hh in range(2):
                                cp(QP[0:64, which * 2 + hh, tb * 512:(tb + 1) * 512], psum[64 * hh:64 * hh + 64, b, :], [psn(b)], ["QP%d" % (which * 2 + hh)])
                        proj_fm(slot, which * 128, 128, evac)

                    def evac_v(t, b):
                        cp(VA[:, t, 0:2, 0:64], psum[:, b, 0:128].rearrange("p (h d) -> p h d", h=2), [psn(b)], ["VA"])
                    proj_tm(slot, 256, 128, lambda t, c: hnT[:, c, t * 128:(t + 1) * 128], evac_v)
                    for hh in range(2):
                        h = 12 + 2 * j + hh
                        qn, kn = "QP%d" % hh, "QP%d" % (2 + hh)
                        P.op("vector", lambda e, hh=hh: e.tensor_reduce(out=gx[0:64, 0:8], in_=QP[0:64, 2 + hh, :].rearrange("p (n k) -> p n k", k=256),
                                                                      axis=AX.X, op=ALU.add),
                             reads=[kn], writes=["gx"])
                        P.op("vector", lambda e, hh=hh: e.tensor_scalar(out=kmT[0:64, hh * 8:hh * 8 + 8], in0=gx[0:64, 0:8], scalar1=1.0 / 256, scalar2=None, op0=ALU.mult),
                             reads=["gx"], writes=["kmT"])
                        sc64, m64, nm64 = gu[:, 0:64], gu[:, 64:128], gx[:, 64:128]
                        for ti in range(8):
                            i = 8 + ti
                            P.op("tensor", lambda e, i=i, ti=ti, hh=hh: e.matmul(psum[:, 7, ti * 8:ti * 8 + 8], lhsT=QP[0:64, hh, i * 128:(i + 1) * 128],
                                                                               rhs=kmT[0:64, hh * 8:hh * 8 + 8], start=True, stop=True),
                                 reads=[qn, "kmT"], writes=["ps7"], inc=(ti == 7))
                        P.op("vector", lambda e: e.tensor_tensor(out=sc64, in0=psum[:, 7, 0:64], in1=tabs[:, 384:448], op=ALU.add),
                             reads=["ps7", "tabs"], writes=["gu"])
                        for ti in range(8):
                            P.op("vector", lambda e, ti=ti: e.max(out=m64[:, ti * 8:ti * 8 + 8], in_=sc64[:, ti * 8:ti * 8 + 8]), reads=["gu"], writes=["gu_m"])
                        for ti in range(8):
                            P.op("vector", lambda e, ti=ti: e.tensor_scalar(out=nm64[:, ti * 8:ti * 8 + 8], in0=sc64[:, ti * 8:ti * 8 + 8],
                                                                            scalar1=m64[:, ti * 8 + 3:ti * 8 + 4], scalar2=None, op0=ALU.is_ge),
                                 reads=["gu", "gu_m"], writes=["gx"])
                        P.op("vector", lambda e: e.tensor_scalar(out=nm64, in0=nm64, scalar1=-1.0, scalar2=BIG, op0=ALU.add, op1=ALU.mult),
                             reads=["gx"], writes=["gx"])
                        for ti in range(8):
                            bt = 5 + ti // 4
                            P.op("tensor", lambda e, ti=ti, bt=bt: e.transpose(psum[0:8, bt, (ti % 4) * 128:(ti % 4 + 1) * 128], nm64[:, ti * 8:ti * 8 + 8], ident),
                                 reads=["gx", "tabs"], writes=[psn(bt)], inc=(ti % 4 == 3))
                        for half_ in range(2):
                            P.op("scalar", lambda e, half_=half_, hh=hh: e.copy(out=QP[64:72, hh, 1024 + half_ * 512:1536 + half_ * 512], in_=psum[0:8, 5 + half_, :]),
                                 reads=[psn(5 + half_)], writes=[qn])
                        for qb in range(4):
                            tiles = []
                            for kt in range(4 * qb + 4):
                                dl = 512 * qb - 128 * kt
                                if dl >= 256:
                                    mode, strip = "const", None
                                else:
                                    mode, strip = "eb", SB(h, dl + 384, 512)
                                tiles.append((QP[0:72, 2 + hh, kt * 128:(kt + 1) * 128], [kn], VA[:, kt, hh, :], ["VA"], mode, strip, None,
                                              max(0, -dl), 512))

                            def evac_o(ob, qb=qb, hh=hh, j=j):
                                rb = nxt("e", 2)
                                P.op("scalar", lambda e: e.activation(out=rtmp[0:64, rb, :], in_=psum[64:128, ob, :], func=AF.Ln), reads=[psn(ob)], writes=["rtmp%d" % rb])
                                P.op("scalar", lambda e: e.activation(out=rtmp[0:64, rb, :], in_=rtmp[0:64, rb, :], func=AF.Exp, scale=-1.0), reads=["rtmp%d" % rb], writes=["rtmp%d" % rb])
                                P.op("vector", lambda e: e.tensor_tensor(out=omT[64 * hh:64 * hh + 64, 6 + j, qb * 512:(qb + 1) * 512], in0=psum[0:64, ob, :],
                                                                       in1=rtmp[0:64, rb, :], op=ALU.mult),
                                     reads=[psn(ob), "rtmp%d" % rb], writes=["omT"])
                            attend_block(h, QP[0:72, hh, qb * 512:(qb + 1) * 512], 72, tiles, [qn], evac_o)
                        flush_all()

                P.phase = "B"
                for g in (range(2) if "B" in ST else []):
                    P.phase = "B.compress"
                    slot = nxt("w", 2)
                    for kv, cbase in enumerate((C_BKC, C_BVC)):
                        for rep in range(2):
                            load_w(slot, kv * 128 + rep * 64, W[:, cbase + g * 64:cbase + g * 64 + 64], 64, "kc")
                    for kv in range(2):
                        def evac(tb, b, kv=kv):
                            cp(QP[0:64, kv, tb * 512:(tb + 1) * 512], psum[0:64, b, :], [psn(b)], ["QP%d" % kv])
                            if tb == 0:
                                cp(QP[64:128, kv, 0:511], psum[64:128, b, 1:512], [psn(b)], ["QP%d" % kv])
                            else:
                                cp(QP[64:128, kv, tb * 512 - 1:tb * 512 + 511], psum[64:128, b, :], [psn(b)], ["QP%d" % kv])
                        proj_fm(slot, kv * 128, 128, evac)
                    for kv, (w1d, w2d) in enumerate(((w1k, w2k), (w1v, w2v))):
                        slot = nxt("w", 2)
                        w1s = wsl[:, slot, :, :].rearrange("p c n -> p (c n)").rearrange("p (r m) -> p r m", m=256)
                        P.dma("gpsimd", "ldw%d" % slot, w1s, w1d[l].rearrange("(r p) m -> p r m", p=128), writes=["wsl%d" % slot])
                        P.dma("gpsimd", "ldg", w2s, w2d[l].rearrange("(m p) d -> p m d", p=128), reads=["gT"], writes=["w2s"])
                        P.dma("gpsimd", "ldg", pe2s, pe2[l, kv], writes=["pe2s"])
                        for mc in range(2):
                            b = nxt("m", 2, 5)
                            for r in range(16):
                                P.op("tensor", lambda e, b=b, r=r, mc=mc: e.matmul(psum[:, b, 0:1], lhsT=w1s[:, r, mc * 128:(mc + 1) * 128], rhs=pe2s[:, r:r + 1],
                                                                                 start=(r == 0), stop=(r == 15)),
                                     reads=["wsl%d" % slot, "pe2s"], writes=[psn(b)], inc=(r == 15))
                            P.op("vector", lambda e, b=b, mc=mc: e.tensor_copy(out=hb[:, mc:mc + 1], in_=psum[:, b, 0:1]), reads=[psn(b)], writes=["hb"])
                            b = nxt("m", 2, 5)
                            for r in range(16):
                                P.op("tensor", lambda e, b=b, r=r, mc=mc: e.matmul(psum[:, b, 0:127], lhsT=w1s[:, r, mc * 128:(mc + 1) * 128],
                                                                                 rhs=QP[:, kv, 2 * r:2 * r + 16 * 126 + 1:16], start=(r == 0), stop=(r == 15)),
                                     reads=["wsl%d" % slot, "QP%d" % kv], writes=[psn(b)], inc=(r == 15))
                            P.op("scalar", lambda e, b=b, mc=mc: e.activation(out=gx[:, 0:127], in_=psum[:, b, 0:127], func=AF.Identity, bias=hb[:, mc:mc + 1], scale=1.0),
                                 reads=[psn(b), "hb"], writes=["gx"])
                            P.op("vector", lambda e: e.tensor_tensor(out=gu[:, 0:127], in0=gx[:, 0:127], in1=gx[:, 0:127], op=ALU.mult), reads=["gx"], writes=["gu"])
                            P.op("vector", lambda e: e.tensor_scalar(out=gu[:, 0:127], in0=gu[:, 0:127], scalar1=0.044715, scalar2=1.0, op0=ALU.mult, op1=ALU.add),
                                 reads=["gu"], writes=["gu"])
                            P.op("vector", lambda e: e.tensor_tensor(out=gu[:, 0:127], in0=gu[:, 0:127], in1=gx[:, 0:127], op=ALU.mult), reads=["gu", "gx"], writes=["gu"])
                            P.op("scalar", lambda e: e.activation(out=gu[:, 0:127], in_=gu[:, 0:127], func=AF.Sigmoid, scale=2.0 * math.sqrt(2.0 / math.pi)),
                                 reads=["gu"], writes=["gu"])
                            P.op("vector", lambda e, mc=mc: e.tensor_tensor(out=gT[:, mc, 0:127], in0=gu[:, 0:127], in1=gx[:, 0:127], op=ALU.mult),
                                 reads=["gu", "gx"], writes=["gT"])
                        b = nxt("m", 2, 5)
                        if kv == 0:
                            for mc in range(2):
                                P.op("tensor", lambda e, b=b, mc=mc: e.matmul(psum[0:64, b, 0:128], lhsT=w2s[:, mc, :], rhs=gT[:, mc, :], start=(mc == 0), stop=(mc == 1)),
                                     reads=["w2s", "gT"], writes=[psn(b)], inc=(mc == 1))
                            P.op("vector", lambda e, b=b: e.tensor_copy(out=KCMP[0:64, :], in_=psum[0:64, b, 0:128]), reads=[psn(b)], writes=["KCMP"])
                        else:
                            for mc in range(2):
                                P.op("tensor", lambda e, b=b, mc=mc: e.matmul(psum[:, b, 0:64], lhsT=gT[:, mc, :], rhs=w2s[:, mc, :], start=(mc == 0), stop=(mc == 1)),
                                     reads=["w2s", "gT"], writes=[psn(b)], inc=(mc == 1))
                            P.op("vector", lambda e, b=b: e.tensor_copy(out=VCMP[:, 0:64], in_=psum[:, b, 0:64]), reads=[psn(b)], writes=["VCMP"])
                    P.phase = "B.proj"
                    slot = nxt("w", 2)
                    load_w(slot, 0, W[:, C_BQ + g * 192:C_BQ + g * 192 + 192], 192, "bq")
                    load_w(slot, 192, W[:, C_BKS + g * 64:C_BKS + g * 64 + 64], 64, "bks")
                    load_w(slot, 256, W[:, C_BKW + g * 64:C_BKW + g * 64 + 64], 64, "bkw")
                    load_w(slot, 320, W[:, C_BVS + g * 64:C_BVS + g * 64 + 64], 64, "bvs")
                    load_w(slot, 384, W[:, C_BVW + g * 64:C_BVW + g * 64 + 64], 64, "bvw")
                    load_w(slot, 448, W[:, C_BG + g * 9:C_BG + g * 9 + 9], 9, "bg")
                    for hq in range(3):
                        P.op("vector", lambda e, hq=hq: e.memset(QP[64:96, hq, 0:1024], 0.0), reads=["QP0", "QP1"], writes=["QP%d" % hq])
                    P.dma("gpsimd", "ldg", QP[64:96, 3, :], inds_d, writes=["QP3"])

                    def evac1(tb, b):
                        for hh in range(2):
                            cp(QP[0:64, hh, tb * 512:(tb + 1) * 512], psum[64 * hh:64 * hh + 64, b, :], [psn(b)], ["QP%d" % hh])
                    proj_fm(slot, 0, 128, evac1)

                    def evac2(tb, b):
                        for hh in range(2):
                            cp(QP[0:64, 2 + hh, tb * 512:(tb + 1) * 512], psum[64 * hh:64 * hh + 64, b, :], [psn(b)], ["QP%d" % (2 + hh)])
                    proj_fm(slot, 128, 128, evac2)

                    def evac3(tb, b):
                        cp(QP[0:64, 4, tb * 512:(tb + 1) * 512], psum[0:64, b, :], [psn(b)], ["QP4"])
                    proj_fm(slot, 256, 64, evac3)

                    def evac4(tb, b):
                        P.op("scalar", lambda e: e.activation(out=GT[0:9, tb * 512:(tb + 1) * 512], in_=psum[0:9, b, :], func=AF.Sigmoid),
                             reads=[psn(b)], writes=["GT"])
                    proj_fm(slot, 448, 9, evac4)

                    def evac_v(t, b):
                        cp(VA[:, t, 0:2, 0:64], psum[:, b, 0:128].rearrange("p (h d) -> p h d", h=2), [psn(b)], ["VA"])
                    proj_tm(slot, 320, 128, lambda t, c: hnT[:, c, t * 128:(t + 1) * 128], evac_v)

                    def gate_mult(ob, hq, br, qb, clampden, ph):
                        rb = nxt("e", 2)
                        rn = "rtmp%d" % rb
                        if clampden:
                            P.op("vector", lambda e: e.tensor_scalar(out=rtmp[ph:ph + 64, rb, :], in0=psum[64:128, ob, :], scalar1=1e-30, scalar2=None, op0=ALU.max),
                                 reads=[psn(ob)], writes=[rn])
                            P.op("scalar", lambda e: e.activation(out=rtmp[ph:ph + 64, rb, :], in_=rtmp[ph:ph + 64, rb, :], func=AF.Ln), reads=[rn], writes=[rn])
                        else:
                            P.op("scalar", lambda e: e.activation(out=rtmp[ph:ph + 64, rb, :], in_=psum[64:128, ob, :], func=AF.Ln), reads=[psn(ob)], writes=[rn])
                        P.op("scalar", lambda e: e.activation(out=rtmp[ph:ph + 64, rb, :], in_=rtmp[ph:ph + 64, rb, :], func=AF.Exp, scale=-1.0), reads=[rn], writes=[rn])
                        b = nxt("m", 2, 5)
                        jj = hq * 3 + br
                        P.op("tensor", lambda e: e.matmul(psum[0:64, b, :], lhsT=sel_t[0:9, jj * 64:(jj + 1) * 64], rhs=GT[0:9, qb * 512:(qb + 1) * 512], start=True, stop=True),
                             reads=["sel", "GT"], writes=[psn(b)])
                        P.op("vector", lambda e: e.tensor_tensor(out=rtmp[ph:ph + 64, rb, :], in0=rtmp[ph:ph + 64, rb, :], in1=psum[0:64, b, :], op=ALU.mult),
                             reads=[rn, psn(b)], writes=[rn])
                        return rb

                    P.phase = "B.cmp"
                    for qb in range(4):
                        impb = 7 if qb >= 2 else None
                        for hq in range(3):
                            hb_ = 3 * g + hq
                            h = 6 + hb_
                            ch, ph = 3 + hb_ // 2, 64 * (hb_ % 2)
                            pbs = {}

                            def evac_o(ob, qb=qb, hq=hq, ch=ch, ph=ph):
                                rb = gate_mult(ob, hq, 0, qb, True, ph)
                                P.op("vector", lambda e: e.tensor_tensor(out=omT[ph:ph + 64, ch, qb * 512:(qb + 1) * 512], in0=psum[0:64, ob, :],
                                                                       in1=rtmp[ph:ph + 64, rb, :], op=ALU.mult),
                                     reads=[psn(ob), "rtmp%d" % rb], writes=["omT"])
                            ob = nxt("o", 2, 3)
                            sb_ = nxt("s", 3, 0)
                            s_tile(sb_, KCMP[0:64, :], QP[0:64, hq, qb * 512:(qb + 1) * 512], 64, ["KCMP", "QP%d" % hq])
                            pb = softmax_tile(sb_, "eb", h, strips[:, OFF_CMP + qb * 512:OFF_CMP + (qb + 1) * 512])
                            P.op("tensor", lambda e, ob=ob, pb=pb: e.matmul(PS(ob), lhsT=VCMP[:, :], rhs=Pbuf[:, pb, :], start=True, stop=True),
                                 reads=["VCMP", "P%d" % pb], writes=[psn(ob)])
                            if impb is not None:
                                for ti in range(4):
                                    P.op("tensor", lambda e, pb=pb, ti=ti, hq=hq: e.matmul(psum[:, 7, ti * 128 + hq * 33:ti * 128 + hq * 33 + 33],
                                                                                         lhsT=Pbuf[:, pb, ti * 128:(ti + 1) * 128], rhs=ov_t[:, 0:33], start=True, stop=True),
                                         reads=["P%d" % pb, "ov"], writes=["ps7"], inc=(ti == 3))
                            flush_evac()
                            deferred.append(lambda evac_o=evac_o, ob=ob: evac_o(ob))
                        if impb is not None:
                            for ti in range(4):
                                i = 4 * qb + ti
                                base = ti * 128
                                P.op("vector", lambda e, base=base: e.reciprocal(out=rec3[:, 0:3], in_=psum[:, 7, base + 32:base + 99:33]), reads=["ps7"], writes=["rec3"])
                                P.op("vector", lambda e, base=base: e.tensor_scalar(out=scb[:, :], in0=psum[:, 7, base:base + 32], scalar1=rec3[:, 0:1], scalar2=None, op0=ALU.mult),
                                     reads=["ps7", "rec3"], writes=["scb"])
                                for hq in (1, 2):
                                    P.op("vector", lambda e, base=base, hq=hq: e.scalar_tensor_tensor(out=scb[:, :], in0=psum[:, 7, base + hq * 33:base + hq * 33 + 32],
                                                                                                     scalar=rec3[:, hq:hq + 1], in1=scb[:, :], op0=ALU.mult, op1=ALU.add),
                                         reads=["ps7", "rec3", "scb"], writes=["scb"])
                                P.op("vector", lambda e, i=i: e.tensor_tensor(out=scb[:, :], in0=scb[:, :], in1=tabs[:, 128 + (i - 8) * 32:128 + (i - 8) * 32 + 32], op=ALU.add),
                                     reads=["scb", "tabs"], writes=["scb"])
                                P.op("vector", lambda e: e.max(out=m8[:, 0:8], in_=scb[:, :]), reads=["scb"], writes=["m8"])
                                P.op("vector", lambda e: e.match_replace(out=sctmp[:, :], in_to_replace=m8[:, 0:8], in_values=scb[:, :], imm_value=-1e30),
                                     reads=["scb", "m8"], writes=["sctmp"])
                                P.op("vector", lambda e: e.max(out=m8[:, 8:16], in_=sctmp[:, :]), reads=["sctmp"], writes=["m8"])
                                P.op("vector", lambda e: e.tensor_scalar(out=nmk[:, :], in0=scb[:, :], scalar1=m8[:, 15:16], scalar2=None, op0=ALU.is_ge),
                                     reads=["scb", "m8"], writes=["nmk"])
                                P.op("vector", lambda e: e.tensor_scalar(out=nmk[:, :], in0=nmk[:, :], scalar1=-1.0, scalar2=BIG, op0=ALU.add, op1=ALU.mult),
                                     reads=["nmk"], writes=["nmk"])
                                b2 = nxt("m", 2, 5)
                                P.op("tensor", lambda e, b2=b2: e.transpose(psum[0:32, b2, 0:128], nmk[:, :], ident), reads=["nmk", "tabs"], writes=[psn(b2)])
                                for hq in range(3):
                                    cp(QP[64:96, hq, i * 128:(i + 1) * 128], psum[0:32, b2, 0:128], [psn(b2)], ["QP%d" % hq], eng="scalar")
                    flush_evac()
                    P.phase = "B.slcwin"
                    for hq in range(3):
                        hb_ = 3 * g + hq
                        h = 6 + hb_
                        ch, ph = 3 + hb_ // 2, 64 * (hb_ % 2)
                        qn = "QP%d" % hq
                        for qb in range(4):
                            for br in (1, 2):
                                tiles = []
                                if br == 1:
                                    for kt in range(4 * qb + 4):
                                        dl = 512 * qb - 128 * kt
                                        if dl >= 256:
                                            mode, strip = "const", None
                                        else:
                                            mode, strip = "eb", SB(h, dl + 384, 512)
                                        tiles.append((QP[0:96, 3, kt * 128:(kt + 1) * 128], ["QP3"], VA[:, kt, 0, :], ["VA"], mode, strip, None,
                                                      max(0, -dl), 512))
                                    kd = 96
                                else:
                                    for kt in range(max(0, 4 * qb - 4), 4 * qb + 4):
                                        dl = 512 * qb - 128 * kt
                                        if dl >= 256:
                                            mode, strip, strip2 = "cb", strips[:, OFF_UM + dl:OFF_UM + dl + 512], None
                                        elif dl == 128:
                                            mode, strip, strip2 = "eb", strips[:, OFF_UM + dl:OFF_UM + dl + 512], SB(h, dl + 384, 512)
                                        else:
                                            mode, strip, strip2 = "eb", SB(h, dl + 384, 512), None
                                        c0_ = max(0, -dl)
                                        c1_ = 512 if dl <= 128 else 640 - dl
                                        tiles.append((QP[0:64, 4, kt * 128:(kt + 1) * 128], ["QP4"], VA[:, kt, 1, :], ["VA"], mode, strip, strip2, c0_, c1_))
                                    tiles.sort(key=lambda tl: 0 if (tl[7] == 0 and tl[8] == 512) else 1)
                                    kd = 64

                                def evac_o(ob, qb=qb, hq=hq, ch=ch, ph=ph, br=br):
                                    rb = gate_mult(ob, hq, br, qb, False, ph)
                                    rn = "rtmp%d" % rb
                                    P.op("vector", lambda e: e.tensor_tensor(out=rtmp[ph:ph + 64, rb, :], in0=psum[0:64, ob, :], in1=rtmp[ph:ph + 64, rb, :], op=ALU.mult),
                                         reads=[psn(ob), rn], writes=[rn])
                                    P.op("vector", lambda e: e.tensor_tensor(out=omT[ph:ph + 64, ch, qb * 512:(qb + 1) * 512], in0=omT[ph:ph + 64, ch, qb * 512:(qb + 1) * 512],
                                                                           in1=rtmp[ph:ph + 64, rb, :], op=ALU.add),
                                         reads=["omT", rn], writes=["omT"])
                                attend_block(h, QP[0:kd, hq, qb * 512:(qb + 1) * 512], kd, tiles, [qn], evac_o)
                    flush_all()

                if debug and sq == 0 and l == 0:
                    P.dma("sync", "dbg", dbg["omT"], omT, reads=["omT"])

                P.phase = "merge"
                Gm = dent.rearrange("p (b n) -> p b n", b=4)
                for fc in (range(8) if "merge" in ST else []):
                    slot = nxt("w", 2)
                    load_w(slot, 0, w_br[l][:, fc * 128:(fc + 1) * 128], 128, "wb")
                    for gi in range(3):
                        load_w(slot, 128 + gi * 128, W[:, C_MG + gi * 1024 + fc * 128:C_MG + gi * 1024 + fc * 128 + 128], 128, "mg")
                    for tb in range(4):
                        tsl = slice(tb * 512, (tb + 1) * 512)
                        bs = [5, 6, 7]
                        gb = [nxt("s", 3, 0) for _ in range(3)]
                        rng = ((0, 3), (3, 6), (6, 8))
                        for gi in range(3):
                            for c in range(8):
                                P.op("tensor", lambda e, gi=gi, c=c: e.matmul(PS(gb[gi]), lhsT=wsl[:, slot, c, 128 + gi * 128:256 + gi * 128], rhs=hnT[:, c, tsl],
                                                                            start=(c == 0), stop=(c == 7)),
                                     reads=["wsl%d" % slot, "hnT"], writes=[psn(gb[gi])], inc=(c == 7))
                            P.op("scalar", lambda e, gi=gi: e.activation(out=Gm[:, gi, :], in_=PS(gb[gi]), func=AF.Sigmoid), reads=[psn(gb[gi])], writes=["Gm%d" % gi])
                            k0, k1 = rng[gi]
                            for k in range(k0, k1):
                                P.op("tensor", lambda e, gi=gi, k=k, k0=k0, k1=k1: e.matmul(PS(bs[gi]), lhsT=wsl[:, slot, k, 0:128], rhs=omT[:, k, tsl],
                                                                                          start=(k == k0), stop=(k == k1 - 1)),
                                     reads=["wsl%d" % slot, "omT"], writes=[psn(bs[gi])], inc=(k == k1 - 1))
                        P.op("vector", lambda e: e.tensor_tensor(out=Gm[:, 0, :], in0=Gm[:, 0, :], in1=PS(bs[0]), op=ALU.mult), reads=["Gm0", psn(bs[0])], writes=["Gm0"])
                        P.op("vector", lambda e: e.tensor_tensor(out=Gm[:, 1, :], in0=Gm[:, 1, :], in1=PS(bs[1]), op=ALU.mult), reads=["Gm1", psn(bs[1])], writes=["Gm1"])
                        P.op("vector", lambda e: e.tensor_tensor(out=Gm[:, 2, :], in0=Gm[:, 2, :], in1=PS(bs[2]), op=ALU.mult), reads=["Gm2", psn(bs[2])], writes=["Gm2"])
                        P.op("vector", lambda e: e.tensor_tensor(out=Gm[:, 0, :], in0=Gm[:, 0, :], in1=Gm[:, 1, :], op=ALU.add), reads=["Gm0", "Gm1"], writes=["Gm0"])
                        P.op("vector", lambda e, fc=fc, tsl=tsl: e.tensor_tensor(out=mergedT[:, fc, tsl], in0=Gm[:, 0, :], in1=Gm[:, 2, :], op=ALU.add),
                             reads=["Gm0", "Gm2", "QP0", "QP1", "QP2", "QP3", "QP4", "VA"], writes=["mergedT"])

                P.phase = "outproj"
                wo = wsl_raw[:].rearrange("p (c n) -> p c n", c=8)
                P.dma("gpsimd", "ldw0", wo, w_out[l].rearrange("(c p) n -> p c n", p=128), reads=["wsl1"], writes=["wsl0", "wsl1"])
                prev_post = None
                for t in (range(16) if "outproj" in ST else []):
                    xb = nxt("x", 2)
                    xn = "xt%d" % xb
                    P.dma("sync", "ldx%d" % xb, xt[:, xb, :], src_x(t), reads=["scr"], writes=[xn])
                    for hf in range(2):
                        b = nxt("f", 8, 0)
                        for c in range(8):
                            P.op("tensor", lambda e, b=b, c=c, hf=hf, t=t: e.matmul(PS(b), lhsT=mergedT[:, c, t * 128:(t + 1) * 128], rhs=wo[:, c, hf * 512:(hf + 1) * 512],
                                                                                  start=(c == 0), stop=(c == 7)),
                                 reads=["mergedT", "wsl0", "wsl1"], writes=[psn(b)], inc=(c == 7))
                        P.op("vector", lambda e, b=b, hf=hf: e.tensor_tensor(out=xt[:, xb, hf * 512:(hf + 1) * 512], in0=xt[:, xb, hf * 512:(hf + 1) * 512], in1=PS(b), op=ALU.add),
                             reads=[psn(b), xn], writes=[xn])
                    P.dma("sync", "stx", scr[t * 128:(t + 1) * 128, :], xt[:, xb, :], reads=[xn], writes=["scr_t%d" % t])
                    if debug and sq == 0 and l == 0:
                        P.dma("sync", "dbg", dbg["x1"][t * 128:(t + 1) * 128, :], xt[:, xb, :], reads=[xn])
                    norm_tile(xb, t, (l * 2 + 1) * 8, post=False)
                    if prev_post is not None:
                        norm_post(*prev_post)
                    prev_post = (xb, t, (l * 2 + 1) * 8)
                if prev_post is not None:
                    norm_post(*prev_post)
                P.barrier()

                P.phase = "ffn"
                P.dma("sync", "ldc", cw_t[:], cwt[l], writes=["cw"])
                P.op("vector", lambda e: e.memset(halo[:], 0.0), writes=["halo%d" % q_ for q_ in range(44)])
                wsl4 = wsl_raw[:].rearrange("p (s c n) -> p s c n", s=4, c=8)
                for hf in (range(2) if "ffn" in ST else []):
                    P.phase = "ffn.up"
                    for j in range(22):
                        s4 = nxt("w4", 4)
                        wn = "wf%d" % s4
                        P.dma("gpsimd", "ldf%d" % s4, wsl4[:, s4, :, 0:128], w_up[l][:, j * 128:(j + 1) * 128].rearrange("(c p) n -> p c n", p=128),
                              writes=[wn, "wsl%d" % (s4 // 2)])
                        P.dma("gpsimd", "ldf%d" % s4, wsl4[:, s4, :, 128:256], w_up[l][:, DFF + j * 128:DFF + (j + 1) * 128].rearrange("(c p) n -> p c n", p=128),
                              writes=[wn])
                        par = j % 2
                        for au in range(2):
                            rbuf = rbufs[au][par]
                            obuf = obufs[au][par]
                            jj = au * 22 + j
                            rn, on = "r%d_%d" % (au, par), "o%d_%d" % (au, par)
                            P.op("scalar", lambda e, rbuf=rbuf, jj=jj: e.copy(out=rbuf[:, 0:2], in_=halo[:, jj, :]), reads=["halo%d" % jj], writes=[rn])
                            for tbl in range(2):
                                tb = hf * 2 + tbl
                                b = nxt("f", 8, 0)
                                for c in range(8):
                                    P.op("tensor", lambda e, b=b, c=c, tb=tb, au=au: e.matmul(PS(b), lhsT=wsl4[:, s4, c, au * 128:(au + 1) * 128], rhs=hnT[:, c, tb * 512:(tb + 1) * 512],
                                                                                            start=(c == 0), stop=(c == 7)),
                                         reads=[wn, "hnT"], writes=[psn(b)], inc=(c == 7))
                                P.op("scalar", lambda e, b=b, rbuf=rbuf, tbl=tbl: e.copy(out=rbuf[:, 2 + tbl * 512:2 + (tbl + 1) * 512], in_=PS(b)), reads=[psn(b)], writes=[rn])
                            cwb = jj * 4
                            P.op("scalar", lambda e, rbuf=rbuf, obuf=obuf, cwb=cwb: e.activation(out=obuf[:, :], in_=rbuf[:, 2:1026], func=AF.Identity,
                                                                                               bias=cw_t[:, cwb + 3:cwb + 4], scale=cw_t[:, cwb + 2:cwb + 3]),
                                 reads=[rn, "cw"], writes=[on])
                            P.op("vector", lambda e, rbuf=rbuf, obuf=obuf, cwb=cwb: e.scalar_tensor_tensor(out=obuf[:, :], in0=rbuf[:, 1:1025], scalar=cw_t[:, cwb + 1:cwb + 2],
                                                                                                         in1=obuf[:, :], op0=ALU.mult, op1=ALU.add),
                                 reads=[rn, on, "cw"], writes=[on])
                            P.op("vector", lambda e, rbuf=rbuf, obuf=obuf, cwb=cwb: e.scalar_tensor_tensor(out=obuf[:, :], in0=rbuf[:, 0:1024], scalar=cw_t[:, cwb:cwb + 1],
                                                                                                         in1=obuf[:, :], op0=ALU.mult, op1=ALU.add),
                                 reads=[rn, on, "cw"], writes=[on])
                            if hf == 0:
                                P.op("scalar", lambda e, rbuf=rbuf, jj=jj: e.copy(out=halo[:, jj, :], in_=rbuf[:, 1024:1026]), reads=[rn], writes=["halo%d" % jj])
                        oa_, ou_ = obufs[0][par], obufs[1][par]
                        P.op("scalar", lambda e: e.activation(out=sg[:, :], in_=oa_[:, :], func=AF.Silu), reads=["o0_%d" % par], writes=["sg"])
                        P.op("vector", lambda e, j=j: e.tensor_tensor(out=gTf[:, j, :], in0=sg[:, :], in1=ou_[:, :], op=ALU.mult), reads=["sg", "o1_%d" % par], writes=["gTf%d" % j])
                    P.phase = "ffn.dn"
                    xt4 = xt[:].rearrange("p b (h n) -> p (b h) n", h=2)
                    for chh in range(2):
                        xbufs = {}
                        for tt in range(4):
                            t = hf * 8 + tt
                            xn = "xq%d" % tt
                            P.dma("sync", "ldx4_%d" % tt, xt4[:, tt, :], scr[t * 128:(t + 1) * 128, chh * 512:(chh + 1) * 512],
                                  reads=["scr_t%d" % t], writes=[xn, "xt0", "xt1"])
                            xbufs[tt] = (tt, xn)
                        for j in range(22):
                            sd = nxt("wd", 6)
                            dn = "wd%d" % sd
                            P.dma("gpsimd", "ldd%d" % sd, wdsl[:, sd, :], w_dn[l][j * 128:(j + 1) * 128, chh * 512:(chh + 1) * 512], writes=[dn])
                            for tt in range(8):
                                P.op("tensor", lambda e, j=j, tt=tt: e.matmul(PS(tt), lhsT=gTf[:, j, tt * 128:(tt + 1) * 128], rhs=wdsl[:, sd, :], start=(j == 0), stop=(j == 21)),
                                     reads=["gTf%d" % j, dn], writes=[psn(tt)], inc=(tt == 7))
                        def dn_add(tt):
                            xb4, xn = xbufs[tt]
                            P.op("vector", lambda e: e.tensor_tensor(out=xt4[:, xb4, :], in0=xt4[:, xb4, :], in1=PS(tt), op=ALU.add), reads=[psn(tt), xn], writes=[xn])

                        def dn_store(tt):
                            t = hf * 8 + tt
                            xb4, xn = xbufs[tt]
                            P.dma("sync", "stx", scr[t * 128:(t + 1) * 128, chh * 512:(chh + 1) * 512], xt4[:, xb4, :], reads=[xn], writes=["scr_t%d" % t, "scr"])

                        for tt in range(4):
                            dn_add(tt)
                        for tt in range(4):
                            dn_store(tt)
                        for tt in range(4, 8):
                            t2 = hf * 8 + tt
                            xb4, xn = xbufs[tt - 4]
                            P.dma("sync", "ldx4_%d" % xb4, xt4[:, xb4, :], scr[t2 * 128:(t2 + 1) * 128, chh * 512:(chh + 1) * 512], reads=["scr_t%d" % t2], writes=[xn])
                            xbufs[tt] = (xb4, xn)
                        for tt in range(4, 8):
                            dn_add(tt)
                            dn_store(tt)
                P.barrier()
            P.phase = "final"
            for t in (range(16) if "final" in ST else []):
                xb = nxt("x", 2)
                xn = "xt%d" % xb
                P.dma("sync", "ldx%d" % xb, xt[:, xb, :], scr[t * 128:(t + 1) * 128, :], reads=["scr", "scr_t%d" % t], writes=[xn])
                norm_tile(xb, t, None)
                P.op("vector", lambda e, xb=xb: e.tensor_tensor(out=xt[:, xb, :], in0=xt[:, xb, :], in1=gfin_t[:, :], op=ALU.mult), reads=[xn, "gfin"], writes=[xn])
                final_toks.append(P.dma("sync", "sto", out_d[sq, t * 128:(t + 1) * 128, :], xt[:, xb, :], reads=[xn], writes=["scr"]))
            P.barrier()
        P.finish(final_toks)
    return nc


_CACHE = {}


def _prep_inputs(inputs):
    f = lambda a: np.ascontiguousarray(np.asarray(a, np.float32))
    hc = _host_consts(inputs["rel_bias"])
    gains = np.zeros((128, 32), np.float32)
    for l in range(DEPTH):
        gains[:, (l * 2 + 0) * 8:(l * 2 + 0) * 8 + 8] = f(inputs["norm_mix"])[l].reshape(8, 128).T
        gains[:, (l * 2 + 1) * 8:(l * 2 + 1) * 8 + 8] = f(inputs["norm_ffn"])[l].reshape(8, 128).T
    cwt = np.zeros((DEPTH, 128, 176), np.float32)
    for l in range(DEPTH):
        cw = f(inputs["conv_w"])[l]
        cb = f(inputs["conv_b"])[l]
        for k in range(3):
            cwt[l, :, k::4] = cw[k].reshape(44, 128).T
        cwt[l, :, 3::4] = cb.reshape(44, 128).T
    pe2 = np.zeros((DEPTH, 2, 128, 16), np.float32)
    for l in range(DEPTH):
        for kv, nm in enumerate(("cmp_pe_k", "cmp_pe_v")):
            pe = f(inputs[nm])[l]
            pe2[l, kv] = pe.reshape(16, 2, 64).transpose(1, 2, 0).reshape(128, 16)
    shared = dict(
        w_in=f(inputs["w_in"]), cmp_w1_k=f(inputs["cmp_w1_k"]), cmp_w2_k=f(inputs["cmp_w2_k"]),
        cmp_w1_v=f(inputs["cmp_w1_v"]), cmp_w2_v=f(inputs["cmp_w2_v"]), pe2=pe2,
        w_branch=f(inputs["w_branch"]), w_out=f(inputs["w_out"]), w_up=f(inputs["w_up"]), w_down=f(inputs["w_down"]),
        gains_in=gains, cwt=cwt, norm_final=f(inputs["norm_final"]),
        strips_in=hc["strips"], tabs_in=hc["tabs"], ov_in=hc["ov"], inds=hc["inds"], indm=hc["indm"], sel_in=hc["sel"],
    )
    x = f(inputs["x"])
    return [dict(shared, x=np.ascontiguousarray(x[c * SEQ_PER_CORE:(c + 1) * SEQ_PER_CORE])) for c in range(NCORE)]


def kernel(**inputs):
    if "nc" not in _CACHE:
        _CACHE["nc"] = build_program()
    nc = _CACHE["nc"]
    in_maps = _prep_inputs(inputs)
    res = run_bass_kernel_spmd(nc, in_maps, core_ids=list(range(NCORE)))
    out = np.concatenate([np.asarray(r["out"], np.float32) for r in res.results], axis=0)
    return out
```
